# Optimizing a Trainium2 kernel written in Bass

```python
import math
import jax, jax.numpy as jnp
from jax import lax
import numpy as np

D_MODEL = 1024
BATCH = 4
SEQ = 4096
DEPTH = 2
DEC_BATCH = 16
DEC_SEQ = 16
PAST_LEN = 2048

CHUNK = 64
N_MIXERS = 2
N_RWKV = (DEPTH + 1) // 2
N_FOX = DEPTH // 2
HEAD_A = 64
N_HEADS_A = D_MODEL // HEAD_A
DECAY_LORA = max(32, int(round(1.8 * D_MODEL ** 0.5 / 32)) * 32)
AAA_LORA = max(32, int(round(1.8 * D_MODEL ** 0.5 / 32)) * 32)
GATE_LORA = max(32, int(round(0.6 * D_MODEL ** 0.8 / 32)) * 32)
HEAD_B = 64
N_HEADS_B = D_MODEL // HEAD_B
D_FF = 4 * D_MODEL
Q_BLOCK = 128
ALPHA = (2 * DEPTH) ** 0.25
BETA = (8 * DEPTH) ** -0.25
LN_EPS = 1e-5
GN_EPS = 64e-5

kernel_name = "rwkv7_fox_hybrid_stream_step"


def layer_norm(x, g, b):
    xf = x.astype(jnp.float32)
    mu = jnp.mean(xf, axis=-1, keepdims=True)
    var = jnp.mean(jnp.square(xf - mu), axis=-1, keepdims=True)
    return ((xf - mu) * lax.rsqrt(var + LN_EPS) * g + b).astype(x.dtype)


def sq_relu_mlp(x, w1, w2):
    return jnp.square(jax.nn.relu(x @ w1)) @ w2


def rwkv_time_mix(x, shift_prev, wkv0, mu, w0, w1, w2, a0, a1, a2, g1, g2,
                  k_k, k_a, r_k, w_r, w_k, w_v, w_o, lnx_g, lnx_b):
    B, T, D = x.shape
    x_prev = jnp.concatenate([shift_prev[:, None].astype(x.dtype), x[:, :-1]], axis=1)
    xx = x_prev - x
    xr, xw, xk, xv, xa, xg = (x + xx * mu[i] for i in range(6))
    r = xr @ w_r
    w_raw = (w0 + jnp.tanh(xw @ w1) @ w2).astype(jnp.float32)
    w_log = -jax.nn.softplus(-w_raw) - 0.5
    k = xk @ w_k
    v = xv @ w_v
    a = jax.nn.sigmoid(a0 + (xa @ a1) @ a2)
    g = jax.nn.sigmoid(xg @ g1) @ g2

    def heads(t):
        return t.reshape(B, T, N_HEADS_A, HEAD_A).astype(jnp.float32)

    r, w_log, k, v, a = heads(r), heads(w_log), heads(k), heads(v), heads(a)
    kk = k * k_k.reshape(N_HEADS_A, HEAD_A).astype(jnp.float32)
    kk = kk / jnp.maximum(jnp.sqrt(jnp.sum(jnp.square(kk), axis=-1, keepdims=True)), 1e-12)
    k = k * (1.0 + (a - 1.0) * k_a.reshape(N_HEADS_A, HEAD_A).astype(jnp.float32))
    decay = jnp.exp(-jnp.exp(w_log))

    def step(S, inp):
        r_t, d_t, k_t, v_t, av_t, bv_t = inp
        sa = jnp.einsum('bhvk,bhk->bhv', S, av_t)
        S = S * d_t[:, :, None, :] + sa[..., None] * bv_t[:, :, None, :] + v_t[..., None] * k_t[:, :, None, :]
        y_t = jnp.einsum('bhvk,bhk->bhv', S, r_t)
        return S, y_t

    def tmaj(t):
        return jnp.swapaxes(t, 0, 1)

    S_fin, y = lax.scan(step, wkv0.astype(jnp.float32),
                        (tmaj(r), tmaj(decay), tmaj(k), tmaj(v), tmaj(-kk), tmaj(kk * a)))
    y = tmaj(y)
    mu_y = jnp.mean(y, axis=-1, keepdims=True)
    var_y = jnp.mean(jnp.square(y - mu_y), axis=-1, keepdims=True)
    yn = ((y - mu_y) * lax.rsqrt(var_y + GN_EPS)).reshape(B, T, D) * lnx_g + lnx_b
    bonus = jnp.sum(r * k * r_k.astype(jnp.float32), axis=-1, keepdims=True) * v
    out = (yn + bonus.reshape(B, T, D)) * g
    return out.astype(x.dtype) @ w_o, x[:, -1], S_fin.astype(wkv0.dtype)


def fox_project(x, w_in, b_f):
    B, T, D = x.shape
    proj = x @ w_in
    q, k, v, f = jnp.split(proj, [D_MODEL, 2 * D_MODEL, 3 * D_MODEL], axis=-1)
    shp = (B, T, N_HEADS_B, HEAD_B)
    logf = jax.nn.log_sigmoid((f + b_f).astype(jnp.float32))
    return q.reshape(shp), k.reshape(shp), v.reshape(shp), logf


def fox_attend(q, k, v, c_q, c_k, q_pos, k_pos):
    s = jnp.einsum('bqhd,bkhd->bhqk', q, k, preferred_element_type=jnp.float32) * (HEAD_B ** -0.5)
    s = s + jnp.transpose(c_q, (0, 2, 1))[..., :, None] - jnp.transpose(c_k, (0, 2, 1))[..., None, :]
    s = jnp.where(k_pos[None, :] <= q_pos[:, None], s, -jnp.inf)
    p = jax.nn.softmax(s, axis=-1)
    return jnp.einsum('bhqk,bkhd->bqhd', p.astype(v.dtype), v)


def fox_prompt(x, w_in, b_f, w_o):
    B, T, D = x.shape
    q, k, v, logf = fox_project(x, w_in, b_f)
    c = jnp.cumsum(logf, axis=1)
    nb = T // Q_BLOCK
    qb = jnp.swapaxes(q.reshape(B, nb, Q_BLOCK, N_HEADS_B, HEAD_B), 0, 1)
    cb = jnp.swapaxes(c.reshape(B, nb, Q_BLOCK, N_HEADS_B), 0, 1)
    k_pos = jnp.arange(T)

    def block(args):
        q_i, c_i, i = args
        q_pos = i * Q_BLOCK + jnp.arange(Q_BLOCK)
        return fox_attend(q_i, k, v, c_i, c, q_pos, k_pos)

    o = lax.map(block, (qb, cb, jnp.arange(nb)))
    o = jnp.swapaxes(o, 0, 1).reshape(B, T, D)
    return o @ w_o, k, v, logf


def fox_sample(x, ck, cv, clogf, w_in, b_f, w_o):
    B, T, D = x.shape
    P = ck.shape[1]
    q, k, v, logf = fox_project(x, w_in, b_f)
    k_all = jnp.concatenate([ck.astype(k.dtype), k], axis=1)
    v_all = jnp.concatenate([cv.astype(v.dtype), v], axis=1)
    c = jnp.cumsum(jnp.concatenate([clogf.astype(jnp.float32), logf], axis=1), axis=1)
    o = fox_attend(q, k_all, v_all, c[:, P:], c, P + jnp.arange(T), jnp.arange(P + T))
    return o.reshape(B, T, D) @ w_o, k, v, logf


def setup_inputs(seed: int = 0) -> dict:
    key = jax.random.key(seed)
    ks = iter(jax.random.split(key, 40))

    def nrm(shape, scale):
        return jax.random.normal(next(ks), shape, jnp.float32) * scale

    def unif(shape, lo, hi):
        return jax.random.uniform(next(ks), shape, jnp.float32, lo, hi)

    D = D_MODEL
    fox_col_scale = jnp.concatenate([jnp.ones((2 * D,)), jnp.full((D,), BETA), jnp.ones((N_HEADS_B,))]) * D ** -0.5
    return {
        "x_prompt": nrm((BATCH, SEQ, D), 1.0),
        "x_sample": nrm((DEC_BATCH, DEC_SEQ, D), 1.0),
        "state_wkv": nrm((N_RWKV, DEC_BATCH, N_HEADS_A, HEAD_A, HEAD_A), 0.3),
        "state_shift": nrm((N_RWKV, DEC_BATCH, D), 1.0),
        "cache_k": nrm((N_FOX, DEC_BATCH, PAST_LEN, N_HEADS_B, HEAD_B), 1.0),
        "cache_v": nrm((N_FOX, DEC_BATCH, PAST_LEN, N_HEADS_B, HEAD_B), BETA),
        "cache_logf": jax.nn.log_sigmoid(2.0 + nrm((N_FOX, DEC_BATCH, PAST_LEN, N_HEADS_B), 1.0)),
        "rwkv_mu": unif((N_RWKV, 6, D), 0.0, 1.0),
        "rwkv_w0": unif((N_RWKV, D), -5.0, -1.0),
        "rwkv_w1": nrm((N_RWKV, D, DECAY_LORA), D ** -0.5),
        "rwkv_w2": nrm((N_RWKV, DECAY_LORA, D), 0.1 * DECAY_LORA ** -0.5),
        "rwkv_a0": nrm((N_RWKV, D), 0.1),
        "rwkv_a1": nrm((N_RWKV, D, AAA_LORA), D ** -0.5),
        "rwkv_a2": nrm((N_RWKV, AAA_LORA, D), 0.1 * AAA_LORA ** -0.5),
        "rwkv_g1": nrm((N_RWKV, D, GATE_LORA), D ** -0.5),
        "rwkv_g2": nrm((N_RWKV, GATE_LORA, D), GATE_LORA ** -0.5),
        "rwkv_k_k": 0.85 + nrm((N_RWKV, D), 0.05),
        "rwkv_k_a": 1.0 + nrm((N_RWKV, D), 0.05),
        "rwkv_r_k": nrm((N_RWKV, N_HEADS_A, HEAD_A), 0.1),
        "rwkv_w_r": nrm((N_RWKV, D, D), D ** -0.5),
        "rwkv_w_k": nrm((N_RWKV, D, D), D ** -0.5),
        "rwkv_w_v": nrm((N_RWKV, D, D), BETA * D ** -0.5),
        "rwkv_w_o": nrm((N_RWKV, D, D), BETA * D ** -0.5),
        "rwkv_lnx_g": 1.0 + nrm((N_RWKV, D), 0.05),
        "rwkv_lnx_b": nrm((N_RWKV, D), 0.01),
        "fox_w_in": nrm((N_FOX, D, 3 * D + N_HEADS_B), 1.0) * fox_col_scale,
        "fox_b_f": 2.0 + nrm((N_FOX, N_HEADS_B), 0.5),
        "fox_w_o": nrm((N_FOX, D, D), BETA * D ** -0.5),
        "ffn_w1": nrm((DEPTH, D, D_FF), BETA * D ** -0.5),
        "ffn_w2": nrm((DEPTH, D_FF, D), BETA * D_FF ** -0.5),
        "ln_mix_g": 1.0 + nrm((DEPTH, D), 0.05),
        "ln_mix_b": nrm((DEPTH, D), 0.01),
        "ln_ffn_g": 1.0 + nrm((DEPTH, D), 0.05),
        "ln_ffn_b": nrm((DEPTH, D), 0.01),
    }


def reference(x_prompt, x_sample, state_wkv, state_shift, cache_k, cache_v, cache_logf,
              rwkv_mu, rwkv_w0, rwkv_w1, rwkv_w2, rwkv_a0, rwkv_a1, rwkv_a2, rwkv_g1, rwkv_g2,
              rwkv_k_k, rwkv_k_a, rwkv_r_k, rwkv_w_r, rwkv_w_k, rwkv_w_v, rwkv_w_o,
              rwkv_lnx_g, rwkv_lnx_b, fox_w_in, fox_b_f, fox_w_o,
              ffn_w1, ffn_w2, ln_mix_g, ln_mix_b, ln_ffn_g, ln_ffn_b):
    xp, xs = x_prompt, x_sample
    bp = xp.shape[0]
    wkv_p, shift_p, k_p, v_p, lf_p = [], [], [], [], []
    wkv_s, shift_s, k_s, v_s, lf_s = [], [], [], [], []
    for i in range(DEPTH):
        j = i // N_MIXERS
        if i % N_MIXERS == 0:
            prm = (rwkv_mu[j], rwkv_w0[j], rwkv_w1[j], rwkv_w2[j], rwkv_a0[j], rwkv_a1[j], rwkv_a2[j],
                   rwkv_g1[j], rwkv_g2[j], rwkv_k_k[j], rwkv_k_a[j], rwkv_r_k[j], rwkv_w_r[j],
                   rwkv_w_k[j], rwkv_w_v[j], rwkv_w_o[j], rwkv_lnx_g[j], rwkv_lnx_b[j])
            zero_shift = jnp.zeros((bp, D_MODEL), xp.dtype)
            zero_wkv = jnp.zeros((bp, N_HEADS_A, HEAD_A, HEAD_A), state_wkv.dtype)
            hp, sh_p, S_p = rwkv_time_mix(xp, zero_shift, zero_wkv, *prm)
            hs, sh_s, S_s = rwkv_time_mix(xs, state_shift[j], state_wkv[j], *prm)
            wkv_p.append(S_p); shift_p.append(sh_p)
            wkv_s.append(S_s); shift_s.append(sh_s)
        else:
            hp, kp_, vp_, lfp_ = fox_prompt(xp, fox_w_in[j], fox_b_f[j], fox_w_o[j])
            hs, ks_, vs_, lfs_ = fox_sample(xs, cache_k[j], cache_v[j], cache_logf[j],
                                            fox_w_in[j], fox_b_f[j], fox_w_o[j])
            k_p.append(kp_); v_p.append(vp_); lf_p.append(lfp_)
            k_s.append(ks_); v_s.append(vs_); lf_s.append(lfs_)
        xp = layer_norm(ALPHA * xp + hp, ln_mix_g[i], ln_mix_b[i])
        xs = layer_norm(ALPHA * xs + hs, ln_mix_g[i], ln_mix_b[i])
        xp = layer_norm(ALPHA * xp + sq_relu_mlp(xp, ffn_w1[i], ffn_w2[i]), ln_ffn_g[i], ln_ffn_b[i])
        xs = layer_norm(ALPHA * xs + sq_relu_mlp(xs, ffn_w1[i], ffn_w2[i]), ln_ffn_g[i], ln_ffn_b[i])
    return (xp, xs,
            jnp.stack(wkv_p), jnp.stack(shift_p), jnp.stack(k_p), jnp.stack(v_p), jnp.stack(lf_p),
            jnp.stack(wkv_s), jnp.stack(shift_s), jnp.stack(k_s), jnp.stack(v_s), jnp.stack(lf_s))
```

```python
import math
import numpy as np
from contextlib import ExitStack
import concourse.bass as bass
import concourse.mybir as mybir
from concourse.bass_utils import run_bass_kernel_spmd

F32 = mybir.dt.float32
BF16 = mybir.dt.bfloat16
AF = mybir.ActivationFunctionType
ALU = mybir.AluOpType
AX = mybir.AxisListType

D = 1024
H = 16
HD = 64
DFF = 4096
ALPHA = 4.0 ** 0.25
LN_EPS = 1e-5
GN_EPS = 64e-5
EH = math.exp(-0.5)
LW, LA, LG = 64, 64, 160
PAST = 2048


class Buf:
    __slots__ = ("t", "name", "lw", "rd", "dkey", "is_dram")

    def __init__(self, t, name, dkey=None):
        self.t = t
        self.name = name
        self.lw = None
        self.rd = {}
        self.dkey = dkey or name
        self.is_dram = False


class V:
    __slots__ = ("buf", "ap")

    def __init__(self, buf, ap=None):
        self.buf = buf
        self.ap = buf.t[:] if ap is None else ap

    def __getitem__(self, k):
        return V(self.buf, self.ap[k])

    def re(self, pat, **kw):
        return V(self.buf, self.ap.rearrange(pat, **kw))

    def bc(self, dt):
        return V(self.buf, self.ap.bitcast(dt))

    def bcast(self, shape):
        return V(self.buf, self.ap.broadcast_to(list(shape)))

    def unsq(self, ax):
        return V(self.buf, self.ap.unsqueeze(ax))


class Eng:
    def __init__(self, key, h):
        self.key = key
        self.h = h
        self.sem = None
        self.cnt = 0
        self.seen = {}


class FW:
    def __init__(self, nc, es):
        self.nc = nc
        self.es = es
        self.engs = {}
        for key, h in (("pe", nc.tensor), ("act", nc.scalar), ("dve", nc.vector),
                       ("pool", nc.gpsimd), ("sp", nc.sync)):
            e = Eng(key, h)
            e.sem = es.enter_context(nc.semaphore("sem_" + key))
            self.engs[key] = e
        self.dsem = {}
        self.dcnt = {}
        self.nb = 0

    def sb(self, es, shape, dt, name, dkey=None):
        self.nb += 1
        t = es.enter_context(self.nc.sbuf_tensor(f"{name}_{self.nb}", list(shape), dt))
        return V(Buf(t, f"{name}_{self.nb}", dkey or name))

    def ps(self, es, shape, dt, name):
        self.nb += 1
        t = es.enter_context(self.nc.psum_tensor(f"{name}_{self.nb}", list(shape), dt))
        return V(Buf(t, f"{name}_{self.nb}"))

    def _deps(self, reads, writes):
        deps = {}

        def add(k, c):
            if deps.get(k, 0) < c:
                deps[k] = c
        for b in reads:
            if b.lw is not None:
                add(*b.lw)
        for b in writes:
            if b.lw is not None:
                add(*b.lw)
            for k, c in b.rd.items():
                add(k, c)
        return deps

    def _semof(self, k):
        return self.dsem[k] if isinstance(k, tuple) else self.engs[k].sem

    def _waits(self, e, deps):
        for k, c in deps.items():
            if k == "pe" and e.key == "pe":
                continue
            if e.seen.get(k, 0) >= c:
                continue
            e.h.wait_ge(self._semof(k), c)
            e.seen[k] = c

    def _mark(self, key, cnt, reads, writes):
        for b in reads:
            if b.rd.get(key, 0) < cnt:
                b.rd[key] = cnt
        for b in writes:
            b.lw = (key, cnt)
            b.rd = {}

    def op(self, eng, fn, reads=(), writes=()):
        e = self.engs[eng]
        reads = [v.buf for v in reads]
        writes = [v.buf for v in writes]
        self._waits(e, self._deps(reads, writes))
        ins = fn()
        e.cnt += 1
        ins.then_inc(e.sem, 1)
        self._mark(e.key, e.cnt, reads, writes)

    def dma(self, eng, out, in_, owner=None):
        e = self.engs[eng]
        if owner is None:
            owner = out if not getattr(out.buf, "is_dram", False) else in_
        qk = "sw" if eng == "pool" else "hw"
        key = ("d", owner.buf.dkey, qk)
        if key not in self.dsem:
            self.dsem[key] = self.es.enter_context(self.nc.semaphore("ds_" + qk + "_" + owner.buf.dkey))
            self.dcnt[key] = 0
        reads, writes = [in_.buf], [out.buf]
        self._waits(e, self._deps(reads, writes))
        ins = e.h.dma_start(out=out.ap, in_=in_.ap)
        self.dcnt[key] += 16
        ins.then_inc(self.dsem[key], 16)
        self._mark(key, self.dcnt[key], reads, writes)

    def barrier(self):
        for e in self.engs.values():
            for key, c in self.dcnt.items():
                if c > 0 and e.seen.get(key, 0) < c:
                    e.h.wait_ge(self.dsem[key], c)
                    e.seen[key] = c
            for k, o in self.engs.items():
                if o.cnt > 0 and e.seen.get(k, 0) < o.cnt:
                    e.h.wait_ge(o.sem, o.cnt)
                    e.seen[k] = o.cnt


def dram_v(ap, name="dram"):
    b = Buf(None, name)
    b.is_dram = True
    return V(b, ap)


class Slots:
    def __init__(self, fw, es, n, shape, dt, name):
        self.free = [fw.sb(es, shape, dt, f"{name}{i}", dkey=f"{name}{i}") for i in range(n)]

    def get(self):
        return self.free.pop(0)

    def put(self, *vs):
        for v in vs:
            self.free.append(V(v.buf))


class Prog:
    def __init__(self, NP, NS):
        self.NP, self.NS = NP, NS
        self.TP = NP * 128
        self.TT = self.TP + NS * 16
        self.tiles = [(i * 128, 128, "p", i) for i in range(NP)] + \
                     [(self.TP + j * 16, 16, "s", j) for j in range(NS)]
        nc = self.nc = bass.Bass("TRN2", target_bir_lowering=False)
        self.din = {}
        self.dout = {}
        self.do_fox = True

    def inp(self, name, shape, dt=F32):
        ap = self.nc.dram_tensor(name, list(shape), dt, kind="ExternalInput").ap()
        self.din[name] = dram_v(ap, name)
        return self.din[name]

    def outp(self, name, shape, dt=F32):
        ap = self.nc.dram_tensor(name, list(shape), dt, kind="ExternalOutput").ap()
        self.dout[name] = ap
        return ap

    def scratch(self, name, shape, dt):
        return self.nc.dram_tensor(name, list(shape), dt, kind="Internal").ap()

    def tt(self, eng, out, a, b, op):
        h = self.nc.vector if eng == "dve" else self.nc.gpsimd
        self.fw.op(eng, lambda: h.tensor_tensor(out.ap, a.ap, b.ap, op), [a, b], [out])

    def ts(self, eng, out, a, s1, s2, op0, op1=None):
        h = self.nc.vector if eng == "dve" else self.nc.gpsimd
        rd = [a] + [s for s in (s1, s2) if isinstance(s, V)]
        g = lambda s: s.ap if isinstance(s, V) else s
        if op1 is None:
            self.fw.op(eng, lambda: h.tensor_scalar(out.ap, a.ap, g(s1), None, op0), rd, [out])
        else:
            self.fw.op(eng, lambda: h.tensor_scalar(out.ap, a.ap, g(s1), g(s2), op0, op1), rd, [out])

    def stt(self, out, a, s, b, op0, op1):
        rd = [a, b] + ([s] if isinstance(s, V) else [])
        sv = s.ap if isinstance(s, V) else s
        self.fw.op("dve", lambda: self.nc.vector.scalar_tensor_tensor(out.ap, a.ap, sv, b.ap, op0, op1), rd, [out])

    def act(self, out, a, func, bias=None, scale=None):
        rd = [a] + ([bias] if isinstance(bias, V) else [])
        kw = {}
        if bias is not None:
            kw["bias"] = bias.ap if isinstance(bias, V) else bias
        if scale is not None:
            kw["scale"] = scale
        self.fw.op("act", lambda: self.nc.scalar.activation(out.ap, a.ap, func, **kw), rd, [out])

    def cp(self, eng, out, a):
        if eng == "act":
            self.fw.op("act", lambda: self.nc.scalar.copy(out.ap, a.ap), [a], [out])
        else:
            h = self.nc.vector if eng == "dve" else self.nc.gpsimd
            self.fw.op(eng, lambda: h.tensor_copy(out.ap, a.ap), [a], [out])

    def mm(self, out, groups):
        rd = []
        for l, r in groups:
            rd += [l, r]

        def fn():
            ins = None
            for i, (l, r) in enumerate(groups):
                ins = self.nc.tensor.matmul(out.ap, l.ap, r.ap, start=(i == 0), stop=(i == len(groups) - 1))
            return ins
        self.fw.op("pe", fn, rd, [out])

    def mms(self, items):
        rd, wr = [], []
        for o, groups in items:
            wr.append(o)
            for l, r in groups:
                rd += [l, r]

        def fn():
            ins = None
            for o, groups in items:
                for i, (l, r) in enumerate(groups):
                    ins = self.nc.tensor.matmul(o.ap, l.ap, r.ap, start=(i == 0), stop=(i == len(groups) - 1))
            return ins
        self.fw.op("pe", fn, rd, wr)

    def transposes(self, out_ps, src, ident, n, nchunk=8, w=128):
        def fn():
            ins = None
            for c in range(nchunk):
                ins = self.nc.tensor.transpose(out_ps.ap[:, c, 0:n], src.ap[0:n, c * w:(c + 1) * w], ident.ap[0:n, 0:n])
            return ins
        self.fw.op("pe", fn, [src, ident], [out_ps])

    def load_w(self, dst, src, K, N, stg, col0=0, ncols=None, dcol0=0):
        ncols = ncols or N
        KC = (K + 127) // 128
        SW = stg[0].ap.shape[-1]
        for kc in range(KC):
            rows = min(128, K - kc * 128)
            for c0 in range(0, ncols, SW):
                w = min(SW, ncols - c0)
                s = stg[self._stg_i % len(stg)]
                self._stg_i += 1
                self.fw.dma("sp", s[0:rows, 0:w], src[kc * 128:kc * 128 + rows, col0 + c0:col0 + c0 + w])
                self.cp(("pool", "dve", "act")[self._stg_i % 3], dst[0:rows, kc, dcol0 + c0:dcol0 + c0 + w], s[0:rows, 0:w])

    def load_rep(self, dst, src1d, n=D):
        self.fw.dma("sp", dst, V(src1d.buf, src1d.ap.partition_broadcast(128)))

    def layer_norm(self, z, n, g, b, st, mv, rs, out):
        nc = self.nc
        zz = z[0:n, :]
        self.fw.op("dve", lambda: nc.vector.bn_stats(st.ap[0:n, 0, :], z.ap[0:n, 0:512]), [z], [st])
        self.fw.op("dve", lambda: nc.vector.bn_stats(st.ap[0:n, 1, :], z.ap[0:n, 512:1024]), [z], [st])
        self.fw.op("dve", lambda: nc.vector.bn_aggr(mv.ap[0:n, :], st.ap[0:n].rearrange("p a b -> p (a b)")), [st], [mv])
        self.ts("dve", rs[0:n, :], mv[0:n, 1:2], LN_EPS, None, ALU.add)
        self.act(rs[0:n, :], rs[0:n, :], AF.Sqrt)
        self.fw.op("dve", lambda: nc.vector.reciprocal(rs.ap[0:n, :], rs.ap[0:n, :]), [rs], [rs])
        self.ts("dve", zz, zz, mv[0:n, 0:1], rs[0:n, 0:1], ALU.subtract, ALU.mult)
        self.tt("dve", zz, zz, g[0:n, :], ALU.mult)
        self.tt("dve", out[0:n, :], zz, b[0:n, :], ALU.add)

    def build(self):
        nc = self.nc
        NP, NS, TP, TT = self.NP, self.NS, self.TP, self.TT
        I = self.inp
        xin = I("xin", [TT, D])
        sshift = I("sshift", [max(NS, 1), D])
        swkv = I("swkv", [max(NS, 1), H, HD, HD])
        ck = I("ck", [max(NS, 1), PAST, D])
        cv = I("cv", [max(NS, 1), PAST, D])
        clf = I("clf", [max(NS, 1), PAST, H])
        W = {}
        for nm, shp in (("rwkv_mu", [6, D]), ("rwkv_w0", [D]), ("rwkv_w1", [D, LW]), ("rwkv_w2", [LW, D]),
                        ("rwkv_a0", [D]), ("rwkv_a1", [D, LA]), ("rwkv_a2", [LA, D]), ("rwkv_g1", [D, LG]),
                        ("rwkv_g2", [LG, D]), ("rwkv_k_k", [D]), ("rwkv_k_a", [D]), ("rwkv_r_k", [D]),
                        ("rwkv_w_r", [D, D]), ("rwkv_w_k", [D, D]), ("rwkv_w_v", [D, D]), ("rwkv_w_o", [D, D]),
                        ("rwkv_lnx_g", [D]), ("rwkv_lnx_b", [D]), ("fox_w_in", [D, 3 * D + H]), ("fox_b_f", [H]),
                        ("fox_w_o", [D, D]), ("ffn_w1", [2, D, DFF]), ("ffn_w2", [2, DFF, D]),
                        ("ln_mix_g", [2, D]), ("ln_mix_b", [2, D]), ("ln_ffn_g", [2, D]), ("ln_ffn_b", [2, D])):
            W[nm] = I(nm, shp)
        cst = {}
        for nm, shp in (("c_ident", [128, 128]), ("c_ms", [128, 512]), ("c_ml", [128, 512]), ("c_tri_inc", [128, 128]),
                        ("c_tri_rev", [128, 128]), ("c_last128", [128, 128]), ("c_last16", [128, 128]),
                        ("c_ones", [128, 128]), ("c_tri1", [128, 128])):
            cst[nm] = I(nm, shp)
        self.W, self.cst = W, cst

        O = self.outp
        self.o_y = O("y", [TT, D])
        self.o_wkvp = O("wkv_p", [H, HD, HD])
        self.o_shiftp = O("shift_p", [1, D])
        self.o_kp = O("k_p", [max(TP, 1), D])
        self.o_vp = O("v_p", [max(TP, 1), D])
        self.o_lfp = O("lf_p", [max(TP, 1), H])
        self.o_wkvs = O("wkv_s", [max(NS, 1), H, HD, HD])
        self.o_shifts = O("shift_s", [max(NS, 1), D])
        self.o_ks = O("k_s", [max(NS, 1) * 16, D])
        self.o_vs = O("v_s", [max(NS, 1) * 16, D])
        self.o_lfs = O("lf_s", [max(NS, 1) * 16, H])

        self.o_scr = self.scratch("o_scr", [TT, D], BF16)
        self.x1_scr = self.scratch("x1_scr", [TT, D], F32)
        self.TK = TP + NS * (PAST + 16)
        self.qT_scr = self.scratch("qT_scr", [H, 68, TT], BF16)
        self.kT_scr = self.scratch("kT_scr", [H, 68, self.TK], BF16)
        self.v1_scr = self.scratch("v1_scr", [self.TK, H, 65], BF16)

        with ExitStack() as es:
            self.fw = FW(nc, es)
            self._stg_i = 0
            self.o_tiles = [dram_v(self.o_scr[r0:r0 + n, :], f"oscr{i}") for i, (r0, n, _, _) in enumerate(self.tiles)]
            self.x1_tiles = [dram_v(self.x1_scr[r0:r0 + n, :], f"x1scr{i}") for i, (r0, n, _, _) in enumerate(self.tiles)]
            self.y_tiles = [dram_v(self.o_y[r0:r0 + n, :], f"yout{i}") for i, (r0, n, _, _) in enumerate(self.tiles)]
            self.xin_tiles = [V(xin.buf, xin.ap[r0:r0 + n, :]) for (r0, n, _, _) in self.tiles]
            import os
            self.dbg = os.environ.get('KDBG', '')
            if self.dbg != 'B' and not self.dbg.startswith('C'):
                self.phase_rwkv(xin, sshift, swkv)
            self.fw.barrier()
            if self.dbg.startswith('A'):
                return nc
            if self.dbg.startswith('C'):
                self.phase_fox(ck, cv, clf)
                self.fw.barrier()
                return nc
            self.phase_ffn(0, self.xin_tiles, self.x1_tiles if self.do_fox else self.y_tiles, W["rwkv_w_o"])
            self.fw.barrier()
            if self.do_fox:
                self.phase_fox(ck, cv, clf)
                self.fw.barrier()
                self.phase_ffn(1, self.x1_tiles, self.y_tiles, W["fox_w_o"])
                self.fw.barrier()
        return nc

    def phase_ffn(self, L, res_tiles, dst_tiles, wo_d):
        nc, fw, W, cst = self.nc, self.fw, self.W, self.cst
        with ExitStack() as es:
            sb = lambda shape, dt, name: fw.sb(es, shape, dt, name + f"L{L}", dkey=name)
            wo = sb([128, 8, D], BF16, "b_wo")
            w1 = sb([128, 8, DFF], BF16, "b_w1")
            w2 = sb([128, 32, D], BF16, "b_w2")
            idb = sb([128, 128], BF16, "b_idb")
            g1, b1, g2, b2 = [sb([128, D], F32, f"b_ln{i}") for i in range(4)]
            with ExitStack() as es2:
                stg = [fw.sb(es2, [128, 2048], F32, f"b_stg{i}L{L}", dkey=f"b_stg{i}") for i in range(4)]
                fw.dma("sp", stg[0][:, 0:128], cst["c_ident"])
                self.cp("pool", idb, stg[0][:, 0:128])
                self._stg_i = 1
                self.load_w(wo, wo_d, D, D, stg)
                self.load_w(w1, V(W["ffn_w1"].buf, W["ffn_w1"].ap[L]), D, DFF, stg)
                self.load_w(w2, V(W["ffn_w2"].buf, W["ffn_w2"].ap[L]), DFF, D, stg)
                fw.barrier()
            for dst, nm in ((g1, "ln_mix_g"), (b1, "ln_mix_b"), (g2, "ln_ffn_g"), (b2, "ln_ffn_b")):
                self.load_rep(dst, V(W[nm].buf, W[nm].ap[L]))
            NBUF = 2
            ob = [sb([128, D], BF16, f"b_ob{i}") for i in range(NBUF)]
            A = [sb([128, D], F32, f"b_A{i}") for i in range(NBUF)]
            oT = sb([128, 8, 128], BF16, "b_oT")
            xmb = sb([128, D], BF16, "b_xmb")
            xmT = sb([128, 8, 128], BF16, "b_xmT")
            hr = [sb([128, 512], F32, f"b_hr{i}") for i in range(2)]
            hT = sb([128, 32, 128], BF16, "b_hT")
            st = sb([128, 2, 6], F32, "b_st")
            mv = sb([128, 2], F32, "b_mv")
            rs = sb([128, 1], F32, "b_rs")
            st2 = sb([128, 2, 6], F32, "b_st2")
            mv2 = sb([128, 2], F32, "b_mv2")
            rs2 = sb([128, 1], F32, "b_rs2")
            pT = fw.ps(es, [128, 8, 128], BF16, "b_pT")
            pz = fw.ps(es, [128, D], F32, "b_pz")
            ph = [fw.ps(es, [128, 512], F32, f"b_ph{i}") for i in range(2)]
            pz2 = fw.ps(es, [128, D], F32, "b_pz2")
            T = self.tiles

            def P1(ti):
                r0, n, kind, j = T[ti]
                p = ti % NBUF
                a = A[p]
                fw.dma("sp", ob[p][0:n, :], self.o_tiles[ti])
                fw.dma("sp", a[0:n, :], res_tiles[ti])
                self.transposes(pT, ob[p], idb, n)
                self.cp("act", oT[:, :, 0:n], pT[:, :, 0:n])
                self.mms([(pz[0:n, hf * 512:(hf + 1) * 512],
                           [(oT[:, kc, 0:n], wo[:, kc, hf * 512:(hf + 1) * 512]) for kc in range(8)])
                          for hf in range(2)])

            def P2(ti):
                r0, n, kind, j = T[ti]
                a = A[ti % NBUF]
                self.stt(a[0:n, :], a[0:n, :], ALPHA, pz[0:n, :], ALU.mult, ALU.add)
                self.layer_norm(a, n, g1, b1, st, mv, rs, a)
                self.cp("dve", xmb[0:n, :], a[0:n, :])

            def P3(ti):
                r0, n, kind, j = T[ti]
                self.transposes(pT, xmb, idb, n)
                self.cp("act", xmT[:, :, 0:n], pT[:, :, 0:n])

            def F1(ti):
                r0, n, kind, j = T[ti]
                for fg in range(8):
                    pp = ph[fg % 2]
                    self.mms([(pp[:, q * 128:q * 128 + n],
                               [(w1[:, kc, (fg * 4 + q) * 128:(fg * 4 + q + 1) * 128], xmT[:, kc, 0:n]) for kc in range(8)])
                              for q in range(4)])
                    h_ = hr[fg % 2]
                    ppv = pp.re("p (q t) -> p q t", q=4)[:, :, 0:n]
                    hv = h_.re("p (q t) -> p q t", q=4)[:, :, 0:n]
                    self.act(hv, ppv, AF.Relu)
                    self.tt("pool", hT[:, fg * 4:fg * 4 + 4, 0:n], hv, hv, ALU.mult)

            def F2(ti):
                r0, n, kind, j = T[ti]
                a = A[ti % NBUF]
                self.mms([(pz2[0:n, hf * 512:(hf + 1) * 512],
                           [(hT[:, fc, 0:n], w2[:, fc, hf * 512:(hf + 1) * 512]) for fc in range(32)])
                          for hf in range(2)])
                self.stt(a[0:n, :], a[0:n, :], ALPHA, pz2[0:n, :], ALU.mult, ALU.add)
                self.layer_norm(a, n, g2, b2, st2, mv2, rs2, a)
                fw.dma("pool", dst_tiles[ti], a[0:n, :])

            NT = len(T)
            if NT:
                P1(0); P2(0); P3(0)
            for ti in range(NT):
                if ti + 1 < NT:
                    P1(ti + 1)
                    P2(ti + 1)
                F1(ti)
                if ti + 1 < NT:
                    P3(ti + 1)
                F2(ti)
            fw.barrier()

    def phase_rwkv(self, xin, sshift, swkv):
        nc, fw, W, cst = self.nc, self.fw, self.W, self.cst
        TP = self.TP
        with ExitStack() as es:
            sb = lambda shape, dt, name: fw.sb(es, shape, dt, "a_" + name)
            S4 = Slots(fw, es, 11, [128, D], F32, "a_s4_")
            S2 = Slots(fw, es, 13, [128, D], BF16, "a_s2_")
            wr, wk, wv = [sb([128, 8, D], BF16, nm) for nm in ("wr", "wk", "wv")]
            wli = sb([128, 8, 288], BF16, "wli")
            w2e = sb([128, D], BF16, "w2e")
            a2e = sb([128, D], BF16, "a2e")
            g2a = sb([128, 1, D], BF16, "g2a")
            g2b = sb([128, 1, D], BF16, "g2b")[0:32]
            kkc, kac, lgc, lbc = [sb([128, D], F32, nm) for nm in ("kkc", "kac", "lgc", "lbc")]
            rkc = sb([128, D], BF16, "rkc")
            mu = [sb([128, D], BF16, f"mu{i}") for i in range(6)]
            idb = sb([128, 128], BF16, "idb")
            idf = sb([128, 128], F32, "idf")
            ms = sb([128, 4, 128], BF16, "ms")
            ml = sb([128, 4, 128], BF16, "ml")
            tri_inc = sb([128, 128], F32, "tri_inc")
            tri_rev = sb([128, 128], F32, "tri_rev")
            ST = sb([128, 8, 64], F32, "ST")
            STb = sb([128, 8, 64], BF16, "STb")
            hw = sb([128, 128], BF16, "hw")
            ha = sb([128, 128], BF16, "ha")
            hg1 = sb([128, 128], BF16, "hg1")
            hg2 = sb([128, 128], BF16, "hg2")[0:32]
            mixT = [sb([128, 8, 128], BF16, f"mixT{i}") for i in range(2)]
            ARs = [sb([128, 8, 2, 128], BF16, f"AR{i}") for i in range(2)]
            BTs = [sb([128, 8, 128], BF16, f"BT{i}") for i in range(2)]
            KTs = [sb([128, 8, 128], BF16, f"KT{i}") for i in range(2)]
            Abk = sb([128, 8, 4, 128], BF16, "Abk")
            PA = sb([128, 8, 128], BF16, "PA")
            PB = sb([128, 8, 128], BF16, "PB")
            PTA = sb([128, 8, 128], BF16, "PTA")
            PTB = sb([128, 8, 128], BF16, "PTB")
            Xb = [sb([128, 512], BF16, f"Xb{i}") for i in range(2)]
            gCs = [sb([128, 8, 2], F32, f"gC{i}") for i in range(2)]
            sm = [sb([128, 16], F32, f"sm{i}") for i in range(4)]
            smf = [sb([128, 16], F32, f"smf{i}") for i in range(2)]
            pT = fw.ps(es, [128, 8, 128], BF16, "a_pT")
            pbA = fw.ps(es, [128, D], F32, "a_pbA")
            pbB = fw.ps(es, [128, D], F32, "a_pbB")
            phid = fw.ps(es, [128, 512], F32, "a_phid")
            psc = [fw.ps(es, [128, 512], F32, f"a_psc{i}") for i in range(2)]

            s0, s1 = S4.get(), S4.get()
            stg = [s0, s1]
            self._stg_i = 0

            def ldc(dst, src, w, cast_eng="pool"):
                s = stg[self._stg_i % 2]
                self._stg_i += 1
                fw.dma("sp", s[:, 0:w], src)
                self.cp(cast_eng, dst, s[:, 0:w])
            ldc(idb, cst["c_ident"], 128)
            fw.dma("sp", idf, cst["c_ident"])
            ldc(ms.re("p a b -> p (a b)"), cst["c_ms"], 512)
            ldc(ml.re("p a b -> p (a b)"), cst["c_ml"], 512)
            fw.dma("sp", tri_inc, cst["c_tri_inc"])
            fw.dma("sp", tri_rev, cst["c_tri_rev"])
            for i in range(6):
                s = stg[self._stg_i % 2]
                self._stg_i += 1
                self.load_rep(s, V(W["rwkv_mu"].buf, W["rwkv_mu"].ap[i]))
                self.cp("pool", mu[i], s)
            for dst, nm in ((kkc, "rwkv_k_k"), (kac, "rwkv_k_a"), (lgc, "rwkv_lnx_g"), (lbc, "rwkv_lnx_b")):
                self.load_rep(dst, W[nm])
            s = stg[self._stg_i % 2]
            self._stg_i += 1
            self.load_rep(s, W["rwkv_r_k"])
            self.cp("pool", rkc, s)
            for dst, nm in ((wr, "rwkv_w_r"), (wk, "rwkv_w_k"), (wv, "rwkv_w_v")):
                self.load_w1k(dst, W[nm], D, D, stg)
            self.load_w1k(wli, W["rwkv_w1"], D, LW, stg, dcol0=0)
            self.load_w1k(wli, W["rwkv_a1"], D, LA, stg, dcol0=64)
            self.load_w1k(wli, W["rwkv_g1"], D, LG, stg, dcol0=128)
            for dst, wn, bn in ((w2e, "rwkv_w2", "rwkv_w0"), (a2e, "rwkv_a2", "rwkv_a0")):
                fw.op("pool", lambda: nc.gpsimd.memset(dst.ap, 0.0), [], [dst])
                s = stg[self._stg_i % 2]
                self._stg_i += 1
                fw.dma("sp", s[0:64, :], W[wn])
                self.cp("pool", dst[0:64, :], s[0:64, :])
                s = stg[self._stg_i % 2]
                self._stg_i += 1
                bsrc = V(W[bn].buf, W[bn].ap.partition_broadcast(128))
                fw.dma("sp", s[64:65, :], bsrc[64:65, :])
                fw.dma("sp", s[96:97, :], bsrc[96:97, :])
                self.cp("pool", dst[64:65, :], s[64:65, :])
                self.cp("pool", dst[96:97, :], s[96:97, :])
                self.tt("pool", dst[96:97, :], s[96:97, :], dst[96:97, :], ALU.subtract)
            s = stg[self._stg_i % 2]
            self._stg_i += 1
            fw.dma("sp", s[:, 0:D], W["rwkv_g2"][0:128, :])
            self.cp("pool", g2a[:, 0, :], s[:, 0:D])
            s = stg[self._stg_i % 2]
            self._stg_i += 1
            fw.dma("sp", s[0:32, 0:D], W["rwkv_g2"][128:160, :])
            self.cp("pool", g2b[:, 0, :], s[0:32, 0:D])
            for hh_ in (hw, ha):
                fw.op("pool", lambda: nc.gpsimd.memset(hh_.ap, 0.0), [], [hh_])
                fw.op("pool", lambda: nc.gpsimd.memset(hh_.ap[64:65, :], 1.0), [], [hh_])
                fw.op("pool", lambda: nc.gpsimd.memset(hh_.ap[96:97, :], 1.0), [], [hh_])
            S4.put(s0, s1)

            def out_state(dst_ap, name):
                pso = pbA.re("p (c q) -> p c q", c=8)

                def fn():
                    ins = None
                    for c in range(8):
                        ins = nc.tensor.transpose(pso.ap[0:64, c, :], ST.ap[:, c, :], idf.ap)
                    return ins
                fw.op("pe", fn, [ST, idf], [pbA])
                t = S4.get()
                self.cp("act", t[0:64, :], pbA[0:64, :])
                fw.dma("pool", dram_v(dst_ap.rearrange("(c hh) v k -> v c hh k", hh=2), name),
                       t[0:64, :].re("p (c hh k) -> p c hh k", c=8, hh=2))
                S4.put(t)

            if self.dbg == 'A0':
                fw.barrier()
                return
            ctxs = {}
            def front(ti):
                r0, n, kind, j = self.tiles[ti]
                nlev = int(round(math.log2(n)))
                first = (j == 0) if kind == "p" else True
                par = ti % 2
                AR, BT, KT, gC = ARs[par], BTs[par], KTs[par], gCs[par]
                x32, xp = S4.get(), S4.get()
                fw.dma("sp", x32[0:n, :], xin[r0:r0 + n, :])
                if first:
                    if kind == "p":
                        fw.op("pool", lambda: nc.gpsimd.memset(xp.ap[0:1, :], 0.0), [], [xp])
                    else:
                        fw.dma("sp", xp[0:1, :], sshift[j:j + 1, :])
                    fw.dma("sp", xp[1:n, :], xin[r0:r0 + n - 1, :])
                else:
                    fw.dma("sp", xp[0:n, :], xin[r0 - 1:r0 + n - 1, :])
                self.tt("pool", xp[0:n, :], xp[0:n, :], x32[0:n, :], ALU.subtract)
                r32 = k32 = sg = a32 = vb = gb = None
                for i in range(6):
                    t = S4.get()
                    self.tt("pool", t[0:n, :], xp[0:n, :], mu[i][0:n, :], ALU.mult)
                    mb = S2.get()
                    self.tt("pool", mb[0:n, :], t[0:n, :], x32[0:n, :], ALU.add)
                    S4.put(t)
                    self.transposes(pT, mb, idb, n)
                    mT = mixT[i % 2]
                    self.cp("act", mT[:, :, 0:n], pT[:, :, 0:n])
                    S2.put(mb)
                    yield

                    def proj(pb, w):
                        self.mms([(pb[0:n, hf * 512:(hf + 1) * 512],
                                   [(mT[:, kc, 0:n], w[:, kc, hf * 512:(hf + 1) * 512]) for kc in range(8)])
                                  for hf in range(2)])

                    def lora_out(pb, pairs):
                        self.mms([(pb[0:n, hf * 512:(hf + 1) * 512],
                                   [(hl_, w_[:, hf * 512:(hf + 1) * 512]) for hl_, w_ in pairs])
                                  for hf in range(2)])
                    if i == 0:
                        proj(pbA, wr)
                        r32 = S4.get()
                        self.cp("act", r32[0:n, :], pbA[0:n, :])
                    elif i == 1:
                        self.mm(phid[0:64, 0:n], [(wli[:, kc, 0:64], mT[:, kc, 0:n]) for kc in range(8)])
                        self.act(hw[0:64, 0:n], phid[0:64, 0:n], AF.Tanh)
                        lora_out(pbB, [(hw[:, 0:n], w2e)])
                        sg = S4.get()
                        self.act(sg[0:n, :], pbB[0:n, :], AF.Sigmoid)
                    elif i == 2:
                        proj(pbA, wk)
                        k32 = S4.get()
                        self.cp("act", k32[0:n, :], pbA[0:n, :])
                    elif i == 3:
                        proj(pbB, wv)
                        vb = S2.get()
                        self.cp("act", vb[0:n, :], pbB[0:n, :])
                    elif i == 4:
                        self.mm(phid[0:64, 0:n], [(wli[:, kc, 64:128], mT[:, kc, 0:n]) for kc in range(8)])
                        self.cp("act", ha[0:64, 0:n], phid[0:64, 0:n])
                        lora_out(pbA, [(ha[:, 0:n], a2e)])
                        a32 = S4.get()
                        self.act(a32[0:n, :], pbA[0:n, :], AF.Sigmoid)
                    else:
                        self.mm(phid[:, 0:n], [(wli[:, kc, 128:256], mT[:, kc, 0:n]) for kc in range(8)])
                        self.mm(phid[0:32, 128:128 + n], [(wli[:, kc, 256:288], mT[:, kc, 0:n]) for kc in range(8)])
                        self.act(hg1[:, 0:n], phid[:, 0:n], AF.Sigmoid)
                        self.act(hg2[:, 0:n], phid[0:32, 128:128 + n], AF.Sigmoid)
                        lora_out(pbB, [(hg1[:, 0:n], g2a[:, 0, :]), (hg2[:, 0:n], g2b[:, 0, :])])
                        gb = S2.get()
                        self.cp("act", gb[0:n, :], pbB[0:n, :])
                S4.put(x32, xp)
                if self.dbg == 'A1':
                    S4.put(r32, sg, k32, a32); S2.put(vb, gb)
                    return
                self.mms([(pbA[0:n, hf * 512:(hf + 1) * 512], [(tri_inc[0:n, 0:n], sg[0:n, hf * 512:(hf + 1) * 512])])
                          for hf in range(2)])
                self.mms([(pbB[0:n, hf * 512:(hf + 1) * 512], [(tri_rev[0:n, 0:n], sg[0:n, hf * 512:(hf + 1) * 512])])
                          for hf in range(2)])
                self.mms([(psc[0][:, c * 2:c * 2 + 2], [(sg[0:n, c * 128:(c + 1) * 128], tri_inc[0:n, n - 2:n])])
                          for c in range(8)])
                self.act(gC.re("p c q -> p (c q)"), psc[0][:, 0:16], AF.Exp)
                tA, tB, tC = S4.get(), S4.get(), S4.get()
                self.act(tA[0:n, :], pbA[0:n, :], AF.Exp)
                rt = S2.get()
                self.tt("dve", rt[0:n, :], r32[0:n, :], tA[0:n, :], ALU.mult)
                self.stt(tB[0:n, :], sg[0:n, :], EH, pbA[0:n, :], ALU.mult, ALU.add)
                self.act(tB[0:n, :], tB[0:n, :], AF.Exp)
                self.act(tA[0:n, :], pbA[0:n, :], AF.Exp, scale=-1.0)
                self.act(tC[0:n, :], pbB[0:n, :], AF.Exp)
                S4.put(sg)
                yield
                t1, sq = S4.get(), S4.get()
                h3 = lambda v_: v_[0:n, :].re("p (h d) -> p h d", h=16)
                b3 = lambda v_: v_[0:n, :].unsq(2).bcast([n, 16, 64])
                self.tt("dve", t1[0:n, :], k32[0:n, :], kkc[0:n, :], ALU.mult)
                self.act(sq[0:n, :], t1[0:n, :], AF.Square)
                fw.op("dve", lambda: nc.vector.tensor_reduce(smf[0].ap[0:n, :], h3(sq).ap, AX.X, ALU.add), [sq], [smf[0]])
                self.ts("dve", smf[0][0:n, :], smf[0][0:n, :], 1e-24, None, ALU.max)
                self.act(smf[0][0:n, :], smf[0][0:n, :], AF.Sqrt)
                fw.op("dve", lambda: nc.vector.reciprocal(smf[0].ap[0:n, :], smf[0].ap[0:n, :]), [smf[0]], [smf[0]])
                self.tt("dve", h3(t1), h3(t1), b3(smf[0]), ALU.mult)
                yield
                t2 = sq
                self.stt(t2[0:n, :], a32[0:n, :], -1.0, kac[0:n, :], ALU.add, ALU.mult)
                self.stt(t2[0:n, :], t2[0:n, :], 1.0, k32[0:n, :], ALU.add, ALU.mult)
                S4.put(k32)
                yield
                self.tt("pool", a32[0:n, :], t1[0:n, :], a32[0:n, :], ALU.mult)
                at, bt, kt, bh, kh = [S2.get() for _ in range(5)]
                self.stt(at[0:n, :], t1[0:n, :], -1.0, tB[0:n, :], ALU.mult, ALU.mult)
                self.tt("dve", bt[0:n, :], a32[0:n, :], tA[0:n, :], ALU.mult)
                self.tt("pool", kt[0:n, :], t2[0:n, :], tA[0:n, :], ALU.mult)
                yield
                self.tt("dve", bh[0:n, :], a32[0:n, :], tC[0:n, :], ALU.mult)
                self.tt("pool", kh[0:n, :], t2[0:n, :], tC[0:n, :], ALU.mult)
                S4.put(tA, tB, tC)
                yield
                tD = S4.get()
                self.tt("dve", tD[0:n, :], r32[0:n, :], t2[0:n, :], ALU.mult)
                self.tt("pool", tD[0:n, :], tD[0:n, :], rkc[0:n, :], ALU.mult)
                yield
                fw.op("dve", lambda: nc.vector.tensor_reduce(smf[1].ap[0:n, :], h3(tD).ap, AX.X, ALU.add), [tD], [smf[1]])
                self.tt("dve", h3(tD), b3(smf[1]), h3(vb), ALU.mult)
                S4.put(r32, t1, t2, a32)
                for src, dst in ((at, AR[:, :, 0, 0:n]), (rt, AR[:, :, 1, 0:n]), (bt, BT[:, :, 0:n]), (kt, KT[:, :, 0:n])):
                    self.transposes(pT, src, idb, n)
                    self.cp("act", dst, pT[:, :, 0:n])
                    yield
                S2.put(at, rt, bt, kt)
                if self.dbg == 'A2':
                    S4.put(tD); S2.put(vb, gb, bh, kh)
                    return
                ctxs[ti] = dict(vb=vb, gb=gb, bh=bh, kh=kh, tD=tD)
                yield

            def back(ti):
                r0, n, kind, j = self.tiles[ti]
                nlev = int(round(math.log2(n)))
                first = (j == 0) if kind == "p" else True
                par = ti % 2
                AR, BT, KT, gC = ARs[par], BTs[par], KTs[par], gCs[par]
                c_ = ctxs.pop(ti)
                vb, gb, bh, kh, tD = c_['vb'], c_['gb'], c_['bh'], c_['kh'], c_['tD']
                h3 = lambda v_: v_[0:n, :].re("p (h d) -> p h d", h=16)
                b3 = lambda v_: v_[0:n, :].unsq(2).bcast([n, 16, 64])
                if kind == "p" and j == 0:
                    fw.op("pool", lambda: nc.gpsimd.memset(ST.ap, 0.0), [], [ST])
                    fw.op("pool", lambda: nc.gpsimd.memset(STb.ap, 0.0), [], [STb])
                elif kind == "s":
                    t = S4.get()
                    fw.dma("sp", t[0:64, :].re("p (h k) -> p h k", h=16),
                           V(swkv.buf, swkv.ap[j].rearrange("h v k -> v h k")))
                    psi = pbA[:, 0:512].re("p (c v) -> p c v", c=8)

                    def fn():
                        ins = None
                        for c in range(8):
                            ins = nc.tensor.transpose(psi.ap[:, c, :], t.ap[0:64, c * 128:(c + 1) * 128], idf.ap[0:64, 0:64])
                        return ins
                    fw.op("pe", fn, [t, idf], [pbA])
                    self.cp("act", ST, psi)
                    self.cp("pool", STb, ST)
                    S4.put(t)
                yield
                y32 = S4.get()
                for hh in range(2):
                    hd = lambda hl: (hh * 8 + hl, (hh * 8 + hl) // 2, ((hh * 8 + hl) % 2) * 64, (hh * 8 + hl) * 64)
                    for hl in range(8):
                        h, c, po, col = hd(hl)
                        pr = slice(po, po + 64)
                        pp = psc[hl % 2]
                        ppv = pp.re("p (a t) -> p a t", a=4)
                        self.mms([(ppv[0:n, 0:2, 0:n], [(BT[pr, c, 0:n], AR[pr, c, :, 0:n])]),
                                  (ppv[0:n, 2:4, 0:n], [(KT[pr, c, 0:n], AR[pr, c, :, 0:n])])])
                        self.tt("dve", Abk[0:n, hl, :, 0:n], ppv[0:n, :, 0:n], ms[0:n, :, 0:n], ALU.mult)
                    xc = lambda hl: (hl % 2) * 256 + (hl // 2) * 64
                    pc = lambda hl: (hl % 2) * 512 + (hl // 2) * 64
                    banks = (phid.re("p (a t) -> p a t", a=4), psc[0].re("p (a t) -> p a t", a=4))
                    self.mms([(banks[hl % 2][0:n, hl // 2, 0:n],
                               [(AR[slice(hd(hl)[2], hd(hl)[2] + 64), hd(hl)[1], 0, 0:n],
                                 BT[slice(hd(hl)[2], hd(hl)[2] + 64), hd(hl)[1], 0:n])]) for hl in range(8)])
                    PTAv = PTA.re("p (g two) t -> p g two t", two=2)
                    for par in range(2):
                        self.tt("dve", PTAv[0:n, :, par, 0:n], banks[par][0:n, :, 0:n], ml[0:n, :, 0:n], ALU.mult)
                    yield
                    if self.dbg == 'A3':
                        return
                    items = []
                    for hl in range(8):
                        h, c, po, col = hd(hl)
                        pr = slice(po, po + 64)
                        items.append((pbA[0:n, pc(hl):pc(hl) + 64],
                                      [(AR[pr, c, 0, 0:n], STb[pr, c, :]),
                                       (Abk[0:n, hl, 2, 0:n], vb[0:n, col:col + 64])]))
                    self.mms(items)
                    self.cp("act", Xb[0][0:n, :].re("p (b x) -> p b x", b=2),
                            pbA[0:n, :].re("p (b x) -> p b x", b=2)[:, :, 0:256])
                    xi = 0
                    yield
                    P, PT = Abk[:, :, 0, :], PTA
                    Pn, PTn = PB, PTB
                    for lev in range(nlev):
                        pX = pbB
                        self.mms([(pX[0:n, xc(hl):xc(hl) + 64],
                                   [(idb[0:n, 0:n], Xb[xi][0:n, xc(hl):xc(hl) + 64]),
                                    (P[0:n, hl, 0:n], Xb[xi][0:n, xc(hl):xc(hl) + 64])]) for hl in range(8)])
                        self.cp("act", Xb[1 - xi][0:n, :], pX[0:n, 0:512])
                        xi = 1 - xi
                        yield
                        if lev < nlev - 1:
                            for g4 in range(2):
                                pq = psc[g4].re("p (a t) -> p a t", a=4)
                                self.mms([(pq[0:n, q, 0:n], [(PT[0:n, g4 * 4 + q, 0:n], P[0:n, g4 * 4 + q, 0:n])]) for q in range(4)])
                                self.cp("dve", Pn[0:n, g4 * 4:g4 * 4 + 4, 0:n], pq[0:n, :, 0:n])
                            for g4 in range(2):
                                pq = phid.re("p (a t) -> p a t", a=4)
                                self.mms([(pq[0:n, q, 0:n], [(P[0:n, g4 * 4 + q, 0:n], PT[0:n, g4 * 4 + q, 0:n])]) for q in range(4)])
                                self.cp("act", PTn[0:n, g4 * 4:g4 * 4 + 4, 0:n], pq[0:n, :, 0:n])
                            P, PT, Pn, PTn = Pn, PTn, (PA if Pn is PB else PB), (PTA if PTn is PTB else PTB)
                    UT = Xb[xi]
                    yield
                    items = []
                    for hl in range(8):
                        h, c, po, col = hd(hl)
                        pr = slice(po, po + 64)
                        items.append((pbA[0:n, pc(hl):pc(hl) + 64],
                                      [(AR[pr, c, 1, 0:n], STb[pr, c, :]),
                                       (Abk[0:n, hl, 1, 0:n], UT[0:n, xc(hl):xc(hl) + 64]),
                                       (Abk[0:n, hl, 3, 0:n], vb[0:n, col:col + 64])]))
                    self.mms(items)
                    y32v = y32[0:n, hh * 512:(hh + 1) * 512].re("p (g two d) -> p g two d", two=2, d=64)
                    for par in range(2):
                        self.cp("act", y32v[:, :, par, :],
                                pbA[0:n, par * 512:par * 512 + 256].re("p (g d) -> p g d", d=64))
                    yield
                    items = []
                    for hl in range(8):
                        h, c, po, col = hd(hl)
                        items.append((psc[hl % 2][po:po + 64, (hl // 2) * 64:(hl // 2 + 1) * 64],
                                      [(bh[0:n, col:col + 64], UT[0:n, xc(hl):xc(hl) + 64]),
                                       (kh[0:n, col:col + 64], vb[0:n, col:col + 64])]))
                    self.mms(items)
                    for cl in range(4):
                        c = hh * 4 + cl
                        for par in range(2):
                            pr = slice(par * 64, par * 64 + 64)
                            self.stt(ST[pr, c, :], ST[pr, c, :], gC[pr, c, 1:2], psc[par][pr, cl * 64:(cl + 1) * 64], ALU.mult, ALU.add)
                if self.dbg == 'A3':
                    S4.put(y32, tD); S2.put(vb, gb, bh, kh)
                    return
                self.cp("pool", STb, ST)
                S2.put(bh, kh)
                if self.dbg == 'A4':
                    S4.put(y32, tD); S2.put(vb, gb)
                    return
                sq = S4.get()
                fw.op("dve", lambda: nc.vector.tensor_reduce(sm[0].ap[0:n, :], h3(y32).ap, AX.X, ALU.add), [y32], [sm[0]])
                self.act(sq[0:n, :], y32[0:n, :], AF.Square)
                fw.op("dve", lambda: nc.vector.tensor_reduce(sm[1].ap[0:n, :], h3(sq).ap, AX.X, ALU.add), [sq], [sm[1]])
                S4.put(sq)
                yield
                self.ts("dve", sm[0][0:n, :], sm[0][0:n, :], 1.0 / 64, None, ALU.mult)
                self.tt("dve", sm[2][0:n, :], sm[0][0:n, :], sm[0][0:n, :], ALU.mult)
                self.stt(sm[1][0:n, :], sm[1][0:n, :], 1.0 / 64, sm[2][0:n, :], ALU.mult, ALU.subtract)
                self.ts("dve", sm[1][0:n, :], sm[1][0:n, :], GN_EPS, None, ALU.add)
                self.act(sm[1][0:n, :], sm[1][0:n, :], AF.Sqrt)
                fw.op("dve", lambda: nc.vector.reciprocal(sm[1].ap[0:n, :], sm[1].ap[0:n, :]), [sm[1]], [sm[1]])
                self.tt("dve", h3(y32), h3(y32), b3(sm[0]), ALU.subtract)
                self.tt("dve", h3(y32), h3(y32), b3(sm[1]), ALU.mult)
                self.tt("pool", y32[0:n, :], y32[0:n, :], lgc[0:n, :], ALU.mult)
                self.tt("pool", y32[0:n, :], y32[0:n, :], lbc[0:n, :], ALU.add)
                self.tt("dve", y32[0:n, :], y32[0:n, :], tD[0:n, :], ALU.add)
                ob = S2.get()
                self.tt("dve", ob[0:n, :], y32[0:n, :], gb[0:n, :], ALU.mult)
                fw.dma("pool", self.o_tiles[ti], ob[0:n, :])
                S4.put(y32, tD)
                S2.put(ob, gb, vb)
                if self.dbg == 'A5':
                    return
                if kind == "p" and j == self.NP - 1:
                    out_state(self.o_wkvp, "wkvp")
                    fw.dma("pool", dram_v(self.o_shiftp, "shiftp"), V(xin.buf, xin.ap[TP - 1:TP, :]), owner=dram_v(self.o_shiftp, "shiftp_o"))
                if kind == "s":
                    out_state(self.o_wkvs[j], f"wkvs{j}")
                    fw.dma("pool", dram_v(self.o_shifts[j:j + 1, :], f"shifts{j}"), V(xin.buf, xin.ap[r0 + n - 1:r0 + n, :]),
                           owner=dram_v(self.o_shifts, "shifts_o"))

            NT = len(self.tiles)
            if NT:
                for _ in front(0):
                    pass
            for ti in range(NT):
                gb_ = back(ti)
                gf_ = front(ti + 1) if ti + 1 < NT else None
                alive_b, alive_f = True, gf_ is not None
                step_ = 0
                while alive_b or alive_f:
                    step_ += 1
                    for _rep in range(2 if step_ % 2 else 1):
                        if alive_b:
                            try:
                                next(gb_)
                            except StopIteration:
                                alive_b = False
                    if alive_f:
                        try:
                            next(gf_)
                        except StopIteration:
                            alive_f = False
            fw.barrier()

    def phase_fox(self, ck, cv, clf):
        nc, fw, W, cst = self.nc, self.fw, self.W, self.cst
        NP, NS, TP, TT, TK = self.NP, self.NS, self.TP, self.TT, self.TK
        qT_d = dram_v(self.qT_scr, "qT_scr")
        kT_d = dram_v(self.kT_scr, "kT_scr")
        v1_d = dram_v(self.v1_scr, "v1_scr")
        with ExitStack() as es:
            sb = lambda shape, dt, name: fw.sb(es, shape, dt, "c_" + name)
            win = sb([128, 8, 3 * D + H], BF16, "win")
            stg = [sb([128, D], F32, f"stg{i}") for i in range(2)]
            idb = sb([128, 128], BF16, "idb")
            tri1 = sb([128, 128], F32, "tri1")
            last128 = sb([128, 128], F32, "last128")
            last16 = sb([128, 128], F32, "last16")
            bfc = sb([128, H], F32, "bfc")
            ones = sb([128, 2, 1024], BF16, "ones")[0:16]
            self._stg_i = 0
            fw.dma("sp", stg[0][:, 0:128], cst["c_ident"])
            self.cp("pool", idb, stg[0][:, 0:128])
            self._stg_i = 1
            fw.dma("sp", tri1, cst["c_tri1"])
            fw.dma("sp", last128, cst["c_last128"])
            fw.dma("sp", last16, cst["c_last16"])
            import os
            ksub = os.environ.get('KSUB', '')
            if 'nobfc' not in ksub:
                self.load_rep(bfc, W["fox_b_f"])
            if 'noones' not in ksub:
                fw.op("pool", lambda: nc.gpsimd.memset(ones.ap, 1.0), [], [ones])
            if 'now' not in ksub:
                self.load_w1k(win, W["fox_w_in"], D, 3 * D + H, stg)
            for t0 in range(0, TT if self.dbg != 'C0a' else 0, 1024):
                w = min(1024, TT - t0)
                fw.dma("pool", dram_v(self.qT_scr[:, 66:68, t0:t0 + w]), ones[:, :, 0:w])
            for t0 in range(0, TK if self.dbg != 'C0a' else 0, 1024):
                w = min(1024, TK - t0)
                fw.dma("pool", dram_v(self.kT_scr[:, 64:66, t0:t0 + w]), ones[:, :, 0:w])
            if self.dbg in ('C0', 'C0a'):
                fw.barrier()
                return
            def mkset(tag, shared=None):
                d = {}
                d["x32"] = [sb([128, D], F32, f"x32{tag}{i}") for i in range(2)]
                d["xb"] = [sb([128, D], BF16, f"xb{tag}{i}") for i in range(2)]
                d["xT"] = [sb([128, 8, 128], BF16, f"xT{tag}{i}") for i in range(2)]
                d["qTt"] = [sb([128, 16, 128], BF16, f"qTt{tag}{i}")[0:64] for i in range(2)]
                d["kTt"] = [sb([128, 16, 128], BF16, f"kTt{tag}{i}")[0:64] for i in range(2)]
                d["kTc"] = [sb([128, 8, 128], BF16, f"kTc{tag}{i}") for i in range(2)]
                d["k32"] = [sb([128, D], F32, f"k32{tag}{i}") for i in range(2)]
                d["v32"] = [sb([128, D], F32, f"v32{tag}{i}") for i in range(2)]
                d["v1t"] = [sb([128, H, 65], BF16, f"v1t{tag}{i}") for i in range(2)]
                d["lf"] = [sb([128, H], F32, f"lf{tag}{i}") for i in range(2)]
                d["cc"] = [sb([128, H], F32, f"cc{tag}{i}") for i in range(2)]
                d["ex"] = sb([128, H], F32, "ex" + tag)
                for nm in ("chi", "clo", "nhi", "nlo"):
                    d[nm] = sb([128, 128], BF16, nm + tag)[0:16]
                for nm in ("cT32", "chi32", "clo32"):
                    d[nm] = sb([128, 128], F32, nm + tag)[0:16]
                for v_ in d["v1t"]:
                    fw.op("pool", lambda: nc.gpsimd.memset(v_.ap, 1.0), [], [v_])
                d["pT"] = fw.ps(es, [128, 8, 128], BF16, "c_pT" + tag)
                d["pf"] = fw.ps(es, [128, 512], F32, "c_pf" + tag)
                if shared is None:
                    d["pq"] = [fw.ps(es, [128, 512], F32, f"c_pq{tag}{i}") for i in range(2)]
                    d["pk"] = fw.ps(es, [128, D], F32, "c_pk" + tag)
                    d["pv"] = d["pk"]
                else:
                    d["pq"], d["pk"], d["pv"] = shared["pq"], shared["pk"], shared["pv"]
                return d
            setA = mkset("A")
            setB = mkset("B", setA)

            streams = []
            st = []
            for i in range(NP):
                st.append(("p", 128, self.x1_tiles[i], i * 128, i * 128, i * 128, None))
            if NP:
                streams.append(st)
            for j in range(NS):
                st = []
                base = TP + j * (PAST + 16)
                for m in range(PAST // 128):
                    st.append(("c", 128, m, None, base + m * 128, None, j))
                st.append(("s", 16, self.x1_tiles[NP + j], TP + j * 16, base + PAST, j * 16, j))
                streams.append(st)
            cnt = 0
            if self.dbg == 'C1a':
                streams = streams[:1]
            if self.dbg == 'C1b':
                streams = streams[1:]
            def run_stream(st, bs):
                x32, xb, xT, qTt, kTt, kTc, k32, v32 = (bs[k_] for k_ in ('x32', 'xb', 'xT', 'qTt', 'kTt', 'kTc', 'k32', 'v32'))
                v1t, lf, cc, ex = bs['v1t'], bs['lf'], bs['cc'], bs['ex']
                chi, clo, nhi, nlo, cT32, chi32, clo32 = (bs[k_] for k_ in ('chi', 'clo', 'nhi', 'nlo', 'cT32', 'chi32', 'clo32'))
                pT, pq, pk, pv, pf = bs['pT'], bs['pq'], bs['pk'], bs['pv'], bs['pf']
                cnt = 0
                prev = None
                for (kind, n, src, qpos, kpos, orow, j) in st:
                    p = cnt % 2
                    cnt += 1
                    lft, cct, v1 = lf[p], cc[p], v1t[p]
                    xb, xT, qTt, kTt, kTc, k32, v32 = (bs[k_][p] for k_ in ('xb', 'xT', 'qTt', 'kTt', 'kTc', 'k32', 'v32'))
                    if kind == "c":
                        m = src
                        a = x32[p]
                        fw.dma("sp", a[0:n, :], V(ck.buf, ck.ap[j, m * 128:(m + 1) * 128, :]))
                        self.cp("pool", xb[0:n, :], a[0:n, :])
                        self.transposes(pT, xb, idb, n)
                        self.cp("act", kTc[:, :, 0:n], pT[:, :, 0:n])
                        for hh in range(2):
                            fw.dma("act", dram_v(self.kT_scr[:, 0:64, kpos:kpos + n].rearrange("(c hh) d t -> hh d c t", hh=2)[hh]),
                                   kTc[hh * 64:(hh + 1) * 64, :, 0:n])
                        b_ = x32[1 - p]
                        fw.dma("sp", b_[0:n, :], V(cv.buf, cv.ap[j, m * 128:(m + 1) * 128, :]))
                        self.cp("pool", v1[0:n, :, 0:64], b_[0:n, :].re("p (h d) -> p h d", h=H))
                        fw.dma("pool", dram_v(self.v1_scr[kpos:kpos + n]), v1[0:n])
                        fw.dma("sp", lft[0:n, :], V(clf.buf, clf.ap[j, m * 128:(m + 1) * 128, :]))
                    else:
                        a = x32[p]
                        fw.dma("sp", a[0:n, :], src)
                        self.cp("pool", xb[0:n, :], a[0:n, :])
                        self.transposes(pT, xb, idb, n)
                        self.cp("act", xT[:, :, 0:n], pT[:, :, 0:n])
                        if 's1' in ksub:
                            continue
                        ko, vo, lo_ = (self.o_kp, self.o_vp, self.o_lfp) if kind == "p" else (self.o_ks, self.o_vs, self.o_lfs)
                        self.mm(pf[0:n, 0:H], [(xT[:, kc, 0:n], win[:, kc, 3 * D:3 * D + H]) for kc in range(8)])
                        self.tt("dve", ex[0:n, :], pf[0:n, 0:H], bfc[0:n, :], ALU.add)
                        self.act(ex[0:n, :], ex[0:n, :], AF.Exp, scale=-1.0)
                        self.act(ex[0:n, :], ex[0:n, :], AF.Ln, bias=1.0)
                        self.ts("dve", lft[0:n, :], ex[0:n, :], -1.0, None, ALU.mult)
                        fw.dma("pool", dram_v(lo_[orow:orow + n, :], f"lo{kind}{orow}"), lft[0:n, :])

                        def tm_proj(pb, coff, dst32, eng_):
                            self.mms([(pb[0:n, hf * 512:(hf + 1) * 512],
                                       [(xT[:, kc, 0:n], win[:, kc, coff + hf * 512:coff + (hf + 1) * 512]) for kc in range(8)])
                                      for hf in range(2)])
                            self.cp(eng_, dst32[0:n, :], pb[0:n, :])

                        def fm_proj(dstT, coff, scale):
                            for g in range(4):
                                pp = pq[g % 2].re("p (a t) -> p a t", a=4)
                                self.mms([(pp[0:64, q_, 0:n],
                                           [(win[:, kc, coff + (g * 4 + q_) * 64:coff + (g * 4 + q_ + 1) * 64], xT[:, kc, 0:n]) for kc in range(8)])
                                          for q_ in range(4)])
                                self.act(dstT[:, g * 4:g * 4 + 4, 0:n], pp[0:64, :, 0:n], AF.Identity, scale=scale)
                        tm_proj(pk, D, k32, "act")
                        fw.dma("act", dram_v(ko[orow:orow + n, :], f"ko{kind}{orow}"), k32[0:n, :])
                        fm_proj(qTt, 0, 0.125)
                        fw.dma("act", dram_v(self.qT_scr[:, 0:64, qpos:qpos + n].rearrange("h d t -> d h t")), qTt[:, :, 0:n])
                        tm_proj(pv, 2 * D, v32, "dve")
                        self.cp("pool", v1[0:n, :, 0:64], v32[0:n, :].re("p (h d) -> p h d", h=H))
                        fw.dma("pool", dram_v(vo[orow:orow + n, :], f"vo{kind}{orow}"), v32[0:n, :])
                        fw.dma("pool", dram_v(self.v1_scr[kpos:kpos + n]), v1[0:n])
                        fm_proj(kTt, D, 1.0)
                        fw.dma("act", dram_v(self.kT_scr[:, 0:64, kpos:kpos + n].rearrange("h d t -> d h t")), kTt[:, :, 0:n])
                    if 's4' in ksub:
                        continue
                    g1 = [(tri1[0:n, 0:n], lft[0:n, :])]
                    g2 = [(lft[0:n, :], tri1[0:n, 0:n])]
                    if prev is not None:
                        pcc, pn = prev
                        lastm = last128 if pn == 128 else last16
                        g1.append((lastm[0:pn, 0:n], pcc[0:pn, :]))
                        g2.append((pcc[0:pn, :], lastm[0:pn, 0:n]))
                    self.mm(pf[0:n, 64:64 + H], g1)
                    self.mm(pf[0:16, 128:128 + n], g2)
                    self.cp("dve", cct[0:n, :], pf[0:n, 64:64 + H])
                    prev = (cct, n)
                    if 's6' in ksub:
                        continue
                    self.cp("dve", cT32[:, 0:n], pf[0:16, 128:128 + n])
                    if 's7' in ksub:
                        continue
                    self.cp("act", chi[:, 0:n], cT32[:, 0:n])
                    self.cp("act", chi32[:, 0:n], chi[:, 0:n])
                    if 's8' in ksub:
                        continue
                    self.tt("dve", clo32[:, 0:n], cT32[:, 0:n], chi32[:, 0:n], ALU.subtract)
                    self.cp("act", clo[:, 0:n], clo32[:, 0:n])
                    self.act(nhi[:, 0:n], chi32[:, 0:n], AF.Identity, scale=-1.0)
                    self.act(nlo[:, 0:n], clo32[:, 0:n], AF.Identity, scale=-1.0)
                    if 's5' in ksub:
                        continue
                    fw.dma("pool", dram_v(self.kT_scr[:, 66, kpos:kpos + n]), nhi[:, 0:n])
                    fw.dma("pool", dram_v(self.kT_scr[:, 67, kpos:kpos + n]), nlo[:, 0:n])
                    if kind != "c":
                        fw.dma("pool", dram_v(self.qT_scr[:, 64, qpos:qpos + n]), chi[:, 0:n])
                        fw.dma("pool", dram_v(self.qT_scr[:, 65, qpos:qpos + n]), clo[:, 0:n])
                    yield

            prompt_streams = [st for st in streams if st and st[0][0] == "p"]
            sample_streams = [st for st in streams if st and st[0][0] != "p"]

            def chain(sts, bs):
                for st in sts:
                    for _ in run_stream(st, bs):
                        yield
            ga, gb2 = chain(prompt_streams, setA), chain(sample_streams, setB)
            alive = [True, True]
            while any(alive):
                for gi_, g_ in enumerate((ga, gb2)):
                    if alive[gi_]:
                        try:
                            next(g_)
                        except StopIteration:
                            alive[gi_] = False
            fw.barrier()
        if self.dbg.startswith('C1'):
            return
        with ExitStack() as es:
            sb = lambda shape, dt, name: fw.sb(es, shape, dt, "d_" + name)
            NKT = max(NP, PAST // 128 + 1)
            v1s = sb([128, NKT, H * 65], BF16, "v1s")
            osb = sb([128, max(NP, 1), D], BF16, "osb")
            qTh = [sb([128, max(TP, 16)], BF16, f"qTh{i}")[0:68] for i in range(2)]
            kTh = [sb([128, max(TP, PAST + 16)], BF16, f"kTh{i}")[0:68] for i in range(2)]
            PTb = [sb([128, 4, 128], BF16, f"PT{i}") for i in range(3)]
            rc = [sb([128, 1], F32, f"rc{i}") for i in range(2)]
            psS = [fw.ps(es, [128, 512], F32, f"d_ps{i}") for i in range(3)]
            psO = [fw.ps(es, [128, 512], F32, f"d_po{i}") for i in range(2)]
            seqs = []
            if NP:
                seqs.append(("p", [(i * 128, 128) for i in range(NP)], [(i * 128, 128) for i in range(NP)], 0, 0, None))
            for j in range(NS):
                base = TP + j * (PAST + 16)
                seqs.append(("s", [(TP + j * 16, 16)], [(base + m * 128, 128) for m in range(PAST // 128)] + [(base + PAST, 16)],
                             TP + j * 16, base, j))
            gi = 0
            hcount = 0
            for (kind, qtiles, ktiles, q0, k0, j) in seqs:
                nq_tot = sum(n for _, n in qtiles)
                nk_tot = sum(n for _, n in ktiles)
                nfull = nk_tot // 128
                if nfull:
                    fw.dma("sp", v1s[:, 0:nfull, :],
                           V(v1_d.buf, self.v1_scr[k0:k0 + nfull * 128].rearrange("(kt p) h e -> p kt (h e)", p=128)))
                if nk_tot % 128:
                    r = nk_tot % 128
                    fw.dma("sp", v1s[0:r, nfull, :],
                           V(v1_d.buf, self.v1_scr[k0 + nfull * 128:k0 + nk_tot].rearrange("p h e -> p (h e)")))
                jobs = []
                for h in range(H):
                    hp = hcount % 2
                    hcount += 1
                    first_of_head = True
                    for qi, (qpos, nq) in enumerate(qtiles):
                        last_kt = qi if kind == "p" else len(ktiles) - 1
                        for g0 in range(0, last_kt + 1, 4):
                            kts = list(range(g0, min(g0 + 4, last_kt + 1)))
                            jobs.append(dict(h=h, hp=hp, qi=qi, qpos=qpos, nq=nq, kts=kts, last_kt=last_kt,
                                             load=first_of_head, fin=(kts[-1] == last_kt), gi=gi))
                            gi += 1
                            first_of_head = False

                def emit_scores(jb):
                    h, hp = jb["h"], jb["hp"]
                    if jb["load"]:
                        fw.dma("sp", qTh[hp][:, 0:nq_tot], V(qT_d.buf, self.qT_scr[h, :, q0:q0 + nq_tot]))
                        fw.dma("sp", kTh[hp][:, 0:nk_tot], V(kT_d.buf, self.kT_scr[h, :, k0:k0 + nk_tot]))
                    ps_ = psS[jb["gi"] % 3].re("p (a t) -> p a t", a=4)
                    nq, ql = jb["nq"], jb["qpos"] - q0
                    self.mms([(ps_[0:ktiles[kt][1], a_, 0:nq],
                               [(kTh[hp][:, ktiles[kt][0] - k0:ktiles[kt][0] - k0 + ktiles[kt][1]], qTh[hp][:, ql:ql + nq])])
                              for a_, kt in enumerate(jb["kts"])])

                def emit_rest(jb):
                    h, qi, nq, kts, last_kt = jb["h"], jb["qi"], jb["nq"], jb["kts"], jb["last_kt"]
                    ps_ = psS[jb["gi"] % 3].re("p (a t) -> p a t", a=4)
                    pt_ = PTb[jb["gi"] % 3]
                    po = psO[qi % 2]
                    nkmin = min(ktiles[kt][1] for kt in kts)
                    if nkmin == 128:
                        self.act(pt_[:, 0:len(kts), 0:nq], ps_[:, 0:len(kts), 0:nq], AF.Exp)
                    else:
                        for a_, kt in enumerate(kts):
                            nk = ktiles[kt][1]
                            self.act(pt_[0:nk, a_, 0:nq], ps_[0:nk, a_, 0:nq], AF.Exp)
                    for a_, kt in enumerate(kts):
                        nk = ktiles[kt][1]
                        if kt == last_kt:
                            fw.op("pool", lambda: nc.gpsimd.affine_select(
                                pt_.ap[0:nk, a_, 0:nq], pt_.ap[0:nk, a_, 0:nq], [[1, nq]], ALU.is_ge, 0.0,
                                base=0, channel_multiplier=-1), [pt_], [pt_])

                    def fn():
                        ins = None
                        for a_, kt in enumerate(kts):
                            nk = ktiles[kt][1]
                            ins = nc.tensor.matmul(po.ap[0:nq, 0:65], pt_.ap[0:nk, a_, 0:nq],
                                                   v1s.ap[0:nk, kt, h * 65:(h + 1) * 65],
                                                   start=(kt == 0), stop=(kt == last_kt))
                        return ins
                    fw.op("pe", fn, [pt_, v1s], [po])
                    if jb["fin"]:
                        r_ = rc[qi % 2]
                        fw.op("dve", lambda: nc.vector.reciprocal(r_.ap[0:nq, :], po.ap[0:nq, 64:65]), [po], [r_])
                        self.ts("dve", osb[0:nq, qi, h * 64:(h + 1) * 64], po[0:nq, 0:64], r_[0:nq, 0:1], None, ALU.mult)

                if jobs:
                    emit_scores(jobs[0])
                for k_, jb in enumerate(jobs):
                    if k_ + 1 < len(jobs):
                        emit_scores(jobs[k_ + 1])
                    emit_rest(jb)
                if kind == "p":
                    fw.dma("pool", dram_v(self.o_scr[0:TP].rearrange("(i p) d -> p i d", p=128), "oscr_all"), osb[:, 0:NP, :])
                else:
                    fw.dma("pool", self.o_tiles[NP + j], osb[0:16, 0, :])
            fw.barrier()

    def fw_last_dma(self, _unused, owner):
        key = ("d", owner.buf.dkey)
        return (key, self.fw.dcnt[key])

    def load_w1k(self, dst, src, K, N, stg, dcol0=0):
        KC = K // 128
        for kc in range(KC):
            for c0 in range(0, N, 1024):
                w = min(1024, N - c0)
                s = stg[self._stg_i % len(stg)]
                self._stg_i += 1
                self.fw.dma("sp", s[:, 0:w], src[kc * 128:(kc + 1) * 128, c0:c0 + w])
                self.cp(("pool", "dve", "act")[self._stg_i % 3], dst[:, kc, dcol0 + c0:dcol0 + c0 + w], s[:, 0:w])


def host_consts():
    n = 128
    i = np.arange(n)
    su = (i[:, None] < i[None, :]).astype(np.float32)
    iu = (i[:, None] <= i[None, :]).astype(np.float32)
    sl = (i[:, None] > i[None, :]).astype(np.float32)
    c = {}
    c["c_ident"] = np.eye(n, dtype=np.float32)
    c["c_ms"] = np.concatenate([su, iu, su, iu], axis=1)
    c["c_ml"] = np.concatenate([sl, sl, sl, sl], axis=1)
    c["c_tri_inc"] = (-EH * iu).astype(np.float32)
    c["c_tri_rev"] = (-EH * sl).astype(np.float32)
    l128 = np.zeros((n, n), np.float32); l128[127, :] = 1.0
    l16 = np.zeros((n, n), np.float32); l16[15, :] = 1.0
    c["c_last128"] = l128
    c["c_last16"] = l16
    c["c_ones"] = np.ones((n, n), np.float32)
    c["c_tri1"] = iu.copy()
    return c


_WNAMES = ["rwkv_mu", "rwkv_w0", "rwkv_w1", "rwkv_w2", "rwkv_a0", "rwkv_a1", "rwkv_a2", "rwkv_g1", "rwkv_g2",
           "rwkv_k_k", "rwkv_k_a", "rwkv_r_k", "rwkv_w_r", "rwkv_w_k", "rwkv_w_v", "rwkv_w_o", "rwkv_lnx_g",
           "rwkv_lnx_b", "fox_w_in", "fox_b_f", "fox_w_o", "ffn_w1", "ffn_w2", "ln_mix_g", "ln_mix_b",
           "ln_ffn_g", "ln_ffn_b"]


def make_in_maps(inputs, NP, NS, ncores=8):
    c = host_consts()
    f = lambda a: np.ascontiguousarray(np.asarray(a, dtype=np.float32))
    wd = {}
    for nm in _WNAMES:
        a = f(inputs[nm])
        if nm.startswith("rwkv") or nm.startswith("fox"):
            a = a[0]
        if nm == "rwkv_r_k":
            a = a.reshape(-1)
        wd[nm] = np.ascontiguousarray(a)
    TP = NP * 128
    maps = []
    for core in range(ncores):
        b = core % inputs["x_prompt"].shape[0]
        ss = [(core * NS + j) % inputs["x_sample"].shape[0] for j in range(NS)]
        m = dict(c)
        m.update(wd)
        xp = f(inputs["x_prompt"][b, :TP])
        xs = [f(inputs["x_sample"][s]) for s in ss]
        m["xin"] = np.ascontiguousarray(np.concatenate([xp] + xs, axis=0))
        sel = ss if NS > 0 else [0]
        m["sshift"] = np.ascontiguousarray(f(inputs["state_shift"][0])[sel])
        m["swkv"] = np.ascontiguousarray(f(inputs["state_wkv"][0])[sel])
        m["ck"] = np.ascontiguousarray(f(inputs["cache_k"][0])[sel].reshape(len(sel), PAST, D))
        m["cv"] = np.ascontiguousarray(f(inputs["cache_v"][0])[sel].reshape(len(sel), PAST, D))
        m["clf"] = np.ascontiguousarray(f(inputs["cache_logf"][0])[sel])
        maps.append(m)
    return maps


_PROG_CACHE = {}


def get_prog(NP, NS, do_fox=True):
    key = (NP, NS, do_fox)
    if key not in _PROG_CACHE:
        p = Prog(NP, NS)
        p.do_fox = do_fox
        p.build()
        _PROG_CACHE[key] = p
    return _PROG_CACHE[key]


def kernel(**inputs):
    NP, NS = 32, 2
    prog = get_prog(NP, NS, do_fox=True)
    maps = make_in_maps(inputs, NP, NS)
    res = run_bass_kernel_spmd(prog.nc, maps, core_ids=list(range(8))).results
    B, DB = 4, 16
    TP = NP * 128
    y_p = np.stack([res[b]["y"][:TP] for b in range(B)])
    y_s = np.stack([res[s // 2]["y"][TP + (s % 2) * 16:TP + (s % 2 + 1) * 16] for s in range(DB)])
    wkv_p = np.stack([res[b]["wkv_p"] for b in range(B)])[None]
    shift_p = np.stack([res[b]["shift_p"][0] for b in range(B)])[None]
    k_p = np.stack([res[b]["k_p"].reshape(TP, H, HD) for b in range(B)])[None]
    v_p = np.stack([res[b]["v_p"].reshape(TP, H, HD) for b in range(B)])[None]
    lf_p = np.stack([res[b]["lf_p"] for b in range(B)])[None]
    wkv_s = np.stack([res[s // 2]["wkv_s"][s % 2] for s in range(DB)])[None]
    shift_s = np.stack([res[s // 2]["shift_s"][s % 2] for s in range(DB)])[None]
    k_s = np.stack([res[s // 2]["k_s"][(s % 2) * 16:(s % 2 + 1) * 16].reshape(16, H, HD) for s in range(DB)])[None]
    v_s = np.stack([res[s // 2]["v_s"][(s % 2) * 16:(s % 2 + 1) * 16].reshape(16, H, HD) for s in range(DB)])[None]
    lf_s = np.stack([res[s // 2]["lf_s"][(s % 2) * 16:(s % 2 + 1) * 16] for s in range(DB)])[None]
    outs = (y_p, y_s, wkv_p, shift_p, k_p, v_p, lf_p, wkv_s, shift_s, k_s, v_s, lf_s)
    return tuple(np.ascontiguousarray(o, dtype=np.float32) for o in outs)
```

```python
import math
import numpy as np
from contextlib import ExitStack
import concourse.bass as bass
import concourse.mybir as mybir
from concourse.bass_utils import run_bass_kernel_spmd

F32 = mybir.dt.float32
BF16 = mybir.dt.bfloat16
AF = mybir.ActivationFunctionType
ALU = mybir.AluOpType
AX = mybir.AxisListType

D = 1024
H = 16
HD = 64
DFF = 4096
ALPHA = 4.0 ** 0.25
LN_EPS = 1e-5
GN_EPS = 64e-5
EH = math.exp(-0.5)
LW, LA, LG = 64, 64, 160
PAST = 2048


class Buf:
    __slots__ = ("t", "name", "lw", "rd", "dkey", "is_dram")

    def __init__(self, t, name, dkey=None):
        self.t = t
        self.name = name
        self.lw = None
        self.rd = {}
        self.dkey = dkey or name
        self.is_dram = False


class V:
    __slots__ = ("buf", "ap")

    def __init__(self, buf, ap=None):
        self.buf = buf
        self.ap = buf.t[:] if ap is None else ap

    def __getitem__(self, k):
        return V(self.buf, self.ap[k])

    def re(self, pat, **kw):
        return V(self.buf, self.ap.rearrange(pat, **kw))

    def bc(self, dt):
        return V(self.buf, self.ap.bitcast(dt))

    def bcast(self, shape):
        return V(self.buf, self.ap.broadcast_to(list(shape)))

    def unsq(self, ax):
        return V(self.buf, self.ap.unsqueeze(ax))


class Eng:
    def __init__(self, key, h):
        self.key = key
        self.h = h
        self.sem = None
        self.cnt = 0
        self.seen = {}


class FW:
    def __init__(self, nc, es):
        self.nc = nc
        self.es = es
        self.engs = {}
        for key, h in (("pe", nc.tensor), ("act", nc.scalar), ("dve", nc.vector),
                       ("pool", nc.gpsimd), ("sp", nc.sync)):
            e = Eng(key, h)
            e.sem = es.enter_context(nc.semaphore("sem_" + key))
            self.engs[key] = e
        self.dsem = {}
        self.dcnt = {}
        self.nb = 0

    def sb(self, es, shape, dt, name, dkey=None):
        self.nb += 1
        t = es.enter_context(self.nc.sbuf_tensor(f"{name}_{self.nb}", list(shape), dt))
        return V(Buf(t, f"{name}_{self.nb}", dkey or name))

    def ps(self, es, shape, dt, name):
        self.nb += 1
        t = es.enter_context(self.nc.psum_tensor(f"{name}_{self.nb}", list(shape), dt))
        return V(Buf(t, f"{name}_{self.nb}"))

    def _deps(self, reads, writes):
        deps = {}

        def add(k, c):
            if deps.get(k, 0) < c:
                deps[k] = c
        for b in reads:
            if b.lw is not None:
                add(*b.lw)
        for b in writes:
            if b.lw is not None:
                add(*b.lw)
            for k, c in b.rd.items():
                add(k, c)
        return deps

    def _semof(self, k):
        return self.dsem[k] if isinstance(k, tuple) else self.engs[k].sem

    def _waits(self, e, deps):
        for k, c in deps.items():
            if k == "pe" and e.key == "pe":
                continue
            if e.seen.get(k, 0) >= c:
                continue
            e.h.wait_ge(self._semof(k), c)
            e.seen[k] = c

    def _mark(self, key, cnt, reads, writes):
        for b in reads:
            if b.rd.get(key, 0) < cnt:
                b.rd[key] = cnt
        for b in writes:
            b.lw = (key, cnt)
            b.rd = {}

    def op(self, eng, fn, reads=(), writes=()):
        e = self.engs[eng]
        reads = [v.buf for v in reads]
        writes = [v.buf for v in writes]
        self._waits(e, self._deps(reads, writes))
        ins = fn()
        e.cnt += 1
        ins.then_inc(e.sem, 1)
        self._mark(e.key, e.cnt, reads, writes)

    def dma(self, eng, out, in_, owner=None):
        e = self.engs[eng]
        if owner is None:
            owner = out if not getattr(out.buf, "is_dram", False) else in_
        qk = "sw" if eng == "pool" else "hw"
        key = ("d", owner.buf.dkey, qk)
        if key not in self.dsem:
            self.dsem[key] = self.es.enter_context(self.nc.semaphore("ds_" + qk + "_" + owner.buf.dkey))
            self.dcnt[key] = 0
        reads, writes = [in_.buf], [out.buf]
        self._waits(e, self._deps(reads, writes))
        ins = e.h.dma_start(out=out.ap, in_=in_.ap)
        self.dcnt[key] += 16
        ins.then_inc(self.dsem[key], 16)
        self._mark(key, self.dcnt[key], reads, writes)

    def barrier(self):
        for e in self.engs.values():
            for key, c in self.dcnt.items():
                if c > 0 and e.seen.get(key, 0) < c:
                    e.h.wait_ge(self.dsem[key], c)
                    e.seen[key] = c
            for k, o in self.engs.items():
                if o.cnt > 0 and e.seen.get(k, 0) < o.cnt:
                    e.h.wait_ge(o.sem, o.cnt)
                    e.seen[k] = o.cnt


def dram_v(ap, name="dram"):
    b = Buf(None, name)
    b.is_dram = True
    return V(b, ap)


class Slots:
    def __init__(self, fw, es, n, shape, dt, name):
        self.free = [fw.sb(es, shape, dt, f"{name}{i}", dkey=f"{name}{i}") for i in range(n)]

    def get(self):
        return self.free.pop(0)

    def put(self, *vs):
        for v in vs:
            self.free.append(V(v.buf))


class Prog:
    def __init__(self, NP, NS):
        self.NP, self.NS = NP, NS
        self.TP = NP * 128
        self.TT = self.TP + NS * 16
        self.tiles = [(i * 128, 128, "p", i) for i in range(NP)] + \
                     [(self.TP + j * 16, 16, "s", j) for j in range(NS)]
        nc = self.nc = bass.Bass("TRN2", target_bir_lowering=False)
        self.din = {}
        self.dout = {}
        self.do_fox = True

    def inp(self, name, shape, dt=F32):
        ap = self.nc.dram_tensor(name, list(shape), dt, kind="ExternalInput").ap()
        self.din[name] = dram_v(ap, name)
        return self.din[name]

    def outp(self, name, shape, dt=F32):
        ap = self.nc.dram_tensor(name, list(shape), dt, kind="ExternalOutput").ap()
        self.dout[name] = ap
        return ap

    def scratch(self, name, shape, dt):
        return self.nc.dram_tensor(name, list(shape), dt, kind="Internal").ap()

    def tt(self, eng, out, a, b, op):
        h = self.nc.vector if eng == "dve" else self.nc.gpsimd
        self.fw.op(eng, lambda: h.tensor_tensor(out.ap, a.ap, b.ap, op), [a, b], [out])

    def ts(self, eng, out, a, s1, s2, op0, op1=None):
        h = self.nc.vector if eng == "dve" else self.nc.gpsimd
        rd = [a] + [s for s in (s1, s2) if isinstance(s, V)]
        g = lambda s: s.ap if isinstance(s, V) else s
        if op1 is None:
            self.fw.op(eng, lambda: h.tensor_scalar(out.ap, a.ap, g(s1), None, op0), rd, [out])
        else:
            self.fw.op(eng, lambda: h.tensor_scalar(out.ap, a.ap, g(s1), g(s2), op0, op1), rd, [out])

    def stt(self, out, a, s, b, op0, op1):
        rd = [a, b] + ([s] if isinstance(s, V) else [])
        sv = s.ap if isinstance(s, V) else s
        self.fw.op("dve", lambda: self.nc.vector.scalar_tensor_tensor(out.ap, a.ap, sv, b.ap, op0, op1), rd, [out])

    def act(self, out, a, func, bias=None, scale=None):
        rd = [a] + ([bias] if isinstance(bias, V) else [])
        kw = {}
        if bias is not None:
            kw["bias"] = bias.ap if isinstance(bias, V) else bias
        if scale is not None:
            kw["scale"] = scale
        self.fw.op("act", lambda: self.nc.scalar.activation(out.ap, a.ap, func, **kw), rd, [out])

    def cp(self, eng, out, a):
        if eng == "act":
            self.fw.op("act", lambda: self.nc.scalar.copy(out.ap, a.ap), [a], [out])
        else:
            h = self.nc.vector if eng == "dve" else self.nc.gpsimd
            self.fw.op(eng, lambda: h.tensor_copy(out.ap, a.ap), [a], [out])

    def mm(self, out, groups):
        rd = []
        for l, r in groups:
            rd += [l, r]

        def fn():
            ins = None
            for i, (l, r) in enumerate(groups):
                ins = self.nc.tensor.matmul(out.ap, l.ap, r.ap, start=(i == 0), stop=(i == len(groups) - 1))
            return ins
        self.fw.op("pe", fn, rd, [out])

    def mms(self, items):
        rd, wr = [], []
        for o, groups in items:
            wr.append(o)
            for l, r in groups:
                rd += [l, r]

        def fn():
            ins = None
            for o, groups in items:
                for i, (l, r) in enumerate(groups):
                    ins = self.nc.tensor.matmul(o.ap, l.ap, r.ap, start=(i == 0), stop=(i == len(groups) - 1))
            return ins
        self.fw.op("pe", fn, rd, wr)

    def transposes(self, out_ps, src, ident, n, nchunk=8, w=128):
        def fn():
            ins = None
            for c in range(nchunk):
                ins = self.nc.tensor.transpose(out_ps.ap[:, c, 0:n], src.ap[0:n, c * w:(c + 1) * w], ident.ap[0:n, 0:n])
            return ins
        self.fw.op("pe", fn, [src, ident], [out_ps])

    def load_w(self, dst, src, K, N, stg, col0=0, ncols=None, dcol0=0):
        ncols = ncols or N
        KC = (K + 127) // 128
        SW = stg[0].ap.shape[-1]
        for kc in range(KC):
            rows = min(128, K - kc * 128)
            for c0 in range(0, ncols, SW):
                w = min(SW, ncols - c0)
                s = stg[self._stg_i % len(stg)]
                self._stg_i += 1
                self.fw.dma("sp", s[0:rows, 0:w], src[kc * 128:kc * 128 + rows, col0 + c0:col0 + c0 + w])
                self.cp(("pool", "dve", "act")[self._stg_i % 3], dst[0:rows, kc, dcol0 + c0:dcol0 + c0 + w], s[0:rows, 0:w])

    def load_rep(self, dst, src1d, n=D):
        self.fw.dma("sp", dst, V(src1d.buf, src1d.ap.partition_broadcast(128)))

    def layer_norm(self, z, n, g, b, st, mv, rs, out):
        nc = self.nc
        zz = z[0:n, :]
        self.fw.op("dve", lambda: nc.vector.bn_stats(st.ap[0:n, 0, :], z.ap[0:n, 0:512]), [z], [st])
        self.fw.op("dve", lambda: nc.vector.bn_stats(st.ap[0:n, 1, :], z.ap[0:n, 512:1024]), [z], [st])
        self.fw.op("dve", lambda: nc.vector.bn_aggr(mv.ap[0:n, :], st.ap[0:n].rearrange("p a b -> p (a b)")), [st], [mv])
        self.ts("dve", rs[0:n, :], mv[0:n, 1:2], LN_EPS, None, ALU.add)
        self.act(rs[0:n, :], rs[0:n, :], AF.Sqrt)
        self.fw.op("dve", lambda: nc.vector.reciprocal(rs.ap[0:n, :], rs.ap[0:n, :]), [rs], [rs])
        self.ts("dve", zz, zz, mv[0:n, 0:1], rs[0:n, 0:1], ALU.subtract, ALU.mult)
        self.tt("dve", zz, zz, g[0:n, :], ALU.mult)
        self.tt("dve", out[0:n, :], zz, b[0:n, :], ALU.add)

    def build(self):
        nc = self.nc
        NP, NS, TP, TT = self.NP, self.NS, self.TP, self.TT
        I = self.inp
        xin = I("xin", [TT, D])
        sshift = I("sshift", [max(NS, 1), D])
        swkv = I("swkv", [max(NS, 1), H, HD, HD])
        ck = I("ck", [max(NS, 1), PAST, D])
        cv = I("cv", [max(NS, 1), PAST, D])
        clf = I("clf", [max(NS, 1), PAST, H])
        W = {}
        for nm, shp in (("rwkv_mu", [6, D]), ("rwkv_w0", [D]), ("rwkv_w1", [D, LW]), ("rwkv_w2", [LW, D]),
                        ("rwkv_a0", [D]), ("rwkv_a1", [D, LA]), ("rwkv_a2", [LA, D]), ("rwkv_g1", [D, LG]),
                        ("rwkv_g2", [LG, D]), ("rwkv_k_k", [D]), ("rwkv_k_a", [D]), ("rwkv_r_k", [D]),
                        ("rwkv_w_r", [D, D]), ("rwkv_w_k", [D, D]), ("rwkv_w_v", [D, D]), ("rwkv_w_o", [D, D]),
                        ("rwkv_lnx_g", [D]), ("rwkv_lnx_b", [D]), ("fox_w_in", [D, 3 * D + H]), ("fox_b_f", [H]),
                        ("fox_w_o", [D, D]), ("ffn_w1", [2, D, DFF]), ("ffn_w2", [2, DFF, D]),
                        ("ln_mix_g", [2, D]), ("ln_mix_b", [2, D]), ("ln_ffn_g", [2, D]), ("ln_ffn_b", [2, D])):
            W[nm] = I(nm, shp)
        cst = {}
        for nm, shp in (("c_ident", [128, 128]), ("c_ms", [128, 512]), ("c_ml", [128, 512]), ("c_tri_inc", [128, 128]),
                        ("c_tri_rev", [128, 128]), ("c_last128", [128, 128]), ("c_last16", [128, 128]),
                        ("c_ones", [128, 128]), ("c_tri1", [128, 128])):
            cst[nm] = I(nm, shp)
        self.W, self.cst = W, cst

        O = self.outp
        self.o_y = O("y", [TT, D])
        self.o_wkvp = O("wkv_p", [H, HD, HD])
        self.o_shiftp = O("shift_p", [1, D])
        self.o_kp = O("k_p", [max(TP, 1), D])
        self.o_vp = O("v_p", [max(TP, 1), D])
        self.o_lfp = O("lf_p", [max(TP, 1), H])
        self.o_wkvs = O("wkv_s", [max(NS, 1), H, HD, HD])
        self.o_shifts = O("shift_s", [max(NS, 1), D])
        self.o_ks = O("k_s", [max(NS, 1) * 16, D])
        self.o_vs = O("v_s", [max(NS, 1) * 16, D])
        self.o_lfs = O("lf_s", [max(NS, 1) * 16, H])

        self.o_scr = self.scratch("o_scr", [TT, D], BF16)
        self.x1_scr = self.scratch("x1_scr", [TT, D], F32)
        self.TK = TP + NS * (PAST + 16)
        self.qT_scr = self.scratch("qT_scr", [H, 68, TT], BF16)
        self.kT_scr = self.scratch("kT_scr", [H, 68, self.TK], BF16)
        self.v1_scr = self.scratch("v1_scr", [self.TK, H, 65], BF16)

        with ExitStack() as es:
            self.fw = FW(nc, es)
            self._stg_i = 0
            self.o_tiles = [dram_v(self.o_scr[r0:r0 + n, :], f"oscr{i}") for i, (r0, n, _, _) in enumerate(self.tiles)]
            self.x1_tiles = [dram_v(self.x1_scr[r0:r0 + n, :], f"x1scr{i}") for i, (r0, n, _, _) in enumerate(self.tiles)]
            self.y_tiles = [dram_v(self.o_y[r0:r0 + n, :], f"yout{i}") for i, (r0, n, _, _) in enumerate(self.tiles)]
            self.xin_tiles = [V(xin.buf, xin.ap[r0:r0 + n, :]) for (r0, n, _, _) in self.tiles]
            import os
            self.dbg = os.environ.get('KDBG', '')
            if self.dbg != 'B' and not self.dbg.startswith('C'):
                self.phase_rwkv(xin, sshift, swkv)
            self.fw.barrier()
            if self.dbg.startswith('A'):
                return nc
            if self.dbg.startswith('C'):
                self.phase_fox(ck, cv, clf)
                self.fw.barrier()
                return nc
            self.phase_ffn(0, self.xin_tiles, self.x1_tiles if self.do_fox else self.y_tiles, W["rwkv_w_o"])
            self.fw.barrier()
            if self.do_fox:
                self.phase_fox(ck, cv, clf)
                self.fw.barrier()
                self.phase_ffn(1, self.x1_tiles, self.y_tiles, W["fox_w_o"])
                self.fw.barrier()
        return nc

    def phase_ffn(self, L, res_tiles, dst_tiles, wo_d):
        nc, fw, W, cst = self.nc, self.fw, self.W, self.cst
        with ExitStack() as es:
            sb = lambda shape, dt, name: fw.sb(es, shape, dt, name + f"L{L}", dkey=name)
            wo = sb([128, 8, D], BF16, "b_wo")
            w1 = sb([128, 8, DFF], BF16, "b_w1")
            w2 = sb([128, 32, D], BF16, "b_w2")
            idb = sb([128, 128], BF16, "b_idb")
            g1, b1, g2, b2 = [sb([128, D], F32, f"b_ln{i}") for i in range(4)]
            with ExitStack() as es2:
                stg = [fw.sb(es2, [128, 2048], F32, f"b_stg{i}L{L}", dkey=f"b_stg{i}") for i in range(4)]
                fw.dma("sp", stg[0][:, 0:128], cst["c_ident"])
                self.cp("pool", idb, stg[0][:, 0:128])
                self._stg_i = 1
                self.load_w(wo, wo_d, D, D, stg)
                self.load_w(w1, V(W["ffn_w1"].buf, W["ffn_w1"].ap[L]), D, DFF, stg)
                self.load_w(w2, V(W["ffn_w2"].buf, W["ffn_w2"].ap[L]), DFF, D, stg)
                fw.barrier()
            for dst, nm in ((g1, "ln_mix_g"), (b1, "ln_mix_b"), (g2, "ln_ffn_g"), (b2, "ln_ffn_b")):
                self.load_rep(dst, V(W[nm].buf, W[nm].ap[L]))
            NBUF = 3
            ob = [sb([128, D], BF16, f"b_ob{i}") for i in range(NBUF)]
            A = [sb([128, D], F32, f"b_A{i}") for i in range(NBUF)]
            oT = sb([128, 8, 128], BF16, "b_oT")
            xmb = sb([128, D], BF16, "b_xmb")
            xmT = sb([128, 8, 128], BF16, "b_xmT")
            hr = [sb([128, 512], F32, f"b_hr{i}") for i in range(2)]
            hT = sb([128, 32, 128], BF16, "b_hT")
            st = sb([128, 2, 6], F32, "b_st")
            mv = sb([128, 2], F32, "b_mv")
            rs = sb([128, 1], F32, "b_rs")
            st2 = sb([128, 2, 6], F32, "b_st2")
            mv2 = sb([128, 2], F32, "b_mv2")
            rs2 = sb([128, 1], F32, "b_rs2")
            pT = fw.ps(es, [128, 8, 128], BF16, "b_pT")
            pz = fw.ps(es, [128, D], F32, "b_pz")
            ph = [fw.ps(es, [128, 512], F32, f"b_ph{i}") for i in range(2)]
            pz2 = fw.ps(es, [128, D], F32, "b_pz2")
            T = self.tiles

            def P1(ti):
                r0, n, kind, j = T[ti]
                p = ti % NBUF
                a = A[p]
                fw.dma("sp", ob[p][0:n, :], self.o_tiles[ti])
                fw.dma("sp", a[0:n, :], res_tiles[ti])
                self.transposes(pT, ob[p], idb, n)
                self.cp("act", oT[:, :, 0:n], pT[:, :, 0:n])
                self.mms([(pz[0:n, hf * 512:(hf + 1) * 512],
                           [(oT[:, kc, 0:n], wo[:, kc, hf * 512:(hf + 1) * 512]) for kc in range(8)])
                          for hf in range(2)])

            def P2(ti):
                r0, n, kind, j = T[ti]
                a = A[ti % NBUF]
                self.stt(a[0:n, :], a[0:n, :], ALPHA, pz[0:n, :], ALU.mult, ALU.add)
                self.layer_norm(a, n, g1, b1, st, mv, rs, a)
                self.cp("dve", xmb[0:n, :], a[0:n, :])

            def P3(ti):
                r0, n, kind, j = T[ti]
                self.transposes(pT, xmb, idb, n)
                self.cp("act", xmT[:, :, 0:n], pT[:, :, 0:n])

            def F1(ti):
                r0, n, kind, j = T[ti]
                for fg in range(8):
                    pp = ph[fg % 2]
                    self.mms([(pp[:, q * 128:q * 128 + n],
                               [(w1[:, kc, (fg * 4 + q) * 128:(fg * 4 + q + 1) * 128], xmT[:, kc, 0:n]) for kc in range(8)])
                              for q in range(4)])
                    h_ = hr[fg % 2]
                    ppv = pp.re("p (q t) -> p q t", q=4)[:, :, 0:n]
                    hv = h_.re("p (q t) -> p q t", q=4)[:, :, 0:n]
                    self.act(hv, ppv, AF.Relu)
                    self.tt("pool", hT[:, fg * 4:fg * 4 + 4, 0:n], hv, hv, ALU.mult)

            def F2(ti):
                r0, n, kind, j = T[ti]
                a = A[ti % NBUF]
                self.mms([(pz2[0:n, hf * 512:(hf + 1) * 512],
                           [(hT[:, fc, 0:n], w2[:, fc, hf * 512:(hf + 1) * 512]) for fc in range(32)])
                          for hf in range(2)])
                self.stt(a[0:n, :], a[0:n, :], ALPHA, pz2[0:n, :], ALU.mult, ALU.add)
                self.layer_norm(a, n, g2, b2, st2, mv2, rs2, a)
                fw.dma("pool", dst_tiles[ti], a[0:n, :])

            NT = len(T)
            if NT:
                P1(0); P2(0); P3(0)
            if NT > 1:
                P1(1); P2(1)
            for ti in range(NT):
                F1(ti)
                if ti + 1 < NT:
                    P3(ti + 1)
                if ti + 2 < NT:
                    P1(ti + 2)
                F2(ti)
                if ti + 2 < NT:
                    P2(ti + 2)
            fw.barrier()

    def phase_rwkv(self, xin, sshift, swkv):
        nc, fw, W, cst = self.nc, self.fw, self.W, self.cst
        TP = self.TP
        with ExitStack() as es:
            sb = lambda shape, dt, name: fw.sb(es, shape, dt, "a_" + name)
            S4 = Slots(fw, es, 11, [128, D], F32, "a_s4_")
            S2 = Slots(fw, es, 13, [128, D], BF16, "a_s2_")
            wr, wk, wv = [sb([128, 8, D], BF16, nm) for nm in ("wr", "wk", "wv")]
            wli = sb([128, 8, 288], BF16, "wli")
            w2e = sb([128, D], BF16, "w2e")
            a2e = sb([128, D], BF16, "a2e")
            g2a = sb([128, 1, D], BF16, "g2a")
            g2b = sb([128, 1, D], BF16, "g2b")[0:32]
            kkc, kac, lgc, lbc = [sb([128, D], F32, nm) for nm in ("kkc", "kac", "lgc", "lbc")]
            rkc = sb([128, D], BF16, "rkc")
            mu = [sb([128, D], BF16, f"mu{i}") for i in range(6)]
            idb = sb([128, 128], BF16, "idb")
            idf = sb([128, 128], F32, "idf")
            ms = sb([128, 4, 128], BF16, "ms")
            ml = sb([128, 4, 128], BF16, "ml")
            tri_inc = sb([128, 128], F32, "tri_inc")
            tri_rev = sb([128, 128], F32, "tri_rev")
            ST = sb([128, 8, 64], F32, "ST")
            STb = sb([128, 8, 64], BF16, "STb")
            hw = sb([128, 128], BF16, "hw")
            ha = sb([128, 128], BF16, "ha")
            hg1 = sb([128, 128], BF16, "hg1")
            hg2 = sb([128, 128], BF16, "hg2")[0:32]
            mixT = [sb([128, 8, 128], BF16, f"mixT{i}") for i in range(2)]
            ARs = [sb([128, 8, 2, 128], BF16, f"AR{i}") for i in range(2)]
            BTs = [sb([128, 8, 128], BF16, f"BT{i}") for i in range(2)]
            KTs = [sb([128, 8, 128], BF16, f"KT{i}") for i in range(2)]
            Abk = sb([128, 8, 4, 128], BF16, "Abk")
            PA = sb([128, 8, 128], BF16, "PA")
            PB = sb([128, 8, 128], BF16, "PB")
            PTA = sb([128, 8, 128], BF16, "PTA")
            PTB = sb([128, 8, 128], BF16, "PTB")
            Xb = [sb([128, 512], BF16, f"Xb{i}") for i in range(2)]
            gCs = [sb([128, 8, 2], F32, f"gC{i}") for i in range(2)]
            sm = [sb([128, 16], F32, f"sm{i}") for i in range(4)]
            smf = [sb([128, 16], F32, f"smf{i}") for i in range(2)]
            pT = fw.ps(es, [128, 8, 128], BF16, "a_pT")
            pbA = fw.ps(es, [128, D], F32, "a_pbA")
            pbB = fw.ps(es, [128, D], F32, "a_pbB")
            phid = fw.ps(es, [128, 512], F32, "a_phid")
            psc = [fw.ps(es, [128, 512], F32, f"a_psc{i}") for i in range(2)]

            s0, s1 = S4.get(), S4.get()
            stg = [s0, s1]
            self._stg_i = 0

            def ldc(dst, src, w, cast_eng="pool"):
                s = stg[self._stg_i % 2]
                self._stg_i += 1
                fw.dma("sp", s[:, 0:w], src)
                self.cp(cast_eng, dst, s[:, 0:w])
            ldc(idb, cst["c_ident"], 128)
            fw.dma("sp", idf, cst["c_ident"])
            ldc(ms.re("p a b -> p (a b)"), cst["c_ms"], 512)
            ldc(ml.re("p a b -> p (a b)"), cst["c_ml"], 512)
            fw.dma("sp", tri_inc, cst["c_tri_inc"])
            fw.dma("sp", tri_rev, cst["c_tri_rev"])
            for i in range(6):
                s = stg[self._stg_i % 2]
                self._stg_i += 1
                self.load_rep(s, V(W["rwkv_mu"].buf, W["rwkv_mu"].ap[i]))
                self.cp("pool", mu[i], s)
            for dst, nm in ((kkc, "rwkv_k_k"), (kac, "rwkv_k_a"), (lgc, "rwkv_lnx_g"), (lbc, "rwkv_lnx_b")):
                self.load_rep(dst, W[nm])
            s = stg[self._stg_i % 2]
            self._stg_i += 1
            self.load_rep(s, W["rwkv_r_k"])
            self.cp("pool", rkc, s)
            for dst, nm in ((wr, "rwkv_w_r"), (wk, "rwkv_w_k"), (wv, "rwkv_w_v")):
                self.load_w1k(dst, W[nm], D, D, stg)
            self.load_w1k(wli, W["rwkv_w1"], D, LW, stg, dcol0=0)
            self.load_w1k(wli, W["rwkv_a1"], D, LA, stg, dcol0=64)
            self.load_w1k(wli, W["rwkv_g1"], D, LG, stg, dcol0=128)
            for dst, wn, bn in ((w2e, "rwkv_w2", "rwkv_w0"), (a2e, "rwkv_a2", "rwkv_a0")):
                fw.op("pool", lambda: nc.gpsimd.memset(dst.ap, 0.0), [], [dst])
                s = stg[self._stg_i % 2]
                self._stg_i += 1
                fw.dma("sp", s[0:64, :], W[wn])
                self.cp("pool", dst[0:64, :], s[0:64, :])
                s = stg[self._stg_i % 2]
                self._stg_i += 1
                bsrc = V(W[bn].buf, W[bn].ap.partition_broadcast(128))
                fw.dma("sp", s[64:65, :], bsrc[64:65, :])
                fw.dma("sp", s[96:97, :], bsrc[96:97, :])
                self.cp("pool", dst[64:65, :], s[64:65, :])
                self.cp("pool", dst[96:97, :], s[96:97, :])
                self.tt("pool", dst[96:97, :], s[96:97, :], dst[96:97, :], ALU.subtract)
            s = stg[self._stg_i % 2]
            self._stg_i += 1
            fw.dma("sp", s[:, 0:D], W["rwkv_g2"][0:128, :])
            self.cp("pool", g2a[:, 0, :], s[:, 0:D])
            s = stg[self._stg_i % 2]
            self._stg_i += 1
            fw.dma("sp", s[0:32, 0:D], W["rwkv_g2"][128:160, :])
            self.cp("pool", g2b[:, 0, :], s[0:32, 0:D])
            for hh_ in (hw, ha):
                fw.op("pool", lambda: nc.gpsimd.memset(hh_.ap, 0.0), [], [hh_])
                fw.op("pool", lambda: nc.gpsimd.memset(hh_.ap[64:65, :], 1.0), [], [hh_])
                fw.op("pool", lambda: nc.gpsimd.memset(hh_.ap[96:97, :], 1.0), [], [hh_])
            S4.put(s0, s1)

            def out_state(dst_ap, name):
                pso = pbA.re("p (c q) -> p c q", c=8)

                def fn():
                    ins = None
                    for c in range(8):
                        ins = nc.tensor.transpose(pso.ap[0:64, c, :], ST.ap[:, c, :], idf.ap)
                    return ins
                fw.op("pe", fn, [ST, idf], [pbA])
                t = S4.get()
                self.cp("act", t[0:64, :], pbA[0:64, :])
                fw.dma("pool", dram_v(dst_ap.rearrange("(c hh) v k -> v c hh k", hh=2), name),
                       t[0:64, :].re("p (c hh k) -> p c hh k", c=8, hh=2))
                S4.put(t)

            if self.dbg == 'A0':
                fw.barrier()
                return
            ctxs = {}
            def front(ti):
                r0, n, kind, j = self.tiles[ti]
                nlev = int(round(math.log2(n)))
                first = (j == 0) if kind == "p" else True
                par = ti % 2
                AR, BT, KT, gC = ARs[par], BTs[par], KTs[par], gCs[par]
                x32, xp = S4.get(), S4.get()
                fw.dma("sp", x32[0:n, :], xin[r0:r0 + n, :])
                if first:
                    if kind == "p":
                        fw.op("pool", lambda: nc.gpsimd.memset(xp.ap[0:1, :], 0.0), [], [xp])
                    else:
                        fw.dma("sp", xp[0:1, :], sshift[j:j + 1, :])
                    fw.dma("sp", xp[1:n, :], xin[r0:r0 + n - 1, :])
                else:
                    fw.dma("sp", xp[0:n, :], xin[r0 - 1:r0 + n - 1, :])
                self.tt("pool", xp[0:n, :], xp[0:n, :], x32[0:n, :], ALU.subtract)
                r32 = k32 = sg = a32 = vb = gb = None
                for i in range(6):
                    t = S4.get()
                    self.tt("pool", t[0:n, :], xp[0:n, :], mu[i][0:n, :], ALU.mult)
                    mb = S2.get()
                    self.tt("pool", mb[0:n, :], t[0:n, :], x32[0:n, :], ALU.add)
                    S4.put(t)
                    self.transposes(pT, mb, idb, n)
                    mT = mixT[i % 2]
                    self.cp("act", mT[:, :, 0:n], pT[:, :, 0:n])
                    S2.put(mb)
                    yield

                    def proj(pb, w):
                        self.mms([(pb[0:n, hf * 512:(hf + 1) * 512],
                                   [(mT[:, kc, 0:n], w[:, kc, hf * 512:(hf + 1) * 512]) for kc in range(8)])
                                  for hf in range(2)])

                    def lora_out(pb, pairs):
                        self.mms([(pb[0:n, hf * 512:(hf + 1) * 512],
                                   [(hl_, w_[:, hf * 512:(hf + 1) * 512]) for hl_, w_ in pairs])
                                  for hf in range(2)])
                    if i == 0:
                        proj(pbA, wr)
                        r32 = S4.get()
                        self.cp("act", r32[0:n, :], pbA[0:n, :])
                    elif i == 1:
                        self.mm(phid[0:64, 0:n], [(wli[:, kc, 0:64], mT[:, kc, 0:n]) for kc in range(8)])
                        self.act(hw[0:64, 0:n], phid[0:64, 0:n], AF.Tanh)
                        lora_out(pbB, [(hw[:, 0:n], w2e)])
                        sg = S4.get()
                        self.act(sg[0:n, :], pbB[0:n, :], AF.Sigmoid)
                    elif i == 2:
                        proj(pbA, wk)
                        k32 = S4.get()
                        self.cp("act", k32[0:n, :], pbA[0:n, :])
                    elif i == 3:
                        proj(pbB, wv)
                        vb = S2.get()
                        self.cp("act", vb[0:n, :], pbB[0:n, :])
                    elif i == 4:
                        self.mm(phid[0:64, 0:n], [(wli[:, kc, 64:128], mT[:, kc, 0:n]) for kc in range(8)])
                        self.cp("act", ha[0:64, 0:n], phid[0:64, 0:n])
                        lora_out(pbA, [(ha[:, 0:n], a2e)])
                        a32 = S4.get()
                        self.act(a32[0:n, :], pbA[0:n, :], AF.Sigmoid)
                    else:
                        self.mm(phid[:, 0:n], [(wli[:, kc, 128:256], mT[:, kc, 0:n]) for kc in range(8)])
                        self.mm(phid[0:32, 128:128 + n], [(wli[:, kc, 256:288], mT[:, kc, 0:n]) for kc in range(8)])
                        self.act(hg1[:, 0:n], phid[:, 0:n], AF.Sigmoid)
                        self.act(hg2[:, 0:n], phid[0:32, 128:128 + n], AF.Sigmoid)
                        lora_out(pbB, [(hg1[:, 0:n], g2a[:, 0, :]), (hg2[:, 0:n], g2b[:, 0, :])])
                        gb = S2.get()
                        self.cp("act", gb[0:n, :], pbB[0:n, :])
                S4.put(x32, xp)
                if self.dbg == 'A1':
                    S4.put(r32, sg, k32, a32); S2.put(vb, gb)
                    return
                self.mms([(pbA[0:n, hf * 512:(hf + 1) * 512], [(tri_inc[0:n, 0:n], sg[0:n, hf * 512:(hf + 1) * 512])])
                          for hf in range(2)])
                self.mms([(pbB[0:n, hf * 512:(hf + 1) * 512], [(tri_rev[0:n, 0:n], sg[0:n, hf * 512:(hf + 1) * 512])])
                          for hf in range(2)])
                self.mms([(psc[0][:, c * 2:c * 2 + 2], [(sg[0:n, c * 128:(c + 1) * 128], tri_inc[0:n, n - 2:n])])
                          for c in range(8)])
                self.act(gC.re("p c q -> p (c q)"), psc[0][:, 0:16], AF.Exp)
                tA, tB, tC = S4.get(), S4.get(), S4.get()
                self.act(tA[0:n, :], pbA[0:n, :], AF.Exp)
                rt = S2.get()
                self.tt("dve", rt[0:n, :], r32[0:n, :], tA[0:n, :], ALU.mult)
                self.stt(tB[0:n, :], sg[0:n, :], EH, pbA[0:n, :], ALU.mult, ALU.add)
                self.act(tB[0:n, :], tB[0:n, :], AF.Exp)
                self.act(tA[0:n, :], pbA[0:n, :], AF.Exp, scale=-1.0)
                self.act(tC[0:n, :], pbB[0:n, :], AF.Exp)
                S4.put(sg)
                yield
                t1, sq = S4.get(), S4.get()
                h3 = lambda v_: v_[0:n, :].re("p (h d) -> p h d", h=16)
                b3 = lambda v_: v_[0:n, :].unsq(2).bcast([n, 16, 64])
                self.tt("dve", t1[0:n, :], k32[0:n, :], kkc[0:n, :], ALU.mult)
                self.act(sq[0:n, :], t1[0:n, :], AF.Square)
                fw.op("dve", lambda: nc.vector.tensor_reduce(smf[0].ap[0:n, :], h3(sq).ap, AX.X, ALU.add), [sq], [smf[0]])
                self.ts("dve", smf[0][0:n, :], smf[0][0:n, :], 1e-24, None, ALU.max)
                self.act(smf[0][0:n, :], smf[0][0:n, :], AF.Sqrt)
                fw.op("dve", lambda: nc.vector.reciprocal(smf[0].ap[0:n, :], smf[0].ap[0:n, :]), [smf[0]], [smf[0]])
                self.tt("dve", h3(t1), h3(t1), b3(smf[0]), ALU.mult)
                yield
                t2 = sq
                self.stt(t2[0:n, :], a32[0:n, :], -1.0, kac[0:n, :], ALU.add, ALU.mult)
                self.stt(t2[0:n, :], t2[0:n, :], 1.0, k32[0:n, :], ALU.add, ALU.mult)
                S4.put(k32)
                yield
                self.tt("pool", a32[0:n, :], t1[0:n, :], a32[0:n, :], ALU.mult)
                at, bt, kt, bh, kh = [S2.get() for _ in range(5)]
                self.stt(at[0:n, :], t1[0:n, :], -1.0, tB[0:n, :], ALU.mult, ALU.mult)
                self.tt("dve", bt[0:n, :], a32[0:n, :], tA[0:n, :], ALU.mult)
                self.tt("pool", kt[0:n, :], t2[0:n, :], tA[0:n, :], ALU.mult)
                yield
                self.tt("dve", bh[0:n, :], a32[0:n, :], tC[0:n, :], ALU.mult)
                self.tt("pool", kh[0:n, :], t2[0:n, :], tC[0:n, :], ALU.mult)
                S4.put(tA, tB, tC)
                yield
                tD = S4.get()
                self.tt("dve", tD[0:n, :], r32[0:n, :], t2[0:n, :], ALU.mult)
                self.tt("pool", tD[0:n, :], tD[0:n, :], rkc[0:n, :], ALU.mult)
                yield
                fw.op("dve", lambda: nc.vector.tensor_reduce(smf[1].ap[0:n, :], h3(tD).ap, AX.X, ALU.add), [tD], [smf[1]])
                self.tt("dve", h3(tD), b3(smf[1]), h3(vb), ALU.mult)
                S4.put(r32, t1, t2, a32)
                for src, dst in ((at, AR[:, :, 0, 0:n]), (rt, AR[:, :, 1, 0:n]), (bt, BT[:, :, 0:n]), (kt, KT[:, :, 0:n])):
                    self.transposes(pT, src, idb, n)
                    self.cp("act", dst, pT[:, :, 0:n])
                    yield
                S2.put(at, rt, bt, kt)
                if self.dbg == 'A2':
                    S4.put(tD); S2.put(vb, gb, bh, kh)
                    return
                ctxs[ti] = dict(vb=vb, gb=gb, bh=bh, kh=kh, tD=tD)
                yield

            def back(ti):
                r0, n, kind, j = self.tiles[ti]
                nlev = int(round(math.log2(n)))
                first = (j == 0) if kind == "p" else True
                par = ti % 2
                AR, BT, KT, gC = ARs[par], BTs[par], KTs[par], gCs[par]
                c_ = ctxs.pop(ti)
                vb, gb, bh, kh, tD = c_['vb'], c_['gb'], c_['bh'], c_['kh'], c_['tD']
                h3 = lambda v_: v_[0:n, :].re("p (h d) -> p h d", h=16)
                b3 = lambda v_: v_[0:n, :].unsq(2).bcast([n, 16, 64])
                if kind == "p" and j == 0:
                    fw.op("pool", lambda: nc.gpsimd.memset(ST.ap, 0.0), [], [ST])
                    fw.op("pool", lambda: nc.gpsimd.memset(STb.ap, 0.0), [], [STb])
                elif kind == "s":
                    t = S4.get()
                    fw.dma("sp", t[0:64, :].re("p (h k) -> p h k", h=16),
                           V(swkv.buf, swkv.ap[j].rearrange("h v k -> v h k")))
                    psi = pbA[:, 0:512].re("p (c v) -> p c v", c=8)

                    def fn():
                        ins = None
                        for c in range(8):
                            ins = nc.tensor.transpose(psi.ap[:, c, :], t.ap[0:64, c * 128:(c + 1) * 128], idf.ap[0:64, 0:64])
                        return ins
                    fw.op("pe", fn, [t, idf], [pbA])
                    self.cp("act", ST, psi)
                    self.cp("pool", STb, ST)
                    S4.put(t)
                yield
                y32 = S4.get()
                for hh in range(2):
                    hd = lambda hl: (hh * 8 + hl, (hh * 8 + hl) // 2, ((hh * 8 + hl) % 2) * 64, (hh * 8 + hl) * 64)
                    for hl in range(8):
                        h, c, po, col = hd(hl)
                        pr = slice(po, po + 64)
                        pp = psc[hl % 2]
                        ppv = pp.re("p (a t) -> p a t", a=4)
                        if n == 128:
                            self.mms([(ppv[0:n, 0:2, 0:n], [(BT[pr, c, 0:n], AR[pr, c, :, 0:n])]),
                                      (ppv[0:n, 2:4, 0:n], [(KT[pr, c, 0:n], AR[pr, c, :, 0:n])])])
                        else:
                            self.mms([(ppv[0:n, 2 * w_ + a_, 0:n], [(src_[pr, c, 0:n], AR[pr, c, a_, 0:n])])
                                      for w_, src_ in enumerate((BT, KT)) for a_ in range(2)])
                        self.tt("dve", Abk[0:n, hl, :, 0:n], ppv[0:n, :, 0:n], ms[0:n, :, 0:n], ALU.mult)
                    xc = lambda hl: (hl % 2) * 256 + (hl // 2) * 64
                    pc = lambda hl: (hl % 2) * 512 + (hl // 2) * 64
                    banks = (phid.re("p (a t) -> p a t", a=4), psc[0].re("p (a t) -> p a t", a=4))
                    self.mms([(banks[hl % 2][0:n, hl // 2, 0:n],
                               [(AR[slice(hd(hl)[2], hd(hl)[2] + 64), hd(hl)[1], 0, 0:n],
                                 BT[slice(hd(hl)[2], hd(hl)[2] + 64), hd(hl)[1], 0:n])]) for hl in range(8)])
                    PTAv = PTA.re("p (g two) t -> p g two t", two=2)
                    for par in range(2):
                        self.tt("dve", PTAv[0:n, :, par, 0:n], banks[par][0:n, :, 0:n], ml[0:n, :, 0:n], ALU.mult)
                    yield
                    if self.dbg == 'A3':
                        return
                    items = []
                    for hl in range(8):
                        h, c, po, col = hd(hl)
                        pr = slice(po, po + 64)
                        items.append((pbA[0:n, pc(hl):pc(hl) + 64],
                                      [(AR[pr, c, 0, 0:n], STb[pr, c, :]),
                                       (Abk[0:n, hl, 2, 0:n], vb[0:n, col:col + 64])]))
                    self.mms(items)
                    self.cp("act", Xb[0][0:n, :].re("p (b x) -> p b x", b=2),
                            pbA[0:n, :].re("p (b x) -> p b x", b=2)[:, :, 0:256])
                    xi = 0
                    yield
                    P, PT = Abk[:, :, 0, :], PTA
                    Pn, PTn = PB, PTB
                    for lev in range(nlev):
                        pX = pbB
                        self.mms([(pX[0:n, xc(hl):xc(hl) + 64],
                                   [(idb[0:n, 0:n], Xb[xi][0:n, xc(hl):xc(hl) + 64]),
                                    (P[0:n, hl, 0:n], Xb[xi][0:n, xc(hl):xc(hl) + 64])]) for hl in range(8)])
                        self.cp("act", Xb[1 - xi][0:n, :], pX[0:n, 0:512])
                        xi = 1 - xi
                        yield
                        if lev < nlev - 1:
                            for g4 in range(2):
                                pq = psc[g4].re("p (a t) -> p a t", a=4)
                                self.mms([(pq[0:n, q, 0:n], [(PT[0:n, g4 * 4 + q, 0:n], P[0:n, g4 * 4 + q, 0:n])]) for q in range(4)])
                                self.cp("dve", Pn[0:n, g4 * 4:g4 * 4 + 4, 0:n], pq[0:n, :, 0:n])
                            for g4 in range(2):
                                pq = phid.re("p (a t) -> p a t", a=4)
                                self.mms([(pq[0:n, q, 0:n], [(P[0:n, g4 * 4 + q, 0:n], PT[0:n, g4 * 4 + q, 0:n])]) for q in range(4)])
                                self.cp("act", PTn[0:n, g4 * 4:g4 * 4 + 4, 0:n], pq[0:n, :, 0:n])
                            P, PT, Pn, PTn = Pn, PTn, (PA if Pn is PB else PB), (PTA if PTn is PTB else PTB)
                    UT = Xb[xi]
                    yield
                    items = []
                    for hl in range(8):
                        h, c, po, col = hd(hl)
                        pr = slice(po, po + 64)
                        items.append((pbA[0:n, pc(hl):pc(hl) + 64],
                                      [(AR[pr, c, 1, 0:n], STb[pr, c, :]),
                                       (Abk[0:n, hl, 1, 0:n], UT[0:n, xc(hl):xc(hl) + 64]),
                                       (Abk[0:n, hl, 3, 0:n], vb[0:n, col:col + 64])]))
                    self.mms(items)
                    y32v = y32[0:n, hh * 512:(hh + 1) * 512].re("p (g two d) -> p g two d", two=2, d=64)
                    for par in range(2):
                        self.cp("act", y32v[:, :, par, :],
                                pbA[0:n, par * 512:par * 512 + 256].re("p (g d) -> p g d", d=64))
                    yield
                    items = []
                    for hl in range(8):
                        h, c, po, col = hd(hl)
                        items.append((psc[hl % 2][po:po + 64, (hl // 2) * 64:(hl // 2 + 1) * 64],
                                      [(bh[0:n, col:col + 64], UT[0:n, xc(hl):xc(hl) + 64]),
                                       (kh[0:n, col:col + 64], vb[0:n, col:col + 64])]))
                    self.mms(items)
                    for cl in range(4):
                        c = hh * 4 + cl
                        for par in range(2):
                            pr = slice(par * 64, par * 64 + 64)
                            self.stt(ST[pr, c, :], ST[pr, c, :], gC[pr, c, 1:2], psc[par][pr, cl * 64:(cl + 1) * 64], ALU.mult, ALU.add)
                if self.dbg == 'A3':
                    S4.put(y32, tD); S2.put(vb, gb, bh, kh)
                    return
                self.cp("pool", STb, ST)
                S2.put(bh, kh)
                if self.dbg == 'A4':
                    S4.put(y32, tD); S2.put(vb, gb)
                    return
                sq = S4.get()
                fw.op("dve", lambda: nc.vector.tensor_reduce(sm[0].ap[0:n, :], h3(y32).ap, AX.X, ALU.add), [y32], [sm[0]])
                self.act(sq[0:n, :], y32[0:n, :], AF.Square)
                fw.op("dve", lambda: nc.vector.tensor_reduce(sm[1].ap[0:n, :], h3(sq).ap, AX.X, ALU.add), [sq], [sm[1]])
                S4.put(sq)
                yield
                self.ts("dve", sm[0][0:n, :], sm[0][0:n, :], 1.0 / 64, None, ALU.mult)
                self.tt("dve", sm[2][0:n, :], sm[0][0:n, :], sm[0][0:n, :], ALU.mult)
                self.stt(sm[1][0:n, :], sm[1][0:n, :], 1.0 / 64, sm[2][0:n, :], ALU.mult, ALU.subtract)
                self.ts("dve", sm[1][0:n, :], sm[1][0:n, :], GN_EPS, None, ALU.add)
                self.act(sm[1][0:n, :], sm[1][0:n, :], AF.Sqrt)
                fw.op("dve", lambda: nc.vector.reciprocal(sm[1].ap[0:n, :], sm[1].ap[0:n, :]), [sm[1]], [sm[1]])
                self.tt("dve", h3(y32), h3(y32), b3(sm[0]), ALU.subtract)
                self.tt("dve", h3(y32), h3(y32), b3(sm[1]), ALU.mult)
                self.tt("pool", y32[0:n, :], y32[0:n, :], lgc[0:n, :], ALU.mult)
                self.tt("pool", y32[0:n, :], y32[0:n, :], lbc[0:n, :], ALU.add)
                self.tt("dve", y32[0:n, :], y32[0:n, :], tD[0:n, :], ALU.add)
                ob = S2.get()
                self.tt("dve", ob[0:n, :], y32[0:n, :], gb[0:n, :], ALU.mult)
                fw.dma("pool", self.o_tiles[ti], ob[0:n, :])
                S4.put(y32, tD)
                S2.put(ob, gb, vb)
                if self.dbg == 'A5':
                    return
                if kind == "p" and j == self.NP - 1:
                    out_state(self.o_wkvp, "wkvp")
                    fw.dma("pool", dram_v(self.o_shiftp, "shiftp"), V(xin.buf, xin.ap[TP - 1:TP, :]), owner=dram_v(self.o_shiftp, "shiftp_o"))
                if kind == "s":
                    out_state(self.o_wkvs[j], f"wkvs{j}")
                    fw.dma("pool", dram_v(self.o_shifts[j:j + 1, :], f"shifts{j}"), V(xin.buf, xin.ap[r0 + n - 1:r0 + n, :]),
                           owner=dram_v(self.o_shifts, "shifts_o"))

            NT = len(self.tiles)
            if NT:
                for _ in front(0):
                    pass
            for ti in range(NT):
                gb_ = back(ti)
                gf_ = front(ti + 1) if ti + 1 < NT else None
                alive_b, alive_f = True, gf_ is not None
                step_ = 0
                while alive_b or alive_f:
                    step_ += 1
                    for _rep in range(2 if step_ % 2 else 1):
                        if alive_b:
                            try:
                                next(gb_)
                            except StopIteration:
                                alive_b = False
                    if alive_f:
                        try:
                            next(gf_)
                        except StopIteration:
                            alive_f = False
            fw.barrier()

    def phase_fox(self, ck, cv, clf):
        nc, fw, W, cst = self.nc, self.fw, self.W, self.cst
        NP, NS, TP, TT, TK = self.NP, self.NS, self.TP, self.TT, self.TK
        qT_d = dram_v(self.qT_scr, "qT_scr")
        kT_d = dram_v(self.kT_scr, "kT_scr")
        v1_d = dram_v(self.v1_scr, "v1_scr")
        with ExitStack() as es:
            sb = lambda shape, dt, name: fw.sb(es, shape, dt, "c_" + name)
            win = sb([128, 8, 3 * D + H], BF16, "win")
            stg = [sb([128, D], F32, f"stg{i}") for i in range(2)]
            idb = sb([128, 128], BF16, "idb")
            tri1 = sb([128, 128], F32, "tri1")
            last128 = sb([128, 128], F32, "last128")
            last16 = sb([128, 128], F32, "last16")
            bfc = sb([128, H], F32, "bfc")
            ones = sb([128, 2, 1024], BF16, "ones")[0:16]
            self._stg_i = 0
            fw.dma("sp", stg[0][:, 0:128], cst["c_ident"])
            self.cp("pool", idb, stg[0][:, 0:128])
            self._stg_i = 1
            fw.dma("sp", tri1, cst["c_tri1"])
            fw.dma("sp", last128, cst["c_last128"])
            fw.dma("sp", last16, cst["c_last16"])
            import os
            ksub = os.environ.get('KSUB', '')
            if 'nobfc' not in ksub:
                self.load_rep(bfc, W["fox_b_f"])
            if 'noones' not in ksub:
                fw.op("pool", lambda: nc.gpsimd.memset(ones.ap, 1.0), [], [ones])
            if 'now' not in ksub:
                self.load_w1k(win, W["fox_w_in"], D, 3 * D + H, stg)
            for t0 in range(0, TT if self.dbg != 'C0a' else 0, 1024):
                w = min(1024, TT - t0)
                fw.dma("pool", dram_v(self.qT_scr[:, 66:68, t0:t0 + w]), ones[:, :, 0:w])
            for t0 in range(0, TK if self.dbg != 'C0a' else 0, 1024):
                w = min(1024, TK - t0)
                fw.dma("pool", dram_v(self.kT_scr[:, 64:66, t0:t0 + w]), ones[:, :, 0:w])
            if self.dbg in ('C0', 'C0a'):
                fw.barrier()
                return
            def mkset(tag, shared=None):
                d = {}
                d["x32"] = [sb([128, D], F32, f"x32{tag}{i}") for i in range(2)]
                d["xb"] = [sb([128, D], BF16, f"xb{tag}{i}") for i in range(2)]
                d["xT"] = [sb([128, 8, 128], BF16, f"xT{tag}{i}") for i in range(2)]
                d["qTt"] = [sb([128, 16, 128], BF16, f"qTt{tag}{i}")[0:64] for i in range(2)]
                d["kTt"] = [sb([128, 16, 128], BF16, f"kTt{tag}{i}")[0:64] for i in range(2)]
                d["kTc"] = [sb([128, 8, 128], BF16, f"kTc{tag}{i}") for i in range(2)]
                d["k32"] = [sb([128, D], F32, f"k32{tag}{i}") for i in range(2)]
                d["v32"] = [sb([128, D], F32, f"v32{tag}{i}") for i in range(2)]
                d["v1t"] = [sb([128, H, 65], BF16, f"v1t{tag}{i}") for i in range(2)]
                d["lf"] = [sb([128, H], F32, f"lf{tag}{i}") for i in range(2)]
                d["cc"] = [sb([128, H], F32, f"cc{tag}{i}") for i in range(2)]
                d["ex"] = sb([128, H], F32, "ex" + tag)
                for nm in ("chi", "clo", "nhi", "nlo"):
                    d[nm] = sb([128, 128], BF16, nm + tag)[0:16]
                for nm in ("cT32", "chi32", "clo32"):
                    d[nm] = sb([128, 128], F32, nm + tag)[0:16]
                for v_ in d["v1t"]:
                    fw.op("pool", lambda: nc.gpsimd.memset(v_.ap, 1.0), [], [v_])
                d["pT"] = fw.ps(es, [128, 8, 128], BF16, "c_pT" + tag)
                d["pf"] = fw.ps(es, [128, 512], F32, "c_pf" + tag)
                if shared is None:
                    d["pq"] = [fw.ps(es, [128, 512], F32, f"c_pq{tag}{i}") for i in range(2)]
                    d["pk"] = fw.ps(es, [128, D], F32, "c_pk" + tag)
                    d["pv"] = d["pk"]
                else:
                    d["pq"], d["pk"], d["pv"] = shared["pq"], shared["pk"], shared["pv"]
                return d
            setA = mkset("A")
            setB = mkset("B", setA)

            streams = []
            st = []
            for i in range(NP):
                st.append(("p", 128, self.x1_tiles[i], i * 128, i * 128, i * 128, None))
            if NP:
                streams.append(st)
            for j in range(NS):
                st = []
                base = TP + j * (PAST + 16)
                for m in range(PAST // 128):
                    st.append(("c", 128, m, None, base + m * 128, None, j))
                st.append(("s", 16, self.x1_tiles[NP + j], TP + j * 16, base + PAST, j * 16, j))
                streams.append(st)
            cnt = 0
            if self.dbg == 'C1a':
                streams = streams[:1]
            if self.dbg == 'C1b':
                streams = streams[1:]
            def run_stream(st, bs):
                x32, xb, xT, qTt, kTt, kTc, k32, v32 = (bs[k_] for k_ in ('x32', 'xb', 'xT', 'qTt', 'kTt', 'kTc', 'k32', 'v32'))
                v1t, lf, cc, ex = bs['v1t'], bs['lf'], bs['cc'], bs['ex']
                chi, clo, nhi, nlo, cT32, chi32, clo32 = (bs[k_] for k_ in ('chi', 'clo', 'nhi', 'nlo', 'cT32', 'chi32', 'clo32'))
                pT, pq, pk, pv, pf = bs['pT'], bs['pq'], bs['pk'], bs['pv'], bs['pf']
                cnt = 0
                prev = None
                for (kind, n, src, qpos, kpos, orow, j) in st:
                    p = cnt % 2
                    cnt += 1
                    lft, cct, v1 = lf[p], cc[p], v1t[p]
                    xb, xT, qTt, kTt, kTc, k32, v32 = (bs[k_][p] for k_ in ('xb', 'xT', 'qTt', 'kTt', 'kTc', 'k32', 'v32'))
                    if kind == "c":
                        m = src
                        a = x32[p]
                        fw.dma("sp", a[0:n, :], V(ck.buf, ck.ap[j, m * 128:(m + 1) * 128, :]))
                        self.cp("pool", xb[0:n, :], a[0:n, :])
                        self.transposes(pT, xb, idb, n)
                        self.cp("act", kTc[:, :, 0:n], pT[:, :, 0:n])
                        for hh in range(2):
                            fw.dma("act", dram_v(self.kT_scr[:, 0:64, kpos:kpos + n].rearrange("(c hh) d t -> hh d c t", hh=2)[hh]),
                                   kTc[hh * 64:(hh + 1) * 64, :, 0:n])
                        b_ = x32[1 - p]
                        fw.dma("sp", b_[0:n, :], V(cv.buf, cv.ap[j, m * 128:(m + 1) * 128, :]))
                        self.cp("pool", v1[0:n, :, 0:64], b_[0:n, :].re("p (h d) -> p h d", h=H))
                        fw.dma("pool", dram_v(self.v1_scr[kpos:kpos + n]), v1[0:n])
                        fw.dma("sp", lft[0:n, :], V(clf.buf, clf.ap[j, m * 128:(m + 1) * 128, :]))
                    else:
                        a = x32[p]
                        fw.dma("sp", a[0:n, :], src)
                        self.cp("pool", xb[0:n, :], a[0:n, :])
                        self.transposes(pT, xb, idb, n)
                        self.cp("act", xT[:, :, 0:n], pT[:, :, 0:n])
                        if 's1' in ksub:
                            continue
                        ko, vo, lo_ = (self.o_kp, self.o_vp, self.o_lfp) if kind == "p" else (self.o_ks, self.o_vs, self.o_lfs)
                        self.mm(pf[0:n, 0:H], [(xT[:, kc, 0:n], win[:, kc, 3 * D:3 * D + H]) for kc in range(8)])
                        self.tt("dve", ex[0:n, :], pf[0:n, 0:H], bfc[0:n, :], ALU.add)
                        self.act(ex[0:n, :], ex[0:n, :], AF.Exp, scale=-1.0)
                        self.act(ex[0:n, :], ex[0:n, :], AF.Ln, bias=1.0)
                        self.ts("dve", lft[0:n, :], ex[0:n, :], -1.0, None, ALU.mult)
                        fw.dma("pool", dram_v(lo_[orow:orow + n, :], f"lo{kind}{orow}"), lft[0:n, :])

                        def tm_proj(pb, coff, dst32, eng_):
                            self.mms([(pb[0:n, hf * 512:(hf + 1) * 512],
                                       [(xT[:, kc, 0:n], win[:, kc, coff + hf * 512:coff + (hf + 1) * 512]) for kc in range(8)])
                                      for hf in range(2)])
                            self.cp(eng_, dst32[0:n, :], pb[0:n, :])

                        def fm_proj(dstT, coff, scale):
                            for g in range(4):
                                pp = pq[g % 2].re("p (a t) -> p a t", a=4)
                                self.mms([(pp[0:64, q_, 0:n],
                                           [(win[:, kc, coff + (g * 4 + q_) * 64:coff + (g * 4 + q_ + 1) * 64], xT[:, kc, 0:n]) for kc in range(8)])
                                          for q_ in range(4)])
                                self.act(dstT[:, g * 4:g * 4 + 4, 0:n], pp[0:64, :, 0:n], AF.Identity, scale=scale)
                        tm_proj(pk, D, k32, "act")
                        fw.dma("act", dram_v(ko[orow:orow + n, :], f"ko{kind}{orow}"), k32[0:n, :])
                        fm_proj(qTt, 0, 0.125)
                        fw.dma("act", dram_v(self.qT_scr[:, 0:64, qpos:qpos + n].rearrange("h d t -> d h t")), qTt[:, :, 0:n])
                        tm_proj(pv, 2 * D, v32, "dve")
                        self.cp("pool", v1[0:n, :, 0:64], v32[0:n, :].re("p (h d) -> p h d", h=H))
                        fw.dma("pool", dram_v(vo[orow:orow + n, :], f"vo{kind}{orow}"), v32[0:n, :])
                        fw.dma("pool", dram_v(self.v1_scr[kpos:kpos + n]), v1[0:n])
                        fm_proj(kTt, D, 1.0)
                        fw.dma("act", dram_v(self.kT_scr[:, 0:64, kpos:kpos + n].rearrange("h d t -> d h t")), kTt[:, :, 0:n])
                    if 's4' in ksub:
                        continue
                    g1 = [(tri1[0:n, 0:n], lft[0:n, :])]
                    g2 = [(lft[0:n, :], tri1[0:n, 0:n])]
                    if prev is not None:
                        pcc, pn = prev
                        lastm = last128 if pn == 128 else last16
                        g1.append((lastm[0:pn, 0:n], pcc[0:pn, :]))
                        g2.append((pcc[0:pn, :], lastm[0:pn, 0:n]))
                    self.mm(pf[0:n, 64:64 + H], g1)
                    self.mm(pf[0:16, 128:128 + n], g2)
                    self.cp("dve", cct[0:n, :], pf[0:n, 64:64 + H])
                    prev = (cct, n)
                    if 's6' in ksub:
                        continue
                    self.cp("dve", cT32[:, 0:n], pf[0:16, 128:128 + n])
                    if 's7' in ksub:
                        continue
                    self.cp("act", chi[:, 0:n], cT32[:, 0:n])
                    self.cp("act", chi32[:, 0:n], chi[:, 0:n])
                    if 's8' in ksub:
                        continue
                    self.tt("dve", clo32[:, 0:n], cT32[:, 0:n], chi32[:, 0:n], ALU.subtract)
                    self.cp("act", clo[:, 0:n], clo32[:, 0:n])
                    self.act(nhi[:, 0:n], chi32[:, 0:n], AF.Identity, scale=-1.0)
                    self.act(nlo[:, 0:n], clo32[:, 0:n], AF.Identity, scale=-1.0)
                    if 's5' in ksub:
                        continue
                    fw.dma("pool", dram_v(self.kT_scr[:, 66, kpos:kpos + n]), nhi[:, 0:n])
                    fw.dma("pool", dram_v(self.kT_scr[:, 67, kpos:kpos + n]), nlo[:, 0:n])
                    if kind != "c":
                        fw.dma("pool", dram_v(self.qT_scr[:, 64, qpos:qpos + n]), chi[:, 0:n])
                        fw.dma("pool", dram_v(self.qT_scr[:, 65, qpos:qpos + n]), clo[:, 0:n])
                    yield

            prompt_streams = [st for st in streams if st and st[0][0] == "p"]
            sample_streams = [st for st in streams if st and st[0][0] != "p"]

            def chain(sts, bs):
                for st in sts:
                    for _ in run_stream(st, bs):
                        yield
            ga, gb2 = chain(prompt_streams, setA), chain(sample_streams, setB)
            alive = [True, True]
            while any(alive):
                for gi_, g_ in enumerate((ga, gb2)):
                    if alive[gi_]:
                        try:
                            next(g_)
                        except StopIteration:
                            alive[gi_] = False
            fw.barrier()
        if self.dbg.startswith('C1'):
            return
        with ExitStack() as es:
            sb = lambda shape, dt, name: fw.sb(es, shape, dt, "d_" + name)
            NKT = max(NP, PAST // 128 + 1)
            v1s = sb([128, NKT, H * 65], BF16, "v1s")
            osb = sb([128, max(NP, 1), D], BF16, "osb")
            qTh = [sb([128, max(TP, 16)], BF16, f"qTh{i}")[0:68] for i in range(2)]
            kTh = [sb([128, max(TP, PAST + 16)], BF16, f"kTh{i}")[0:68] for i in range(2)]
            PTb = [sb([128, 4, 128], BF16, f"PT{i}") for i in range(3)]
            rc = [sb([128, 1], F32, f"rc{i}") for i in range(2)]
            psS = [fw.ps(es, [128, 512], F32, f"d_ps{i}") for i in range(3)]
            psO = [fw.ps(es, [128, 512], F32, f"d_po{i}") for i in range(2)]
            seqs = []
            if NP:
                seqs.append(("p", [(i * 128, 128) for i in range(NP)], [(i * 128, 128) for i in range(NP)], 0, 0, None))
            for j in range(NS):
                base = TP + j * (PAST + 16)
                seqs.append(("s", [(TP + j * 16, 16)], [(base + m * 128, 128) for m in range(PAST // 128)] + [(base + PAST, 16)],
                             TP + j * 16, base, j))
            gi = 0
            hcount = 0
            for (kind, qtiles, ktiles, q0, k0, j) in seqs:
                nq_tot = sum(n for _, n in qtiles)
                nk_tot = sum(n for _, n in ktiles)
                nfull = nk_tot // 128
                if nfull:
                    fw.dma("sp", v1s[:, 0:nfull, :],
                           V(v1_d.buf, self.v1_scr[k0:k0 + nfull * 128].rearrange("(kt p) h e -> p kt (h e)", p=128)))
                if nk_tot % 128:
                    r = nk_tot % 128
                    fw.dma("sp", v1s[0:r, nfull, :],
                           V(v1_d.buf, self.v1_scr[k0 + nfull * 128:k0 + nk_tot].rearrange("p h e -> p (h e)")))
                jobs = []
                for h in range(H):
                    hp = hcount % 2
                    hcount += 1
                    first_of_head = True
                    for qi, (qpos, nq) in enumerate(qtiles):
                        last_kt = qi if kind == "p" else len(ktiles) - 1
                        for g0 in range(0, last_kt + 1, 4):
                            kts = list(range(g0, min(g0 + 4, last_kt + 1)))
                            jobs.append(dict(h=h, hp=hp, qi=qi, qpos=qpos, nq=nq, kts=kts, last_kt=last_kt,
                                             load=first_of_head, fin=(kts[-1] == last_kt), gi=gi))
                            gi += 1
                            first_of_head = False

                def emit_scores(jb):
                    h, hp = jb["h"], jb["hp"]
                    if jb["load"]:
                        fw.dma("sp", qTh[hp][:, 0:nq_tot], V(qT_d.buf, self.qT_scr[h, :, q0:q0 + nq_tot]))
                        fw.dma("sp", kTh[hp][:, 0:nk_tot], V(kT_d.buf, self.kT_scr[h, :, k0:k0 + nk_tot]))
                    ps_ = psS[jb["gi"] % 3].re("p (a t) -> p a t", a=4)
                    nq, ql = jb["nq"], jb["qpos"] - q0
                    self.mms([(ps_[0:ktiles[kt][1], a_, 0:nq],
                               [(kTh[hp][:, ktiles[kt][0] - k0:ktiles[kt][0] - k0 + ktiles[kt][1]], qTh[hp][:, ql:ql + nq])])
                              for a_, kt in enumerate(jb["kts"])])

                def emit_rest(jb):
                    h, qi, nq, kts, last_kt = jb["h"], jb["qi"], jb["nq"], jb["kts"], jb["last_kt"]
                    ps_ = psS[jb["gi"] % 3].re("p (a t) -> p a t", a=4)
                    pt_ = PTb[jb["gi"] % 3]
                    po = psO[qi % 2]
                    nkmin = min(ktiles[kt][1] for kt in kts)
                    if nkmin == 128:
                        self.act(pt_[:, 0:len(kts), 0:nq], ps_[:, 0:len(kts), 0:nq], AF.Exp)
                    else:
                        for a_, kt in enumerate(kts):
                            nk = ktiles[kt][1]
                            self.act(pt_[0:nk, a_, 0:nq], ps_[0:nk, a_, 0:nq], AF.Exp)
                    for a_, kt in enumerate(kts):
                        nk = ktiles[kt][1]
                        if kt == last_kt:
                            fw.op("pool", lambda: nc.gpsimd.affine_select(
                                pt_.ap[0:nk, a_, 0:nq], pt_.ap[0:nk, a_, 0:nq], [[1, nq]], ALU.is_ge, 0.0,
                                base=0, channel_multiplier=-1), [pt_], [pt_])

                    def fn():
                        ins = None
                        for a_, kt in enumerate(kts):
                            nk = ktiles[kt][1]
                            ins = nc.tensor.matmul(po.ap[0:nq, 0:65], pt_.ap[0:nk, a_, 0:nq],
                                                   v1s.ap[0:nk, kt, h * 65:(h + 1) * 65],
                                                   start=(kt == 0), stop=(kt == last_kt))
                        return ins
                    fw.op("pe", fn, [pt_, v1s], [po])
                    if jb["fin"]:
                        r_ = rc[qi % 2]
                        fw.op("dve", lambda: nc.vector.reciprocal(r_.ap[0:nq, :], po.ap[0:nq, 64:65]), [po], [r_])
                        self.ts("dve", osb[0:nq, qi, h * 64:(h + 1) * 64], po[0:nq, 0:64], r_[0:nq, 0:1], None, ALU.mult)

                if jobs:
                    emit_scores(jobs[0])
                for k_, jb in enumerate(jobs):
                    if k_ + 1 < len(jobs):
                        emit_scores(jobs[k_ + 1])
                    emit_rest(jb)
                if kind == "p":
                    fw.dma("pool", dram_v(self.o_scr[0:TP].rearrange("(i p) d -> p i d", p=128), "oscr_all"), osb[:, 0:NP, :])
                else:
                    fw.dma("pool", self.o_tiles[NP + j], osb[0:16, 0, :])
            fw.barrier()

    def fw_last_dma(self, _unused, owner):
        key = ("d", owner.buf.dkey)
        return (key, self.fw.dcnt[key])

    def load_w1k(self, dst, src, K, N, stg, dcol0=0):
        KC = K // 128
        for kc in range(KC):
            for c0 in range(0, N, 1024):
                w = min(1024, N - c0)
                s = stg[self._stg_i % len(stg)]
                self._stg_i += 1
                self.fw.dma("sp", s[:, 0:w], src[kc * 128:(kc + 1) * 128, c0:c0 + w])
                self.cp(("pool", "dve", "act")[self._stg_i % 3], dst[:, kc, dcol0 + c0:dcol0 + c0 + w], s[:, 0:w])


def host_consts():
    n = 128
    i = np.arange(n)
    su = (i[:, None] < i[None, :]).astype(np.float32)
    iu = (i[:, None] <= i[None, :]).astype(np.float32)
    sl = (i[:, None] > i[None, :]).astype(np.float32)
    c = {}
    c["c_ident"] = np.eye(n, dtype=np.float32)
    c["c_ms"] = np.concatenate([su, iu, su, iu], axis=1)
    c["c_ml"] = np.concatenate([sl, sl, sl, sl], axis=1)
    c["c_tri_inc"] = (-EH * iu).astype(np.float32)
    c["c_tri_rev"] = (-EH * sl).astype(np.float32)
    l128 = np.zeros((n, n), np.float32); l128[127, :] = 1.0
    l16 = np.zeros((n, n), np.float32); l16[15, :] = 1.0
    c["c_last128"] = l128
    c["c_last16"] = l16
    c["c_ones"] = np.ones((n, n), np.float32)
    c["c_tri1"] = iu.copy()
    return c


_WNAMES = ["rwkv_mu", "rwkv_w0", "rwkv_w1", "rwkv_w2", "rwkv_a0", "rwkv_a1", "rwkv_a2", "rwkv_g1", "rwkv_g2",
           "rwkv_k_k", "rwkv_k_a", "rwkv_r_k", "rwkv_w_r", "rwkv_w_k", "rwkv_w_v", "rwkv_w_o", "rwkv_lnx_g",
           "rwkv_lnx_b", "fox_w_in", "fox_b_f", "fox_w_o", "ffn_w1", "ffn_w2", "ln_mix_g", "ln_mix_b",
           "ln_ffn_g", "ln_ffn_b"]


def make_in_maps(inputs, NP, NS, ncores=8):
    c = host_consts()
    f = lambda a: np.ascontiguousarray(np.asarray(a, dtype=np.float32))
    wd = {}
    for nm in _WNAMES:
        a = f(inputs[nm])
        if nm.startswith("rwkv") or nm.startswith("fox"):
            a = a[0]
        if nm == "rwkv_r_k":
            a = a.reshape(-1)
        wd[nm] = np.ascontiguousarray(a)
    TP = NP * 128
    maps = []
    for core in range(ncores):
        b = core % inputs["x_prompt"].shape[0]
        ss = [(core * NS + j) % inputs["x_sample"].shape[0] for j in range(NS)]
        m = dict(c)
        m.update(wd)
        xp = f(inputs["x_prompt"][b, :TP])
        xs = [f(inputs["x_sample"][s]) for s in ss]
        m["xin"] = np.ascontiguousarray(np.concatenate([xp] + xs, axis=0))
        sel = ss if NS > 0 else [0]
        m["sshift"] = np.ascontiguousarray(f(inputs["state_shift"][0])[sel])
        m["swkv"] = np.ascontiguousarray(f(inputs["state_wkv"][0])[sel])
        m["ck"] = np.ascontiguousarray(f(inputs["cache_k"][0])[sel].reshape(len(sel), PAST, D))
        m["cv"] = np.ascontiguousarray(f(inputs["cache_v"][0])[sel].reshape(len(sel), PAST, D))
        m["clf"] = np.ascontiguousarray(f(inputs["cache_logf"][0])[sel])
        maps.append(m)
    return maps


_PROG_CACHE = {}


def get_prog(NP, NS, do_fox=True):
    key = (NP, NS, do_fox)
    if key not in _PROG_CACHE:
        p = Prog(NP, NS)
        p.do_fox = do_fox
        p.build()
        _PROG_CACHE[key] = p
    return _PROG_CACHE[key]


def kernel(**inputs):
    NP, NS = 32, 2
    prog = get_prog(NP, NS, do_fox=True)
    maps = make_in_maps(inputs, NP, NS)
    res = run_bass_kernel_spmd(prog.nc, maps, core_ids=list(range(8))).results
    B, DB = 4, 16
    TP = NP * 128
    y_p = np.stack([res[b]["y"][:TP] for b in range(B)])
    y_s = np.stack([res[s // 2]["y"][TP + (s % 2) * 16:TP + (s % 2 + 1) * 16] for s in range(DB)])
    wkv_p = np.stack([res[b]["wkv_p"] for b in range(B)])[None]
    shift_p = np.stack([res[b]["shift_p"][0] for b in range(B)])[None]
    k_p = np.stack([res[b]["k_p"].reshape(TP, H, HD) for b in range(B)])[None]
    v_p = np.stack([res[b]["v_p"].reshape(TP, H, HD) for b in range(B)])[None]
    lf_p = np.stack([res[b]["lf_p"] for b in range(B)])[None]
    wkv_s = np.stack([res[s // 2]["wkv_s"][s % 2] for s in range(DB)])[None]
    shift_s = np.stack([res[s // 2]["shift_s"][s % 2] for s in range(DB)])[None]
    k_s = np.stack([res[s // 2]["k_s"][(s % 2) * 16:(s % 2 + 1) * 16].reshape(16, H, HD) for s in range(DB)])[None]
    v_s = np.stack([res[s // 2]["v_s"][(s % 2) * 16:(s % 2 + 1) * 16].reshape(16, H, HD) for s in range(DB)])[None]
    lf_s = np.stack([res[s // 2]["lf_s"][(s % 2) * 16:(s % 2 + 1) * 16] for s in range(DB)])[None]
    outs = (y_p, y_s, wkv_p, shift_p, k_p, v_p, lf_p, wkv_s, shift_s, k_s, v_s, lf_s)
    return tuple(np.ascontiguousarray(o, dtype=np.float32) for o in outs)
```

```python
import math
import numpy as np
from contextlib import ExitStack
import concourse.bass as bass
import concourse.mybir as mybir
from concourse.bass_utils import run_bass_kernel_spmd

F32 = mybir.dt.float32
BF16 = mybir.dt.bfloat16
AF = mybir.ActivationFunctionType
ALU = mybir.AluOpType
AX = mybir.AxisListType

D = 1024
H = 16
HD = 64
DFF = 4096
ALPHA = 4.0 ** 0.25
LN_EPS = 1e-5
GN_EPS = 64e-5
EH = math.exp(-0.5)
LW, LA, LG = 64, 64, 160
PAST = 2048


class Buf:
    __slots__ = ("t", "name", "lw", "rd", "dkey", "is_dram")

    def __init__(self, t, name, dkey=None):
        self.t = t
        self.name = name
        self.lw = None
        self.rd = {}
        self.dkey = dkey or name
        self.is_dram = False


class V:
    __slots__ = ("buf", "ap")

    def __init__(self, buf, ap=None):
        self.buf = buf
        self.ap = buf.t[:] if ap is None else ap

    def __getitem__(self, k):
        return V(self.buf, self.ap[k])

    def re(self, pat, **kw):
        return V(self.buf, self.ap.rearrange(pat, **kw))

    def bc(self, dt):
        return V(self.buf, self.ap.bitcast(dt))

    def bcast(self, shape):
        return V(self.buf, self.ap.broadcast_to(list(shape)))

    def unsq(self, ax):
        return V(self.buf, self.ap.unsqueeze(ax))


class Eng:
    def __init__(self, key, h):
        self.key = key
        self.h = h
        self.sem = None
        self.cnt = 0
        self.seen = {}


class FW:
    def __init__(self, nc, es):
        self.nc = nc
        self.es = es
        self.engs = {}
        for key, h in (("pe", nc.tensor), ("act", nc.scalar), ("dve", nc.vector),
                       ("pool", nc.gpsimd), ("sp", nc.sync)):
            e = Eng(key, h)
            e.sem = es.enter_context(nc.semaphore("sem_" + key))
            self.engs[key] = e
        self.dsem = {}
        self.dcnt = {}
        self.nb = 0

    def sb(self, es, shape, dt, name, dkey=None):
        self.nb += 1
        t = es.enter_context(self.nc.sbuf_tensor(f"{name}_{self.nb}", list(shape), dt))
        return V(Buf(t, f"{name}_{self.nb}", dkey or name))

    def ps(self, es, shape, dt, name):
        self.nb += 1
        t = es.enter_context(self.nc.psum_tensor(f"{name}_{self.nb}", list(shape), dt))
        return V(Buf(t, f"{name}_{self.nb}"))

    def _deps(self, reads, writes):
        deps = {}

        def add(k, c):
            if deps.get(k, 0) < c:
                deps[k] = c
        for b in reads:
            if b.lw is not None:
                add(*b.lw)
        for b in writes:
            if b.lw is not None:
                add(*b.lw)
            for k, c in b.rd.items():
                add(k, c)
        return deps

    def _semof(self, k):
        return self.dsem[k] if isinstance(k, tuple) else self.engs[k].sem

    def _waits(self, e, deps):
        for k, c in deps.items():
            if k == "pe" and e.key == "pe":
                continue
            if e.seen.get(k, 0) >= c:
                continue
            e.h.wait_ge(self._semof(k), c)
            e.seen[k] = c

    def _mark(self, key, cnt, reads, writes):
        for b in reads:
            if b.rd.get(key, 0) < cnt:
                b.rd[key] = cnt
        for b in writes:
            b.lw = (key, cnt)
            b.rd = {}

    def op(self, eng, fn, reads=(), writes=()):
        e = self.engs[eng]
        reads = [v.buf for v in reads]
        writes = [v.buf for v in writes]
        self._waits(e, self._deps(reads, writes))
        ins = fn()
        e.cnt += 1
        ins.then_inc(e.sem, 1)
        self._mark(e.key, e.cnt, reads, writes)

    def dma(self, eng, out, in_, owner=None):
        e = self.engs[eng]
        if owner is None:
            owner = out if not getattr(out.buf, "is_dram", False) else in_
        qk = "sw" if eng == "pool" else "hw"
        key = ("d", owner.buf.dkey, qk)
        if key not in self.dsem:
            self.dsem[key] = self.es.enter_context(self.nc.semaphore("ds_" + qk + "_" + owner.buf.dkey))
            self.dcnt[key] = 0
        reads, writes = [in_.buf], [out.buf]
        self._waits(e, self._deps(reads, writes))
        ins = e.h.dma_start(out=out.ap, in_=in_.ap)
        self.dcnt[key] += 16
        ins.then_inc(self.dsem[key], 16)
        self._mark(key, self.dcnt[key], reads, writes)

    def barrier(self):
        for e in self.engs.values():
            for key, c in self.dcnt.items():
                if c > 0 and e.seen.get(key, 0) < c:
                    e.h.wait_ge(self.dsem[key], c)
                    e.seen[key] = c
            for k, o in self.engs.items():
                if o.cnt > 0 and e.seen.get(k, 0) < o.cnt:
                    e.h.wait_ge(o.sem, o.cnt)
                    e.seen[k] = o.cnt


def dram_v(ap, name="dram"):
    b = Buf(None, name)
    b.is_dram = True
    return V(b, ap)


class Slots:
    def __init__(self, fw, es, n, shape, dt, name):
        self.free = [fw.sb(es, shape, dt, f"{name}{i}", dkey=f"{name}{i}") for i in range(n)]

    def get(self):
        return self.free.pop(0)

    def put(self, *vs):
        for v in vs:
            self.free.append(V(v.buf))


class Prog:
    def __init__(self, NP, NS):
        self.NP, self.NS = NP, NS
        self.TP = NP * 128
        self.TT = self.TP + NS * 16
        self.tiles = [(i * 128, 128, "p", i) for i in range(NP)] + \
                     [(self.TP + j * 16, 16, "s", j) for j in range(NS)]
        nc = self.nc = bass.Bass("TRN2", target_bir_lowering=False)
        self.din = {}
        self.dout = {}
        self.do_fox = True

    def inp(self, name, shape, dt=F32):
        ap = self.nc.dram_tensor(name, list(shape), dt, kind="ExternalInput").ap()
        self.din[name] = dram_v(ap, name)
        return self.din[name]

    def outp(self, name, shape, dt=F32):
        ap = self.nc.dram_tensor(name, list(shape), dt, kind="ExternalOutput").ap()
        self.dout[name] = ap
        return ap

    def scratch(self, name, shape, dt):
        return self.nc.dram_tensor(name, list(shape), dt, kind="Internal").ap()

    def tt(self, eng, out, a, b, op):
        h = self.nc.vector if eng == "dve" else self.nc.gpsimd
        self.fw.op(eng, lambda: h.tensor_tensor(out.ap, a.ap, b.ap, op), [a, b], [out])

    def ts(self, eng, out, a, s1, s2, op0, op1=None):
        h = self.nc.vector if eng == "dve" else self.nc.gpsimd
        rd = [a] + [s for s in (s1, s2) if isinstance(s, V)]
        g = lambda s: s.ap if isinstance(s, V) else s
        if op1 is None:
            self.fw.op(eng, lambda: h.tensor_scalar(out.ap, a.ap, g(s1), None, op0), rd, [out])
        else:
            self.fw.op(eng, lambda: h.tensor_scalar(out.ap, a.ap, g(s1), g(s2), op0, op1), rd, [out])

    def stt(self, out, a, s, b, op0, op1):
        rd = [a, b] + ([s] if isinstance(s, V) else [])
        sv = s.ap if isinstance(s, V) else s
        self.fw.op("dve", lambda: self.nc.vector.scalar_tensor_tensor(out.ap, a.ap, sv, b.ap, op0, op1), rd, [out])

    def act(self, out, a, func, bias=None, scale=None):
        rd = [a] + ([bias] if isinstance(bias, V) else [])
        kw = {}
        if bias is not None:
            kw["bias"] = bias.ap if isinstance(bias, V) else bias
        if scale is not None:
            kw["scale"] = scale
        self.fw.op("act", lambda: self.nc.scalar.activation(out.ap, a.ap, func, **kw), rd, [out])

    def cp(self, eng, out, a):
        if eng == "act":
            self.fw.op("act", lambda: self.nc.scalar.copy(out.ap, a.ap), [a], [out])
        else:
            h = self.nc.vector if eng == "dve" else self.nc.gpsimd
            self.fw.op(eng, lambda: h.tensor_copy(out.ap, a.ap), [a], [out])

    def mm(self, out, groups):
        rd = []
        for l, r in groups:
            rd += [l, r]

        def fn():
            ins = None
            for i, (l, r) in enumerate(groups):
                ins = self.nc.tensor.matmul(out.ap, l.ap, r.ap, start=(i == 0), stop=(i == len(groups) - 1))
            return ins
        self.fw.op("pe", fn, rd, [out])

    def mms(self, items):
        rd, wr = [], []
        for o, groups in items:
            wr.append(o)
            for l, r in groups:
                rd += [l, r]

        def fn():
            ins = None
            for o, groups in items:
                for i, (l, r) in enumerate(groups):
                    ins = self.nc.tensor.matmul(o.ap, l.ap, r.ap, start=(i == 0), stop=(i == len(groups) - 1))
            return ins
        self.fw.op("pe", fn, rd, wr)

    def transposes(self, out_ps, src, ident, n, nchunk=8, w=128):
        def fn():
            ins = None
            for c in range(nchunk):
                ins = self.nc.tensor.transpose(out_ps.ap[:, c, 0:n], src.ap[0:n, c * w:(c + 1) * w], ident.ap[0:n, 0:n])
            return ins
        self.fw.op("pe", fn, [src, ident], [out_ps])

    def load_w(self, dst, src, K, N, stg, col0=0, ncols=None, dcol0=0):
        ncols = ncols or N
        KC = (K + 127) // 128
        SW = stg[0].ap.shape[-1]
        for kc in range(KC):
            rows = min(128, K - kc * 128)
            for c0 in range(0, ncols, SW):
                w = min(SW, ncols - c0)
                s = stg[self._stg_i % len(stg)]
                self._stg_i += 1
                self.fw.dma("sp", s[0:rows, 0:w], src[kc * 128:kc * 128 + rows, col0 + c0:col0 + c0 + w])
                self.cp(("pool", "dve", "act")[self._stg_i % 3], dst[0:rows, kc, dcol0 + c0:dcol0 + c0 + w], s[0:rows, 0:w])

    def load_rep(self, dst, src1d, n=D):
        self.fw.dma("sp", dst, V(src1d.buf, src1d.ap.partition_broadcast(128)))

    def layer_norm(self, z, n, g, b, st, mv, rs, out):
        nc = self.nc
        zz = z[0:n, :]
        self.fw.op("dve", lambda: nc.vector.bn_stats(st.ap[0:n, 0, :], z.ap[0:n, 0:512]), [z], [st])
        self.fw.op("dve", lambda: nc.vector.bn_stats(st.ap[0:n, 1, :], z.ap[0:n, 512:1024]), [z], [st])
        self.fw.op("dve", lambda: nc.vector.bn_aggr(mv.ap[0:n, :], st.ap[0:n].rearrange("p a b -> p (a b)")), [st], [mv])
        self.ts("dve", rs[0:n, :], mv[0:n, 1:2], LN_EPS, None, ALU.add)
        self.act(rs[0:n, :], rs[0:n, :], AF.Sqrt)
        self.fw.op("dve", lambda: nc.vector.reciprocal(rs.ap[0:n, :], rs.ap[0:n, :]), [rs], [rs])
        self.ts("dve", zz, zz, mv[0:n, 0:1], rs[0:n, 0:1], ALU.subtract, ALU.mult)
        self.tt("dve", zz, zz, g[0:n, :], ALU.mult)
        self.tt("dve", out[0:n, :], zz, b[0:n, :], ALU.add)

    def build(self):
        nc = self.nc
        NP, NS, TP, TT = self.NP, self.NS, self.TP, self.TT
        I = self.inp
        xin = I("xin", [TT, D])
        sshift = I("sshift", [max(NS, 1), D])
        swkv = I("swkv", [max(NS, 1), H, HD, HD])
        ck = I("ck", [max(NS, 1), PAST, D])
        cv = I("cv", [max(NS, 1), PAST, D])
        clf = I("clf", [max(NS, 1), PAST, H])
        W = {}
        for nm, shp in (("rwkv_mu", [6, D]), ("rwkv_w0", [D]), ("rwkv_w1", [D, LW]), ("rwkv_w2", [LW, D]),
                        ("rwkv_a0", [D]), ("rwkv_a1", [D, LA]), ("rwkv_a2", [LA, D]), ("rwkv_g1", [D, LG]),
                        ("rwkv_g2", [LG, D]), ("rwkv_k_k", [D]), ("rwkv_k_a", [D]), ("rwkv_r_k", [D]),
                        ("rwkv_w_r", [D, D]), ("rwkv_w_k", [D, D]), ("rwkv_w_v", [D, D]), ("rwkv_w_o", [D, D]),
                        ("rwkv_lnx_g", [D]), ("rwkv_lnx_b", [D]), ("fox_w_in", [D, 3 * D + H]), ("fox_b_f", [H]),
                        ("fox_w_o", [D, D]), ("ffn_w1", [2, D, DFF]), ("ffn_w2", [2, DFF, D]),
                        ("ln_mix_g", [2, D]), ("ln_mix_b", [2, D]), ("ln_ffn_g", [2, D]), ("ln_ffn_b", [2, D])):
            W[nm] = I(nm, shp)
        cst = {}
        for nm, shp in (("c_ident", [128, 128]), ("c_ms", [128, 512]), ("c_ml", [128, 512]), ("c_tri_inc", [128, 128]),
                        ("c_tri_rev", [128, 128]), ("c_last128", [128, 128]), ("c_last16", [128, 128]),
                        ("c_ones", [128, 128]), ("c_tri1", [128, 128])):
            cst[nm] = I(nm, shp)
        self.W, self.cst = W, cst

        O = self.outp
        self.o_y = O("y", [TT, D])
        self.o_wkvp = O("wkv_p", [H, HD, HD])
        self.o_shiftp = O("shift_p", [1, D])
        self.o_kp = O("k_p", [max(TP, 1), D])
        self.o_vp = O("v_p", [max(TP, 1), D])
        self.o_lfp = O("lf_p", [max(TP, 1), H])
        self.o_wkvs = O("wkv_s", [max(NS, 1), H, HD, HD])
        self.o_shifts = O("shift_s", [max(NS, 1), D])
        self.o_ks = O("k_s", [max(NS, 1) * 16, D])
        self.o_vs = O("v_s", [max(NS, 1) * 16, D])
        self.o_lfs = O("lf_s", [max(NS, 1) * 16, H])

        self.o_scr = self.scratch("o_scr", [TT, D], BF16)
        self.x1_scr = self.scratch("x1_scr", [TT, D], F32)
        self.TK = TP + NS * (PAST + 16)
        self.qT_scr = self.scratch("qT_scr", [H, 68, TT], BF16)
        self.kT_scr = self.scratch("kT_scr", [H, 68, self.TK], BF16)
        self.v1_scr = self.scratch("v1_scr", [self.TK, H, 65], BF16)

        with ExitStack() as es:
            self.fw = FW(nc, es)
            self._stg_i = 0
            self.o_tiles = [dram_v(self.o_scr[r0:r0 + n, :], f"oscr{i}") for i, (r0, n, _, _) in enumerate(self.tiles)]
            self.x1_tiles = [dram_v(self.x1_scr[r0:r0 + n, :], f"x1scr{i}") for i, (r0, n, _, _) in enumerate(self.tiles)]
            self.y_tiles = [dram_v(self.o_y[r0:r0 + n, :], f"yout{i}") for i, (r0, n, _, _) in enumerate(self.tiles)]
            self.xin_tiles = [V(xin.buf, xin.ap[r0:r0 + n, :]) for (r0, n, _, _) in self.tiles]
            import os
            self.dbg = os.environ.get('KDBG', '')
            if self.dbg != 'B' and not self.dbg.startswith('C'):
                self.phase_rwkv(xin, sshift, swkv)
            self.fw.barrier()
            if self.dbg.startswith('A'):
                return nc
            if self.dbg.startswith('C'):
                self.phase_fox(ck, cv, clf)
                self.fw.barrier()
                return nc
            self.phase_ffn(0, self.xin_tiles, self.x1_tiles if self.do_fox else self.y_tiles, W["rwkv_w_o"])
            self.fw.barrier()
            if self.do_fox:
                self.phase_fox(ck, cv, clf)
                self.fw.barrier()
                self.phase_ffn(1, self.x1_tiles, self.y_tiles, W["fox_w_o"])
                self.fw.barrier()
        return nc

    def phase_ffn(self, L, res_tiles, dst_tiles, wo_d):
        nc, fw, W, cst = self.nc, self.fw, self.W, self.cst
        with ExitStack() as es:
            sb = lambda shape, dt, name: fw.sb(es, shape, dt, name + f"L{L}", dkey=name)
            wo = sb([128, 8, D], BF16, "b_wo")
            w1 = sb([128, 8, DFF], BF16, "b_w1")
            w2 = sb([128, 32, D], BF16, "b_w2")
            idb = sb([128, 128], BF16, "b_idb")
            g1, b1, g2, b2 = [sb([128, D], F32, f"b_ln{i}") for i in range(4)]
            with ExitStack() as es2:
                stg = [fw.sb(es2, [128, 2048], F32, f"b_stg{i}L{L}", dkey=f"b_stg{i}") for i in range(4)]
                fw.dma("sp", stg[0][:, 0:128], cst["c_ident"])
                self.cp("pool", idb, stg[0][:, 0:128])
                self._stg_i = 1
                self.load_w(wo, wo_d, D, D, stg)
                self.load_w(w1, V(W["ffn_w1"].buf, W["ffn_w1"].ap[L]), D, DFF, stg)
                self.load_w(w2, V(W["ffn_w2"].buf, W["ffn_w2"].ap[L]), DFF, D, stg)
                fw.barrier()
            for dst, nm in ((g1, "ln_mix_g"), (b1, "ln_mix_b"), (g2, "ln_ffn_g"), (b2, "ln_ffn_b")):
                self.load_rep(dst, V(W[nm].buf, W[nm].ap[L]))
            NBUF = 3
            ob = [sb([128, D], BF16, f"b_ob{i}") for i in range(NBUF)]
            A = [sb([128, D], F32, f"b_A{i}") for i in range(NBUF)]
            oT = sb([128, 8, 128], BF16, "b_oT")
            xmb = sb([128, D], BF16, "b_xmb")
            xmT = sb([128, 8, 128], BF16, "b_xmT")
            hr = [sb([128, 512], F32, f"b_hr{i}") for i in range(2)]
            hT = sb([128, 32, 128], BF16, "b_hT")
            st = sb([128, 2, 6], F32, "b_st")
            mv = sb([128, 2], F32, "b_mv")
            rs = sb([128, 1], F32, "b_rs")
            st2 = sb([128, 2, 6], F32, "b_st2")
            mv2 = sb([128, 2], F32, "b_mv2")
            rs2 = sb([128, 1], F32, "b_rs2")
            pT = fw.ps(es, [128, 8, 128], BF16, "b_pT")
            pz = fw.ps(es, [128, D], F32, "b_pz")
            ph = [fw.ps(es, [128, 512], F32, f"b_ph{i}") for i in range(2)]
            pz2 = fw.ps(es, [128, D], F32, "b_pz2")
            T = self.tiles

            def P1(ti):
                r0, n, kind, j = T[ti]
                p = ti % NBUF
                a = A[p]
                fw.dma("sp", ob[p][0:n, :], self.o_tiles[ti])
                fw.dma("sp", a[0:n, :], res_tiles[ti])
                self.transposes(pT, ob[p], idb, n)
                self.cp("act", oT[:, :, 0:n], pT[:, :, 0:n])
                self.mms([(pz[0:n, hf * 512:(hf + 1) * 512],
                           [(oT[:, kc, 0:n], wo[:, kc, hf * 512:(hf + 1) * 512]) for kc in range(8)])
                          for hf in range(2)])

            def P2(ti):
                r0, n, kind, j = T[ti]
                a = A[ti % NBUF]
                self.stt(a[0:n, :], a[0:n, :], ALPHA, pz[0:n, :], ALU.mult, ALU.add)
                self.layer_norm(a, n, g1, b1, st, mv, rs, a)
                self.cp("dve", xmb[0:n, :], a[0:n, :])

            def P3(ti):
                r0, n, kind, j = T[ti]
                self.transposes(pT, xmb, idb, n)
                self.cp("act", xmT[:, :, 0:n], pT[:, :, 0:n])

            def F1(ti):
                r0, n, kind, j = T[ti]
                for fg in range(8):
                    pp = ph[fg % 2]
                    self.mms([(pp[:, q * 128:q * 128 + n],
                               [(w1[:, kc, (fg * 4 + q) * 128:(fg * 4 + q + 1) * 128], xmT[:, kc, 0:n]) for kc in range(8)])
                              for q in range(4)])
                    h_ = hr[fg % 2]
                    ppv = pp.re("p (q t) -> p q t", q=4)[:, :, 0:n]
                    hv = h_.re("p (q t) -> p q t", q=4)[:, :, 0:n]
                    self.act(hv, ppv, AF.Relu)
                    self.tt("pool", hT[:, fg * 4:fg * 4 + 4, 0:n], hv, hv, ALU.mult)

            def F2(ti):
                r0, n, kind, j = T[ti]
                a = A[ti % NBUF]
                self.mms([(pz2[0:n, hf * 512:(hf + 1) * 512],
                           [(hT[:, fc, 0:n], w2[:, fc, hf * 512:(hf + 1) * 512]) for fc in range(32)])
                          for hf in range(2)])
                self.stt(a[0:n, :], a[0:n, :], ALPHA, pz2[0:n, :], ALU.mult, ALU.add)
                self.layer_norm(a, n, g2, b2, st2, mv2, rs2, a)
                fw.dma("pool", dst_tiles[ti], a[0:n, :])

            NT = len(T)
            if NT:
                P1(0); P2(0); P3(0)
            if NT > 1:
                P1(1); P2(1)
            for ti in range(NT):
                F1(ti)
                if ti + 1 < NT:
                    P3(ti + 1)
                if ti + 2 < NT:
                    P1(ti + 2)
                F2(ti)
                if ti + 2 < NT:
                    P2(ti + 2)
            fw.barrier()

    def phase_rwkv(self, xin, sshift, swkv):
        nc, fw, W, cst = self.nc, self.fw, self.W, self.cst
        TP = self.TP
        with ExitStack() as es:
            sb = lambda shape, dt, name: fw.sb(es, shape, dt, "a_" + name)
            S4 = Slots(fw, es, 11, [128, D], F32, "a_s4_")
            S2 = Slots(fw, es, 13, [128, D], BF16, "a_s2_")
            wr, wk, wv = [sb([128, 8, D], BF16, nm) for nm in ("wr", "wk", "wv")]
            wli = sb([128, 8, 288], BF16, "wli")
            w2e = sb([128, D], BF16, "w2e")
            a2e = sb([128, D], BF16, "a2e")
            g2a = sb([128, 1, D], BF16, "g2a")
            g2b = sb([128, 1, D], BF16, "g2b")[0:32]
            kkc, kac, lgc, lbc = [sb([128, D], F32, nm) for nm in ("kkc", "kac", "lgc", "lbc")]
            rkc = sb([128, D], BF16, "rkc")
            mu = [sb([128, D], BF16, f"mu{i}") for i in range(6)]
            idb = sb([128, 128], BF16, "idb")
            idf = sb([128, 128], F32, "idf")
            ms = sb([128, 4, 128], BF16, "ms")
            ml = sb([128, 4, 128], BF16, "ml")
            tri_inc = sb([128, 128], F32, "tri_inc")
            tri_rev = sb([128, 128], F32, "tri_rev")
            ST = sb([128, 8, 64], F32, "ST")
            STb = sb([128, 8, 64], BF16, "STb")
            hw = sb([128, 128], BF16, "hw")
            ha = sb([128, 128], BF16, "ha")
            hg1 = sb([128, 128], BF16, "hg1")
            hg2 = sb([128, 128], BF16, "hg2")[0:32]
            mixT = [sb([128, 8, 128], BF16, f"mixT{i}") for i in range(2)]
            ARs = [sb([128, 8, 2, 128], BF16, f"AR{i}") for i in range(2)]
            BTs = [sb([128, 8, 128], BF16, f"BT{i}") for i in range(2)]
            KTs = [sb([128, 8, 128], BF16, f"KT{i}") for i in range(2)]
            Abk = sb([128, 8, 4, 128], BF16, "Abk")
            PA = sb([128, 8, 128], BF16, "PA")
            PB = sb([128, 8, 128], BF16, "PB")
            PTA = sb([128, 8, 128], BF16, "PTA")
            PTB = sb([128, 8, 128], BF16, "PTB")
            Xb = [sb([128, 512], BF16, f"Xb{i}") for i in range(2)]
            gCs = [sb([128, 8, 2], F32, f"gC{i}") for i in range(2)]
            sm = [sb([128, 16], F32, f"sm{i}") for i in range(4)]
            smf = [sb([128, 16], F32, f"smf{i}") for i in range(2)]
            pT = fw.ps(es, [128, 8, 128], BF16, "a_pT")
            pbA = fw.ps(es, [128, D], F32, "a_pbA")
            pbB = fw.ps(es, [128, D], F32, "a_pbB")
            phid = fw.ps(es, [128, 512], F32, "a_phid")
            psc = [fw.ps(es, [128, 512], F32, f"a_psc{i}") for i in range(2)]

            s0, s1 = S4.get(), S4.get()
            stg = [s0, s1]
            self._stg_i = 0

            def ldc(dst, src, w, cast_eng="pool"):
                s = stg[self._stg_i % 2]
                self._stg_i += 1
                fw.dma("sp", s[:, 0:w], src)
                self.cp(cast_eng, dst, s[:, 0:w])
            ldc(idb, cst["c_ident"], 128)
            fw.dma("sp", idf, cst["c_ident"])
            ldc(ms.re("p a b -> p (a b)"), cst["c_ms"], 512)
            ldc(ml.re("p a b -> p (a b)"), cst["c_ml"], 512)
            fw.dma("sp", tri_inc, cst["c_tri_inc"])
            fw.dma("sp", tri_rev, cst["c_tri_rev"])
            for i in range(6):
                s = stg[self._stg_i % 2]
                self._stg_i += 1
                self.load_rep(s, V(W["rwkv_mu"].buf, W["rwkv_mu"].ap[i]))
                self.cp("pool", mu[i], s)
            for dst, nm in ((kkc, "rwkv_k_k"), (kac, "rwkv_k_a"), (lgc, "rwkv_lnx_g"), (lbc, "rwkv_lnx_b")):
                self.load_rep(dst, W[nm])
            s = stg[self._stg_i % 2]
            self._stg_i += 1
            self.load_rep(s, W["rwkv_r_k"])
            self.cp("pool", rkc, s)
            for dst, nm in ((wr, "rwkv_w_r"), (wk, "rwkv_w_k"), (wv, "rwkv_w_v")):
                self.load_w1k(dst, W[nm], D, D, stg)
            self.load_w1k(wli, W["rwkv_w1"], D, LW, stg, dcol0=0)
            self.load_w1k(wli, W["rwkv_a1"], D, LA, stg, dcol0=64)
            self.load_w1k(wli, W["rwkv_g1"], D, LG, stg, dcol0=128)
            for dst, wn, bn in ((w2e, "rwkv_w2", "rwkv_w0"), (a2e, "rwkv_a2", "rwkv_a0")):
                fw.op("pool", lambda: nc.gpsimd.memset(dst.ap, 0.0), [], [dst])
                s = stg[self._stg_i % 2]
                self._stg_i += 1
                fw.dma("sp", s[0:64, :], W[wn])
                self.cp("pool", dst[0:64, :], s[0:64, :])
                s = stg[self._stg_i % 2]
                self._stg_i += 1
                bsrc = V(W[bn].buf, W[bn].ap.partition_broadcast(128))
                fw.dma("sp", s[64:65, :], bsrc[64:65, :])
                fw.dma("sp", s[96:97, :], bsrc[96:97, :])
                self.cp("pool", dst[64:65, :], s[64:65, :])
                self.cp("pool", dst[96:97, :], s[96:97, :])
                self.tt("pool", dst[96:97, :], s[96:97, :], dst[96:97, :], ALU.subtract)
            s = stg[self._stg_i % 2]
            self._stg_i += 1
            fw.dma("sp", s[:, 0:D], W["rwkv_g2"][0:128, :])
            self.cp("pool", g2a[:, 0, :], s[:, 0:D])
            s = stg[self._stg_i % 2]
            self._stg_i += 1
            fw.dma("sp", s[0:32, 0:D], W["rwkv_g2"][128:160, :])
            self.cp("pool", g2b[:, 0, :], s[0:32, 0:D])
            for hh_ in (hw, ha):
                fw.op("pool", lambda: nc.gpsimd.memset(hh_.ap, 0.0), [], [hh_])
                fw.op("pool", lambda: nc.gpsimd.memset(hh_.ap[64:65, :], 1.0), [], [hh_])
                fw.op("pool", lambda: nc.gpsimd.memset(hh_.ap[96:97, :], 1.0), [], [hh_])
            S4.put(s0, s1)

            def out_state(dst_ap, name):
                pso = pbA.re("p (c q) -> p c q", c=8)

                def fn():
                    ins = None
                    for c in range(8):
                        ins = nc.tensor.transpose(pso.ap[0:64, c, :], ST.ap[:, c, :], idf.ap)
                    return ins
                fw.op("pe", fn, [ST, idf], [pbA])
                t = S4.get()
                self.cp("act", t[0:64, :], pbA[0:64, :])
                fw.dma("pool", dram_v(dst_ap.rearrange("(c hh) v k -> v c hh k", hh=2), name),
                       t[0:64, :].re("p (c hh k) -> p c hh k", c=8, hh=2))
                S4.put(t)

            if self.dbg == 'A0':
                fw.barrier()
                return
            ctxs = {}
            def front(ti):
                r0, n, kind, j = self.tiles[ti]
                nlev = int(round(math.log2(n)))
                first = (j == 0) if kind == "p" else True
                par = ti % 2
                AR, BT, KT, gC = ARs[par], BTs[par], KTs[par], gCs[par]
                x32, xp = S4.get(), S4.get()
                fw.dma("sp", x32[0:n, :], xin[r0:r0 + n, :])
                if first:
                    if kind == "p":
                        fw.op("pool", lambda: nc.gpsimd.memset(xp.ap[0:1, :], 0.0), [], [xp])
                    else:
                        fw.dma("sp", xp[0:1, :], sshift[j:j + 1, :])
                    fw.dma("sp", xp[1:n, :], xin[r0:r0 + n - 1, :])
                else:
                    fw.dma("sp", xp[0:n, :], xin[r0 - 1:r0 + n - 1, :])
                self.tt("pool", xp[0:n, :], xp[0:n, :], x32[0:n, :], ALU.subtract)
                r32 = k32 = sg = a32 = vb = gb = None
                for i in range(6):
                    t = S4.get()
                    me_ = "dve" if i % 2 else "pool"
                    self.tt(me_, t[0:n, :], xp[0:n, :], mu[i][0:n, :], ALU.mult)
                    mb = S2.get()
                    self.tt(me_, mb[0:n, :], t[0:n, :], x32[0:n, :], ALU.add)
                    S4.put(t)
                    self.transposes(pT, mb, idb, n)
                    mT = mixT[i % 2]
                    self.cp("act", mT[:, :, 0:n], pT[:, :, 0:n])
                    S2.put(mb)
                    yield

                    def proj(pb, w):
                        self.mms([(pb[0:n, hf * 512:(hf + 1) * 512],
                                   [(mT[:, kc, 0:n], w[:, kc, hf * 512:(hf + 1) * 512]) for kc in range(8)])
                                  for hf in range(2)])

                    def lora_out(pb, pairs):
                        self.mms([(pb[0:n, hf * 512:(hf + 1) * 512],
                                   [(hl_, w_[:, hf * 512:(hf + 1) * 512]) for hl_, w_ in pairs])
                                  for hf in range(2)])
                    if i == 0:
                        proj(pbA, wr)
                        r32 = S4.get()
                        self.cp("act", r32[0:n, :], pbA[0:n, :])
                    elif i == 1:
                        self.mm(phid[0:64, 0:n], [(wli[:, kc, 0:64], mT[:, kc, 0:n]) for kc in range(8)])
                        self.act(hw[0:64, 0:n], phid[0:64, 0:n], AF.Tanh)
                        lora_out(pbB, [(hw[:, 0:n], w2e)])
                        sg = S4.get()
                        self.act(sg[0:n, :], pbB[0:n, :], AF.Sigmoid)
                    elif i == 2:
                        proj(pbA, wk)
                        k32 = S4.get()
                        self.cp("act", k32[0:n, :], pbA[0:n, :])
                    elif i == 3:
                        proj(pbB, wv)
                        vb = S2.get()
                        self.cp("act", vb[0:n, :], pbB[0:n, :])
                    elif i == 4:
                        self.mm(phid[0:64, 0:n], [(wli[:, kc, 64:128], mT[:, kc, 0:n]) for kc in range(8)])
                        self.cp("act", ha[0:64, 0:n], phid[0:64, 0:n])
                        lora_out(pbA, [(ha[:, 0:n], a2e)])
                        a32 = S4.get()
                        self.act(a32[0:n, :], pbA[0:n, :], AF.Sigmoid)
                    else:
                        self.mm(phid[:, 0:n], [(wli[:, kc, 128:256], mT[:, kc, 0:n]) for kc in range(8)])
                        self.mm(phid[0:32, 128:128 + n], [(wli[:, kc, 256:288], mT[:, kc, 0:n]) for kc in range(8)])
                        self.act(hg1[:, 0:n], phid[:, 0:n], AF.Sigmoid)
                        self.act(hg2[:, 0:n], phid[0:32, 128:128 + n], AF.Sigmoid)
                        lora_out(pbB, [(hg1[:, 0:n], g2a[:, 0, :]), (hg2[:, 0:n], g2b[:, 0, :])])
                        gb = S2.get()
                        self.cp("act", gb[0:n, :], pbB[0:n, :])
                S4.put(x32, xp)
                if self.dbg == 'A1':
                    S4.put(r32, sg, k32, a32); S2.put(vb, gb)
                    return
                self.mms([(pbA[0:n, hf * 512:(hf + 1) * 512], [(tri_inc[0:n, 0:n], sg[0:n, hf * 512:(hf + 1) * 512])])
                          for hf in range(2)])
                self.mms([(pbB[0:n, hf * 512:(hf + 1) * 512], [(tri_rev[0:n, 0:n], sg[0:n, hf * 512:(hf + 1) * 512])])
                          for hf in range(2)])
                self.mms([(psc[0][:, c * 2:c * 2 + 2], [(sg[0:n, c * 128:(c + 1) * 128], tri_inc[0:n, n - 2:n])])
                          for c in range(8)])
                self.act(gC.re("p c q -> p (c q)"), psc[0][:, 0:16], AF.Exp)
                tA, tB, tC = S4.get(), S4.get(), S4.get()
                self.act(tA[0:n, :], pbA[0:n, :], AF.Exp)
                rt = S2.get()
                self.tt("dve", rt[0:n, :], r32[0:n, :], tA[0:n, :], ALU.mult)
                self.stt(tB[0:n, :], sg[0:n, :], EH, pbA[0:n, :], ALU.mult, ALU.add)
                self.act(tB[0:n, :], tB[0:n, :], AF.Exp)
                self.act(tA[0:n, :], pbA[0:n, :], AF.Exp, scale=-1.0)
                self.act(tC[0:n, :], pbB[0:n, :], AF.Exp)
                S4.put(sg)
                yield
                t1, sq = S4.get(), S4.get()
                h3 = lambda v_: v_[0:n, :].re("p (h d) -> p h d", h=16)
                b3 = lambda v_: v_[0:n, :].unsq(2).bcast([n, 16, 64])
                self.tt("dve", t1[0:n, :], k32[0:n, :], kkc[0:n, :], ALU.mult)
                self.act(sq[0:n, :], t1[0:n, :], AF.Square)
                fw.op("dve", lambda: nc.vector.tensor_reduce(smf[0].ap[0:n, :], h3(sq).ap, AX.X, ALU.add), [sq], [smf[0]])
                self.ts("dve", smf[0][0:n, :], smf[0][0:n, :], 1e-24, None, ALU.max)
                self.act(smf[0][0:n, :], smf[0][0:n, :], AF.Sqrt)
                fw.op("dve", lambda: nc.vector.reciprocal(smf[0].ap[0:n, :], smf[0].ap[0:n, :]), [smf[0]], [smf[0]])
                self.tt("dve", h3(t1), h3(t1), b3(smf[0]), ALU.mult)
                yield
                t2 = sq
                self.stt(t2[0:n, :], a32[0:n, :], -1.0, kac[0:n, :], ALU.add, ALU.mult)
                self.stt(t2[0:n, :], t2[0:n, :], 1.0, k32[0:n, :], ALU.add, ALU.mult)
                S4.put(k32)
                yield
                self.tt("pool", a32[0:n, :], t1[0:n, :], a32[0:n, :], ALU.mult)
                at, bt, kt, bh, kh = [S2.get() for _ in range(5)]
                self.stt(at[0:n, :], t1[0:n, :], -1.0, tB[0:n, :], ALU.mult, ALU.mult)
                self.tt("dve", bt[0:n, :], a32[0:n, :], tA[0:n, :], ALU.mult)
                self.tt("pool", kt[0:n, :], t2[0:n, :], tA[0:n, :], ALU.mult)
                yield
                self.tt("dve", bh[0:n, :], a32[0:n, :], tC[0:n, :], ALU.mult)
                self.tt("pool", kh[0:n, :], t2[0:n, :], tC[0:n, :], ALU.mult)
                S4.put(tA, tB, tC)
                yield
                tD = S4.get()
                self.tt("dve", tD[0:n, :], r32[0:n, :], t2[0:n, :], ALU.mult)
                self.tt("pool", tD[0:n, :], tD[0:n, :], rkc[0:n, :], ALU.mult)
                yield
                fw.op("dve", lambda: nc.vector.tensor_reduce(smf[1].ap[0:n, :], h3(tD).ap, AX.X, ALU.add), [tD], [smf[1]])
                self.tt("dve", h3(tD), b3(smf[1]), h3(vb), ALU.mult)
                S4.put(r32, t1, t2, a32)
                for src, dst in ((at, AR[:, :, 0, 0:n]), (rt, AR[:, :, 1, 0:n]), (bt, BT[:, :, 0:n]), (kt, KT[:, :, 0:n])):
                    self.transposes(pT, src, idb, n)
                    self.cp("act", dst, pT[:, :, 0:n])
                    yield
                S2.put(at, rt, bt, kt)
                if self.dbg == 'A2':
                    S4.put(tD); S2.put(vb, gb, bh, kh)
                    return
                ctxs[ti] = dict(vb=vb, gb=gb, bh=bh, kh=kh, tD=tD)
                yield

            def back(ti):
                r0, n, kind, j = self.tiles[ti]
                nlev = int(round(math.log2(n)))
                first = (j == 0) if kind == "p" else True
                par = ti % 2
                AR, BT, KT, gC = ARs[par], BTs[par], KTs[par], gCs[par]
                c_ = ctxs.pop(ti)
                vb, gb, bh, kh, tD = c_['vb'], c_['gb'], c_['bh'], c_['kh'], c_['tD']
                h3 = lambda v_: v_[0:n, :].re("p (h d) -> p h d", h=16)
                b3 = lambda v_: v_[0:n, :].unsq(2).bcast([n, 16, 64])
                if kind == "p" and j == 0:
                    fw.op("pool", lambda: nc.gpsimd.memset(ST.ap, 0.0), [], [ST])
                    fw.op("pool", lambda: nc.gpsimd.memset(STb.ap, 0.0), [], [STb])
                elif kind == "s":
                    t = S4.get()
                    fw.dma("sp", t[0:64, :].re("p (h k) -> p h k", h=16),
                           V(swkv.buf, swkv.ap[j].rearrange("h v k -> v h k")))
                    psi = pbA[:, 0:512].re("p (c v) -> p c v", c=8)

                    def fn():
                        ins = None
                        for c in range(8):
                            ins = nc.tensor.transpose(psi.ap[:, c, :], t.ap[0:64, c * 128:(c + 1) * 128], idf.ap[0:64, 0:64])
                        return ins
                    fw.op("pe", fn, [t, idf], [pbA])
                    self.cp("act", ST, psi)
                    self.cp("pool", STb, ST)
                    S4.put(t)
                yield
                y32 = S4.get()
                for hh in range(2):
                    hd = lambda hl: (hh * 8 + hl, (hh * 8 + hl) // 2, ((hh * 8 + hl) % 2) * 64, (hh * 8 + hl) * 64)
                    for hl in range(8):
                        h, c, po, col = hd(hl)
                        pr = slice(po, po + 64)
                        pp = psc[hl % 2]
                        ppv = pp.re("p (a t) -> p a t", a=4)
                        if n == 128:
                            self.mms([(ppv[0:n, 0:2, 0:n], [(BT[pr, c, 0:n], AR[pr, c, :, 0:n])]),
                                      (ppv[0:n, 2:4, 0:n], [(KT[pr, c, 0:n], AR[pr, c, :, 0:n])])])
                        else:
                            self.mms([(ppv[0:n, 2 * w_ + a_, 0:n], [(src_[pr, c, 0:n], AR[pr, c, a_, 0:n])])
                                      for w_, src_ in enumerate((BT, KT)) for a_ in range(2)])
                        self.tt("dve", Abk[0:n, hl, :, 0:n], ppv[0:n, :, 0:n], ms[0:n, :, 0:n], ALU.mult)
                    xc = lambda hl: (hl % 2) * 256 + (hl // 2) * 64
                    pc = lambda hl: (hl % 2) * 512 + (hl // 2) * 64
                    banks = (phid.re("p (a t) -> p a t", a=4), psc[0].re("p (a t) -> p a t", a=4))
                    self.mms([(banks[hl % 2][0:n, hl // 2, 0:n],
                               [(AR[slice(hd(hl)[2], hd(hl)[2] + 64), hd(hl)[1], 0, 0:n],
                                 BT[slice(hd(hl)[2], hd(hl)[2] + 64), hd(hl)[1], 0:n])]) for hl in range(8)])
                    PTAv = PTA.re("p (g two) t -> p g two t", two=2)
                    for par in range(2):
                        self.tt("dve", PTAv[0:n, :, par, 0:n], banks[par][0:n, :, 0:n], ml[0:n, :, 0:n], ALU.mult)
                    yield
                    if self.dbg == 'A3':
                        return
                    items = []
                    for hl in range(8):
                        h, c, po, col = hd(hl)
                        pr = slice(po, po + 64)
                        items.append((pbA[0:n, pc(hl):pc(hl) + 64],
                                      [(AR[pr, c, 0, 0:n], STb[pr, c, :]),
                                       (Abk[0:n, hl, 2, 0:n], vb[0:n, col:col + 64])]))
                    self.mms(items)
                    self.cp("act", Xb[0][0:n, :].re("p (b x) -> p b x", b=2),
                            pbA[0:n, :].re("p (b x) -> p b x", b=2)[:, :, 0:256])
                    xi = 0
                    yield
                    P, PT = Abk[:, :, 0, :], PTA
                    Pn, PTn = PB, PTB
                    for lev in range(nlev):
                        pX = pbB
                        self.mms([(pX[0:n, xc(hl):xc(hl) + 64],
                                   [(idb[0:n, 0:n], Xb[xi][0:n, xc(hl):xc(hl) + 64]),
                                    (P[0:n, hl, 0:n], Xb[xi][0:n, xc(hl):xc(hl) + 64])]) for hl in range(8)])
                        self.cp("act", Xb[1 - xi][0:n, :], pX[0:n, 0:512])
                        xi = 1 - xi
                        yield
                        if lev < nlev - 1:
                            for g4 in range(2):
                                pq = psc[g4].re("p (a t) -> p a t", a=4)
                                self.mms([(pq[0:n, q, 0:n], [(PT[0:n, g4 * 4 + q, 0:n], P[0:n, g4 * 4 + q, 0:n])]) for q in range(4)])
                                self.cp("dve", Pn[0:n, g4 * 4:g4 * 4 + 4, 0:n], pq[0:n, :, 0:n])
                            for g4 in range(2):
                                pq = phid.re("p (a t) -> p a t", a=4)
                                self.mms([(pq[0:n, q, 0:n], [(P[0:n, g4 * 4 + q, 0:n], PT[0:n, g4 * 4 + q, 0:n])]) for q in range(4)])
                                self.cp("act", PTn[0:n, g4 * 4:g4 * 4 + 4, 0:n], pq[0:n, :, 0:n])
                            P, PT, Pn, PTn = Pn, PTn, (PA if Pn is PB else PB), (PTA if PTn is PTB else PTB)
                    UT = Xb[xi]
                    yield
                    items = []
                    for hl in range(8):
                        h, c, po, col = hd(hl)
                        pr = slice(po, po + 64)
                        items.append((pbA[0:n, pc(hl):pc(hl) + 64],
                                      [(AR[pr, c, 1, 0:n], STb[pr, c, :]),
                                       (Abk[0:n, hl, 1, 0:n], UT[0:n, xc(hl):xc(hl) + 64]),
                                       (Abk[0:n, hl, 3, 0:n], vb[0:n, col:col + 64])]))
                    self.mms(items)
                    y32v = y32[0:n, hh * 512:(hh + 1) * 512].re("p (g two d) -> p g two d", two=2, d=64)
                    for par in range(2):
                        self.cp("act", y32v[:, :, par, :],
                                pbA[0:n, par * 512:par * 512 + 256].re("p (g d) -> p g d", d=64))
                    yield
                    items = []
                    for hl in range(8):
                        h, c, po, col = hd(hl)
                        items.append((psc[hl % 2][po:po + 64, (hl // 2) * 64:(hl // 2 + 1) * 64],
                                      [(bh[0:n, col:col + 64], UT[0:n, xc(hl):xc(hl) + 64]),
                                       (kh[0:n, col:col + 64], vb[0:n, col:col + 64])]))
                    self.mms(items)
                    for cl in range(4):
                        c = hh * 4 + cl
                        for par in range(2):
                            pr = slice(par * 64, par * 64 + 64)
                            self.stt(ST[pr, c, :], ST[pr, c, :], gC[pr, c, 1:2], psc[par][pr, cl * 64:(cl + 1) * 64], ALU.mult, ALU.add)
                if self.dbg == 'A3':
                    S4.put(y32, tD); S2.put(vb, gb, bh, kh)
                    return
                self.cp("pool", STb, ST)
                S2.put(bh, kh)
                if self.dbg == 'A4':
                    S4.put(y32, tD); S2.put(vb, gb)
                    return
                sq = S4.get()
                fw.op("dve", lambda: nc.vector.tensor_reduce(sm[0].ap[0:n, :], h3(y32).ap, AX.X, ALU.add), [y32], [sm[0]])
                self.act(sq[0:n, :], y32[0:n, :], AF.Square)
                fw.op("dve", lambda: nc.vector.tensor_reduce(sm[1].ap[0:n, :], h3(sq).ap, AX.X, ALU.add), [sq], [sm[1]])
                S4.put(sq)
                yield
                self.ts("dve", sm[0][0:n, :], sm[0][0:n, :], 1.0 / 64, None, ALU.mult)
                self.tt("dve", sm[2][0:n, :], sm[0][0:n, :], sm[0][0:n, :], ALU.mult)
                self.stt(sm[1][0:n, :], sm[1][0:n, :], 1.0 / 64, sm[2][0:n, :], ALU.mult, ALU.subtract)
                self.ts("dve", sm[1][0:n, :], sm[1][0:n, :], GN_EPS, None, ALU.add)
                self.act(sm[1][0:n, :], sm[1][0:n, :], AF.Sqrt)
                fw.op("dve", lambda: nc.vector.reciprocal(sm[1].ap[0:n, :], sm[1].ap[0:n, :]), [sm[1]], [sm[1]])
                self.tt("dve", h3(y32), h3(y32), b3(sm[0]), ALU.subtract)
                self.tt("dve", h3(y32), h3(y32), b3(sm[1]), ALU.mult)
                self.tt("pool", y32[0:n, :], y32[0:n, :], lgc[0:n, :], ALU.mult)
                self.tt("pool", y32[0:n, :], y32[0:n, :], lbc[0:n, :], ALU.add)
                self.tt("dve", y32[0:n, :], y32[0:n, :], tD[0:n, :], ALU.add)
                ob = S2.get()
                self.tt("dve", ob[0:n, :], y32[0:n, :], gb[0:n, :], ALU.mult)
                fw.dma("pool", self.o_tiles[ti], ob[0:n, :])
                S4.put(y32, tD)
                S2.put(ob, gb, vb)
                if self.dbg == 'A5':
                    return
                if kind == "p" and j == self.NP - 1:
                    out_state(self.o_wkvp, "wkvp")
                    fw.dma("pool", dram_v(self.o_shiftp, "shiftp"), V(xin.buf, xin.ap[TP - 1:TP, :]), owner=dram_v(self.o_shiftp, "shiftp_o"))
                if kind == "s":
                    out_state(self.o_wkvs[j], f"wkvs{j}")
                    fw.dma("pool", dram_v(self.o_shifts[j:j + 1, :], f"shifts{j}"), V(xin.buf, xin.ap[r0 + n - 1:r0 + n, :]),
                           owner=dram_v(self.o_shifts, "shifts_o"))

            NT = len(self.tiles)
            if NT:
                for _ in front(0):
                    pass
            for ti in range(NT):
                gb_ = back(ti)
                gf_ = front(ti + 1) if ti + 1 < NT else None
                alive_b, alive_f = True, gf_ is not None
                step_ = 0
                while alive_b or alive_f:
                    step_ += 1
                    for _rep in range(2 if step_ % 2 else 1):
                        if alive_b:
                            try:
                                next(gb_)
                            except StopIteration:
                                alive_b = False
                    if alive_f:
                        try:
                            next(gf_)
                        except StopIteration:
                            alive_f = False
            fw.barrier()

    def phase_fox(self, ck, cv, clf):
        nc, fw, W, cst = self.nc, self.fw, self.W, self.cst
        NP, NS, TP, TT, TK = self.NP, self.NS, self.TP, self.TT, self.TK
        qT_d = dram_v(self.qT_scr, "qT_scr")
        kT_d = dram_v(self.kT_scr, "kT_scr")
        v1_d = dram_v(self.v1_scr, "v1_scr")
        with ExitStack() as es:
            sb = lambda shape, dt, name: fw.sb(es, shape, dt, "c_" + name)
            win = sb([128, 8, 3 * D + H], BF16, "win")
            stg = [sb([128, D], F32, f"stg{i}") for i in range(2)]
            idb = sb([128, 128], BF16, "idb")
            tri1 = sb([128, 128], F32, "tri1")
            last128 = sb([128, 128], F32, "last128")
            last16 = sb([128, 128], F32, "last16")
            bfc = sb([128, H], F32, "bfc")
            ones = sb([128, 2, 1024], BF16, "ones")[0:16]
            self._stg_i = 0
            fw.dma("sp", stg[0][:, 0:128], cst["c_ident"])
            self.cp("pool", idb, stg[0][:, 0:128])
            self._stg_i = 1
            fw.dma("sp", tri1, cst["c_tri1"])
            fw.dma("sp", last128, cst["c_last128"])
            fw.dma("sp", last16, cst["c_last16"])
            import os
            ksub = os.environ.get('KSUB', '')
            if 'nobfc' not in ksub:
                self.load_rep(bfc, W["fox_b_f"])
            if 'noones' not in ksub:
                fw.op("pool", lambda: nc.gpsimd.memset(ones.ap, 1.0), [], [ones])
            if 'now' not in ksub:
                self.load_w1k(win, W["fox_w_in"], D, 3 * D + H, stg)
            for t0 in range(0, TT if self.dbg != 'C0a' else 0, 1024):
                w = min(1024, TT - t0)
                fw.dma("pool", dram_v(self.qT_scr[:, 66:68, t0:t0 + w]), ones[:, :, 0:w])
            for t0 in range(0, TK if self.dbg != 'C0a' else 0, 1024):
                w = min(1024, TK - t0)
                fw.dma("pool", dram_v(self.kT_scr[:, 64:66, t0:t0 + w]), ones[:, :, 0:w])
            if self.dbg in ('C0', 'C0a'):
                fw.barrier()
                return
            def mkset(tag, shared=None):
                d = {}
                d["x32"] = [sb([128, D], F32, f"x32{tag}{i}") for i in range(2)]
                d["xb"] = [sb([128, D], BF16, f"xb{tag}{i}") for i in range(2)]
                d["xT"] = [sb([128, 8, 128], BF16, f"xT{tag}{i}") for i in range(2)]
                d["qTt"] = [sb([128, 16, 128], BF16, f"qTt{tag}{i}")[0:64] for i in range(2)]
                d["kTt"] = [sb([128, 16, 128], BF16, f"kTt{tag}{i}")[0:64] for i in range(2)]
                d["kTc"] = [sb([128, 8, 128], BF16, f"kTc{tag}{i}") for i in range(2)]
                d["k32"] = [sb([128, D], F32, f"k32{tag}{i}") for i in range(2)]
                d["v32"] = [sb([128, D], F32, f"v32{tag}{i}") for i in range(2)]
                d["v1t"] = [sb([128, H, 65], BF16, f"v1t{tag}{i}") for i in range(2)]
                d["lf"] = [sb([128, H], F32, f"lf{tag}{i}") for i in range(2)]
                d["cc"] = [sb([128, H], F32, f"cc{tag}{i}") for i in range(2)]
                d["ex"] = sb([128, H], F32, "ex" + tag)
                for nm in ("chi", "clo", "nhi", "nlo"):
                    d[nm] = sb([128, 128], BF16, nm + tag)[0:16]
                for nm in ("cT32", "chi32", "clo32"):
                    d[nm] = sb([128, 128], F32, nm + tag)[0:16]
                for v_ in d["v1t"]:
                    fw.op("pool", lambda: nc.gpsimd.memset(v_.ap, 1.0), [], [v_])
                d["pT"] = fw.ps(es, [128, 8, 128], BF16, "c_pT" + tag)
                d["pf"] = fw.ps(es, [128, 512], F32, "c_pf" + tag)
                if shared is None:
                    d["pq"] = [fw.ps(es, [128, 512], F32, f"c_pq{tag}{i}") for i in range(2)]
                    d["pk"] = fw.ps(es, [128, D], F32, "c_pk" + tag)
                    d["pv"] = d["pk"]
                else:
                    d["pq"], d["pk"], d["pv"] = shared["pq"], shared["pk"], shared["pv"]
                return d
            setA = mkset("A")
            setB = mkset("B", setA)

            streams = []
            st = []
            for i in range(NP):
                st.append(("p", 128, self.x1_tiles[i], i * 128, i * 128, i * 128, None))
            if NP:
                streams.append(st)
            for j in range(NS):
                st = []
                base = TP + j * (PAST + 16)
                for m in range(PAST // 128):
                    st.append(("c", 128, m, None, base + m * 128, None, j))
                st.append(("s", 16, self.x1_tiles[NP + j], TP + j * 16, base + PAST, j * 16, j))
                streams.append(st)
            cnt = 0
            if self.dbg == 'C1a':
                streams = streams[:1]
            if self.dbg == 'C1b':
                streams = streams[1:]
            def run_stream(st, bs):
                x32, xb, xT, qTt, kTt, kTc, k32, v32 = (bs[k_] for k_ in ('x32', 'xb', 'xT', 'qTt', 'kTt', 'kTc', 'k32', 'v32'))
                v1t, lf, cc, ex = bs['v1t'], bs['lf'], bs['cc'], bs['ex']
                chi, clo, nhi, nlo, cT32, chi32, clo32 = (bs[k_] for k_ in ('chi', 'clo', 'nhi', 'nlo', 'cT32', 'chi32', 'clo32'))
                pT, pq, pk, pv, pf = bs['pT'], bs['pq'], bs['pk'], bs['pv'], bs['pf']
                cnt = 0
                prev = None
                for (kind, n, src, qpos, kpos, orow, j) in st:
                    p = cnt % 2
                    cnt += 1
                    lft, cct, v1 = lf[p], cc[p], v1t[p]
                    xb, xT, qTt, kTt, kTc, k32, v32 = (bs[k_][p] for k_ in ('xb', 'xT', 'qTt', 'kTt', 'kTc', 'k32', 'v32'))
                    if kind == "c":
                        m = src
                        a = x32[p]
                        fw.dma("sp", a[0:n, :], V(ck.buf, ck.ap[j, m * 128:(m + 1) * 128, :]))
                        self.cp("pool", xb[0:n, :], a[0:n, :])
                        self.transposes(pT, xb, idb, n)
                        self.cp("act", kTc[:, :, 0:n], pT[:, :, 0:n])
                        for hh in range(2):
                            fw.dma("act", dram_v(self.kT_scr[:, 0:64, kpos:kpos + n].rearrange("(c hh) d t -> hh d c t", hh=2)[hh]),
                                   kTc[hh * 64:(hh + 1) * 64, :, 0:n])
                        b_ = x32[1 - p]
                        fw.dma("sp", b_[0:n, :], V(cv.buf, cv.ap[j, m * 128:(m + 1) * 128, :]))
                        self.cp("pool", v1[0:n, :, 0:64], b_[0:n, :].re("p (h d) -> p h d", h=H))
                        fw.dma("pool", dram_v(self.v1_scr[kpos:kpos + n]), v1[0:n])
                        fw.dma("sp", lft[0:n, :], V(clf.buf, clf.ap[j, m * 128:(m + 1) * 128, :]))
                    else:
                        a = x32[p]
                        fw.dma("sp", a[0:n, :], src)
                        self.cp("pool", xb[0:n, :], a[0:n, :])
                        self.transposes(pT, xb, idb, n)
                        self.cp("act", xT[:, :, 0:n], pT[:, :, 0:n])
                        if 's1' in ksub:
                            continue
                        ko, vo, lo_ = (self.o_kp, self.o_vp, self.o_lfp) if kind == "p" else (self.o_ks, self.o_vs, self.o_lfs)
                        self.mm(pf[0:n, 0:H], [(xT[:, kc, 0:n], win[:, kc, 3 * D:3 * D + H]) for kc in range(8)])
                        self.tt("dve", ex[0:n, :], pf[0:n, 0:H], bfc[0:n, :], ALU.add)
                        self.act(ex[0:n, :], ex[0:n, :], AF.Exp, scale=-1.0)
                        self.act(ex[0:n, :], ex[0:n, :], AF.Ln, bias=1.0)
                        self.ts("dve", lft[0:n, :], ex[0:n, :], -1.0, None, ALU.mult)
                        fw.dma("pool", dram_v(lo_[orow:orow + n, :], f"lo{kind}{orow}"), lft[0:n, :])

                        def tm_proj(pb, coff, dst32, eng_):
                            self.mms([(pb[0:n, hf * 512:(hf + 1) * 512],
                                       [(xT[:, kc, 0:n], win[:, kc, coff + hf * 512:coff + (hf + 1) * 512]) for kc in range(8)])
                                      for hf in range(2)])
                            self.cp(eng_, dst32[0:n, :], pb[0:n, :])

                        def fm_proj(dstT, coff, scale):
                            for g in range(4):
                                pp = pq[g % 2].re("p (a t) -> p a t", a=4)
                                self.mms([(pp[0:64, q_, 0:n],
                                           [(win[:, kc, coff + (g * 4 + q_) * 64:coff + (g * 4 + q_ + 1) * 64], xT[:, kc, 0:n]) for kc in range(8)])
                                          for q_ in range(4)])
                                self.act(dstT[:, g * 4:g * 4 + 4, 0:n], pp[0:64, :, 0:n], AF.Identity, scale=scale)
                        tm_proj(pk, D, k32, "act")
                        fw.dma("act", dram_v(ko[orow:orow + n, :], f"ko{kind}{orow}"), k32[0:n, :])
                        fm_proj(qTt, 0, 0.125)
                        fw.dma("act", dram_v(self.qT_scr[:, 0:64, qpos:qpos + n].rearrange("h d t -> d h t")), qTt[:, :, 0:n])
                        tm_proj(pv, 2 * D, v32, "dve")
                        self.cp("pool", v1[0:n, :, 0:64], v32[0:n, :].re("p (h d) -> p h d", h=H))
                        fw.dma("pool", dram_v(vo[orow:orow + n, :], f"vo{kind}{orow}"), v32[0:n, :])
                        fw.dma("pool", dram_v(self.v1_scr[kpos:kpos + n]), v1[0:n])
                        fm_proj(kTt, D, 1.0)
                        fw.dma("act", dram_v(self.kT_scr[:, 0:64, kpos:kpos + n].rearrange("h d t -> d h t")), kTt[:, :, 0:n])
                    if 's4' in ksub:
                        continue
                    g1 = [(tri1[0:n, 0:n], lft[0:n, :])]
                    g2 = [(lft[0:n, :], tri1[0:n, 0:n])]
                    if prev is not None:
                        pcc, pn = prev
                        lastm = last128 if pn == 128 else last16
                        g1.append((lastm[0:pn, 0:n], pcc[0:pn, :]))
                        g2.append((pcc[0:pn, :], lastm[0:pn, 0:n]))
                    self.mm(pf[0:n, 64:64 + H], g1)
                    self.mm(pf[0:16, 128:128 + n], g2)
                    self.cp("dve", cct[0:n, :], pf[0:n, 64:64 + H])
                    prev = (cct, n)
                    if 's6' in ksub:
                        continue
                    self.cp("dve", cT32[:, 0:n], pf[0:16, 128:128 + n])
                    if 's7' in ksub:
                        continue
                    self.cp("act", chi[:, 0:n], cT32[:, 0:n])
                    self.cp("act", chi32[:, 0:n], chi[:, 0:n])
                    if 's8' in ksub:
                        continue
                    self.tt("dve", clo32[:, 0:n], cT32[:, 0:n], chi32[:, 0:n], ALU.subtract)
                    self.cp("act", clo[:, 0:n], clo32[:, 0:n])
                    self.act(nhi[:, 0:n], chi32[:, 0:n], AF.Identity, scale=-1.0)
                    self.act(nlo[:, 0:n], clo32[:, 0:n], AF.Identity, scale=-1.0)
                    if 's5' in ksub:
                        continue
                    fw.dma("pool", dram_v(self.kT_scr[:, 66, kpos:kpos + n]), nhi[:, 0:n])
                    fw.dma("pool", dram_v(self.kT_scr[:, 67, kpos:kpos + n]), nlo[:, 0:n])
                    if kind != "c":
                        fw.dma("pool", dram_v(self.qT_scr[:, 64, qpos:qpos + n]), chi[:, 0:n])
                        fw.dma("pool", dram_v(self.qT_scr[:, 65, qpos:qpos + n]), clo[:, 0:n])
                    yield

            prompt_streams = [st for st in streams if st and st[0][0] == "p"]
            sample_streams = [st for st in streams if st and st[0][0] != "p"]

            def chain(sts, bs):
                for st in sts:
                    for _ in run_stream(st, bs):
                        yield
            ga, gb2 = chain(prompt_streams, setA), chain(sample_streams, setB)
            alive = [True, True]
            while any(alive):
                for gi_, g_ in enumerate((ga, gb2)):
                    if alive[gi_]:
                        try:
                            next(g_)
                        except StopIteration:
                            alive[gi_] = False
            fw.barrier()
        if self.dbg.startswith('C1'):
            return
        with ExitStack() as es:
            sb = lambda shape, dt, name: fw.sb(es, shape, dt, "d_" + name)
            NKT = max(NP, PAST // 128 + 1)
            v1s = sb([128, NKT, H * 65], BF16, "v1s")
            osb = sb([128, max(NP, 1), D], BF16, "osb")
            qTh = [sb([128, max(TP, 16)], BF16, f"qTh{i}")[0:68] for i in range(2)]
            kTh = [sb([128, max(TP, PAST + 16)], BF16, f"kTh{i}")[0:68] for i in range(2)]
            PTb = [sb([128, 4, 128], BF16, f"PT{i}") for i in range(3)]
            rc = [sb([128, 1], F32, f"rc{i}") for i in range(2)]
            psS = [fw.ps(es, [128, 512], F32, f"d_ps{i}") for i in range(3)]
            psO = [fw.ps(es, [128, 512], F32, f"d_po{i}") for i in range(2)]
            seqs = []
            if NP:
                seqs.append(("p", [(i * 128, 128) for i in range(NP)], [(i * 128, 128) for i in range(NP)], 0, 0, None))
            for j in range(NS):
                base = TP + j * (PAST + 16)
                seqs.append(("s", [(TP + j * 16, 16)], [(base + m * 128, 128) for m in range(PAST // 128)] + [(base + PAST, 16)],
                             TP + j * 16, base, j))
            gi = 0
            hcount = 0
            for (kind, qtiles, ktiles, q0, k0, j) in seqs:
                nq_tot = sum(n for _, n in qtiles)
                nk_tot = sum(n for _, n in ktiles)
                nfull = nk_tot // 128
                if nfull:
                    fw.dma("sp", v1s[:, 0:nfull, :],
                           V(v1_d.buf, self.v1_scr[k0:k0 + nfull * 128].rearrange("(kt p) h e -> p kt (h e)", p=128)))
                if nk_tot % 128:
                    r = nk_tot % 128
                    fw.dma("sp", v1s[0:r, nfull, :],
                           V(v1_d.buf, self.v1_scr[k0 + nfull * 128:k0 + nk_tot].rearrange("p h e -> p (h e)")))
                jobs = []
                for h in range(H):
                    hp = hcount % 2
                    hcount += 1
                    first_of_head = True
                    for qi, (qpos, nq) in enumerate(qtiles):
                        last_kt = qi if kind == "p" else len(ktiles) - 1
                        for g0 in range(0, last_kt + 1, 4):
                            kts = list(range(g0, min(g0 + 4, last_kt + 1)))
                            jobs.append(dict(h=h, hp=hp, qi=qi, qpos=qpos, nq=nq, kts=kts, last_kt=last_kt,
                                             load=first_of_head, fin=(kts[-1] == last_kt), gi=gi))
                            gi += 1
                            first_of_head = False

                def emit_scores(jb):
                    h, hp = jb["h"], jb["hp"]
                    if jb["load"]:
                        fw.dma("sp", qTh[hp][:, 0:nq_tot], V(qT_d.buf, self.qT_scr[h, :, q0:q0 + nq_tot]))
                        fw.dma("sp", kTh[hp][:, 0:nk_tot], V(kT_d.buf, self.kT_scr[h, :, k0:k0 + nk_tot]))
                    ps_ = psS[jb["gi"] % 3].re("p (a t) -> p a t", a=4)
                    nq, ql = jb["nq"], jb["qpos"] - q0
                    self.mms([(ps_[0:ktiles[kt][1], a_, 0:nq],
                               [(kTh[hp][:, ktiles[kt][0] - k0:ktiles[kt][0] - k0 + ktiles[kt][1]], qTh[hp][:, ql:ql + nq])])
                              for a_, kt in enumerate(jb["kts"])])

                def emit_rest(jb):
                    h, qi, nq, kts, last_kt = jb["h"], jb["qi"], jb["nq"], jb["kts"], jb["last_kt"]
                    ps_ = psS[jb["gi"] % 3].re("p (a t) -> p a t", a=4)
                    pt_ = PTb[jb["gi"] % 3]
                    po = psO[qi % 2]
                    nkmin = min(ktiles[kt][1] for kt in kts)
                    if nkmin == 128:
                        self.act(pt_[:, 0:len(kts), 0:nq], ps_[:, 0:len(kts), 0:nq], AF.Exp)
                    else:
                        for a_, kt in enumerate(kts):
                            nk = ktiles[kt][1]
                            self.act(pt_[0:nk, a_, 0:nq], ps_[0:nk, a_, 0:nq], AF.Exp)
                    for a_, kt in enumerate(kts):
                        nk = ktiles[kt][1]
                        if kt == last_kt:
                            fw.op("pool", lambda: nc.gpsimd.affine_select(
                                pt_.ap[0:nk, a_, 0:nq], pt_.ap[0:nk, a_, 0:nq], [[1, nq]], ALU.is_ge, 0.0,
                                base=0, channel_multiplier=-1), [pt_], [pt_])

                    def fn():
                        ins = None
                        for a_, kt in enumerate(kts):
                            nk = ktiles[kt][1]
                            ins = nc.tensor.matmul(po.ap[0:nq, 0:65], pt_.ap[0:nk, a_, 0:nq],
                                                   v1s.ap[0:nk, kt, h * 65:(h + 1) * 65],
                                                   start=(kt == 0), stop=(kt == last_kt))
                        return ins
                    fw.op("pe", fn, [pt_, v1s], [po])
                    if jb["fin"]:
                        r_ = rc[qi % 2]
                        fw.op("dve", lambda: nc.vector.reciprocal(r_.ap[0:nq, :], po.ap[0:nq, 64:65]), [po], [r_])
                        self.ts("dve", osb[0:nq, qi, h * 64:(h + 1) * 64], po[0:nq, 0:64], r_[0:nq, 0:1], None, ALU.mult)

                if jobs:
                    emit_scores(jobs[0])
                for k_, jb in enumerate(jobs):
                    if k_ + 1 < len(jobs):
                        emit_scores(jobs[k_ + 1])
                    emit_rest(jb)
                if kind == "p":
                    fw.dma("pool", dram_v(self.o_scr[0:TP].rearrange("(i p) d -> p i d", p=128), "oscr_all"), osb[:, 0:NP, :])
                else:
                    fw.dma("pool", self.o_tiles[NP + j], osb[0:16, 0, :])
            fw.barrier()

    def fw_last_dma(self, _unused, owner):
        key = ("d", owner.buf.dkey)
        return (key, self.fw.dcnt[key])

    def load_w1k(self, dst, src, K, N, stg, dcol0=0):
        KC = K // 128
        for kc in range(KC):
            for c0 in range(0, N, 1024):
                w = min(1024, N - c0)
                s = stg[self._stg_i % len(stg)]
                self._stg_i += 1
                self.fw.dma("sp", s[:, 0:w], src[kc * 128:(kc + 1) * 128, c0:c0 + w])
                self.cp(("pool", "dve", "act")[self._stg_i % 3], dst[:, kc, dcol0 + c0:dcol0 + c0 + w], s[:, 0:w])


def host_consts():
    n = 128
    i = np.arange(n)
    su = (i[:, None] < i[None, :]).astype(np.float32)
    iu = (i[:, None] <= i[None, :]).astype(np.float32)
    sl = (i[:, None] > i[None, :]).astype(np.float32)
    c = {}
    c["c_ident"] = np.eye(n, dtype=np.float32)
    c["c_ms"] = np.concatenate([su, iu, su, iu], axis=1)
    c["c_ml"] = np.concatenate([sl, sl, sl, sl], axis=1)
    c["c_tri_inc"] = (-EH * iu).astype(np.float32)
    c["c_tri_rev"] = (-EH * sl).astype(np.float32)
    l128 = np.zeros((n, n), np.float32); l128[127, :] = 1.0
    l16 = np.zeros((n, n), np.float32); l16[15, :] = 1.0
    c["c_last128"] = l128
    c["c_last16"] = l16
    c["c_ones"] = np.ones((n, n), np.float32)
    c["c_tri1"] = iu.copy()
    return c


_WNAMES = ["rwkv_mu", "rwkv_w0", "rwkv_w1", "rwkv_w2", "rwkv_a0", "rwkv_a1", "rwkv_a2", "rwkv_g1", "rwkv_g2",
           "rwkv_k_k", "rwkv_k_a", "rwkv_r_k", "rwkv_w_r", "rwkv_w_k", "rwkv_w_v", "rwkv_w_o", "rwkv_lnx_g",
           "rwkv_lnx_b", "fox_w_in", "fox_b_f", "fox_w_o", "ffn_w1", "ffn_w2", "ln_mix_g", "ln_mix_b",
           "ln_ffn_g", "ln_ffn_b"]


def make_in_maps(inputs, NP, NS, ncores=8):
    c = host_consts()
    f = lambda a: np.ascontiguousarray(np.asarray(a, dtype=np.float32))
    wd = {}
    for nm in _WNAMES:
        a = f(inputs[nm])
        if nm.startswith("rwkv") or nm.startswith("fox"):
            a = a[0]
        if nm == "rwkv_r_k":
            a = a.reshape(-1)
        wd[nm] = np.ascontiguousarray(a)
    TP = NP * 128
    maps = []
    for core in range(ncores):
        b = core % inputs["x_prompt"].shape[0]
        ss = [(core * NS + j) % inputs["x_sample"].shape[0] for j in range(NS)]
        m = dict(c)
        m.update(wd)
        xp = f(inputs["x_prompt"][b, :TP])
        xs = [f(inputs["x_sample"][s]) for s in ss]
        m["xin"] = np.ascontiguousarray(np.concatenate([xp] + xs, axis=0))
        sel = ss if NS > 0 else [0]
        m["sshift"] = np.ascontiguousarray(f(inputs["state_shift"][0])[sel])
        m["swkv"] = np.ascontiguousarray(f(inputs["state_wkv"][0])[sel])
        m["ck"] = np.ascontiguousarray(f(inputs["cache_k"][0])[sel].reshape(len(sel), PAST, D))
        m["cv"] = np.ascontiguousarray(f(inputs["cache_v"][0])[sel].reshape(len(sel), PAST, D))
        m["clf"] = np.ascontiguousarray(f(inputs["cache_logf"][0])[sel])
        maps.append(m)
    return maps


_PROG_CACHE = {}


def get_prog(NP, NS, do_fox=True):
    key = (NP, NS, do_fox)
    if key not in _PROG_CACHE:
        p = Prog(NP, NS)
        p.do_fox = do_fox
        p.build()
        _PROG_CACHE[key] = p
    return _PROG_CACHE[key]


def kernel(**inputs):
    NP, NS = 32, 2
    prog = get_prog(NP, NS, do_fox=True)
    maps = make_in_maps(inputs, NP, NS)
    res = run_bass_kernel_spmd(prog.nc, maps, core_ids=list(range(8))).results
    B, DB = 4, 16
    TP = NP * 128
    y_p = np.stack([res[b]["y"][:TP] for b in range(B)])
    y_s = np.stack([res[s // 2]["y"][TP + (s % 2) * 16:TP + (s % 2 + 1) * 16] for s in range(DB)])
    wkv_p = np.stack([res[b]["wkv_p"] for b in range(B)])[None]
    shift_p = np.stack([res[b]["shift_p"][0] for b in range(B)])[None]
    k_p = np.stack([res[b]["k_p"].reshape(TP, H, HD) for b in range(B)])[None]
    v_p = np.stack([res[b]["v_p"].reshape(TP, H, HD) for b in range(B)])[None]
    lf_p = np.stack([res[b]["lf_p"] for b in range(B)])[None]
    wkv_s = np.stack([res[s // 2]["wkv_s"][s % 2] for s in range(DB)])[None]
    shift_s = np.stack([res[s // 2]["shift_s"][s % 2] for s in range(DB)])[None]
    k_s = np.stack([res[s // 2]["k_s"][(s % 2) * 16:(s % 2 + 1) * 16].reshape(16, H, HD) for s in range(DB)])[None]
    v_s = np.stack([res[s // 2]["v_s"][(s % 2) * 16:(s % 2 + 1) * 16].reshape(16, H, HD) for s in range(DB)])[None]
    lf_s = np.stack([res[s // 2]["lf_s"][(s % 2) * 16:(s % 2 + 1) * 16] for s in range(DB)])[None]
    outs = (y_p, y_s, wkv_p, shift_p, k_p, v_p, lf_p, wkv_s, shift_s, k_s, v_s, lf_s)
    return tuple(np.ascontiguousarray(o, dtype=np.float32) for o in outs)
```
